# Optimizing a Trainium2 kernel written in Bass

```python
import math
import jax, jax.numpy as jnp
from jax import lax
import numpy as np

D_MODEL = 1024
BATCH = 2
SEQ = 16384
DEPTH = 2

CHUNK = 64
Q_BLOCK = 128
A_HEADS = 8
A_HEAD_DIM = 64
A_WIDTH = A_HEADS * A_HEAD_DIM
IDX_HEADS = 8
IDX_DIM = 64
TOPK_MAX = 256
B_HEADS = 8
B_Q_LORA = 384
B_KV_LORA = 256
B_NOPE = 64
B_ROPE = 32
B_QK = B_NOPE + B_ROPE
B_V = 64
B_WIDTH = B_HEADS * B_V
ROPE_BASE = 10000.0
REL_BUCKETS = 32
REL_MAX_DIST = 128
D_FF = 2816
CONV_W = 3
EPS = 1e-6

IN_SIZES = (A_WIDTH, A_WIDTH, A_WIDTH,
            IDX_HEADS * IDX_DIM, IDX_DIM, IDX_HEADS,
            B_Q_LORA, B_KV_LORA, B_ROPE,
            D_MODEL, D_MODEL)
IN_COLS = 3 * A_WIDTH + IDX_HEADS * IDX_DIM + IDX_DIM + IDX_HEADS + B_Q_LORA + B_KV_LORA + B_ROPE + 2 * D_MODEL

kernel_name = "hybrid_dsa_mla_convffn_chunk_causal"


def rms_norm(x, g):
    xf = x.astype(jnp.float32)
    y = xf * lax.rsqrt(jnp.mean(xf * xf, axis=-1, keepdims=True) + EPS)
    return (y * g.astype(jnp.float32)).astype(x.dtype)


def split_cols(z):
    out, start = [], 0
    for n in IN_SIZES:
        out.append(z[..., start:start + n])
        start += n
    return out


def rope(x, pos):
    half = x.shape[-1] // 2
    inv = ROPE_BASE ** (-jnp.arange(half, dtype=jnp.float32) / half)
    ang = pos.astype(jnp.float32)[:, None] * inv[None, :]
    cos = jnp.cos(ang)[:, None, :]
    sin = jnp.sin(ang)[:, None, :]
    xf = x.astype(jnp.float32)
    x1, x2 = xf[..., :half], xf[..., half:]
    return jnp.concatenate([x1 * cos - x2 * sin, x1 * sin + x2 * cos], axis=-1).astype(x.dtype)


def t5_bucket(rel):
    nb = REL_BUCKETS // 2
    max_exact = nb // 2
    side = jnp.where(rel > 0, nb, 0)
    n = jnp.abs(rel)
    nf = jnp.maximum(n, 1).astype(jnp.float32)
    large = max_exact + (jnp.log(nf / max_exact) / math.log(REL_MAX_DIST / max_exact)
                         * (nb - max_exact)).astype(jnp.int32)
    large = jnp.minimum(large, nb - 1)
    return side + jnp.where(n < max_exact, n, large)


def sparse_indexed_attention(q, k, v, q_idx, k_idx, w_idx, rel_bias, topk):
    B, S = q.shape[0], q.shape[1]
    n_blk = S // Q_BLOCK
    key_chunk = jnp.arange(S, dtype=jnp.int32) // CHUNK
    bidx = jnp.arange(B)[:, None, None]
    scale = A_HEAD_DIM ** -0.5
    idx_scale = (IDX_HEADS ** -0.5) * (IDX_DIM ** -0.5)

    def block(i):
        t0 = i * Q_BLOCK
        tq = t0 + jnp.arange(Q_BLOCK, dtype=jnp.int32)
        qi = lax.dynamic_slice_in_dim(q_idx, t0, Q_BLOCK, axis=1)
        wi = lax.dynamic_slice_in_dim(w_idx, t0, Q_BLOCK, axis=1).astype(jnp.float32)
        rel = jax.nn.relu(jnp.einsum('bthd,bsd->bths', qi, k_idx).astype(jnp.float32))
        score = jnp.einsum('bths,bth->bts', rel, wi) * idx_scale
        admissible = key_chunk[None, :] <= (tq // CHUNK)[:, None]
        score = jnp.where(admissible[None], score, -jnp.inf)
        _, sel = lax.top_k(score, topk)
        valid = (sel // CHUNK) <= (tq // CHUNK)[None, :, None]
        k_sel = k[bidx, sel]
        v_sel = v[bidx, sel]
        qb = lax.dynamic_slice_in_dim(q, t0, Q_BLOCK, axis=1)
        logits = jnp.einsum('bthd,btkhd->bthk', qb, k_sel).astype(jnp.float32) * scale
        bias = rel_bias[t5_bucket(sel - tq[None, :, None])]
        logits = logits + jnp.moveaxis(bias, -1, 2).astype(jnp.float32)
        logits = jnp.where(valid[:, :, None, :], logits, -jnp.inf)
        p = jax.nn.softmax(logits, axis=-1).astype(v.dtype)
        return jnp.einsum('bthk,btkhd->bthd', p, v_sel)

    out = lax.map(block, jnp.arange(n_blk))
    return jnp.moveaxis(out, 0, 1).reshape(B, S, A_WIDTH)


def latent_attention(q, k, v):
    B, S = q.shape[0], q.shape[1]
    n_blk = S // Q_BLOCK
    key_chunk = jnp.arange(S, dtype=jnp.int32) // CHUNK
    scale = B_QK ** -0.5

    def block(i):
        t0 = i * Q_BLOCK
        tq = t0 + jnp.arange(Q_BLOCK, dtype=jnp.int32)
        qb = lax.dynamic_slice_in_dim(q, t0, Q_BLOCK, axis=1)
        logits = jnp.einsum('bthd,bshd->bhts', qb, k).astype(jnp.float32) * scale
        mask = key_chunk[None, :] <= (tq // CHUNK)[:, None]
        logits = jnp.where(mask[None, None], logits, -jnp.inf)
        p = jax.nn.softmax(logits, axis=-1).astype(v.dtype)
        return jnp.einsum('bhts,bshd->bthd', p, v)

    out = lax.map(block, jnp.arange(n_blk))
    return jnp.moveaxis(out, 0, 1).reshape(B, S, B_WIDTH)


def causal_dwconv(u, w, b):
    S = u.shape[1]
    up = jnp.pad(u, ((0, 0), (CONV_W - 1, 0), (0, 0)))
    acc = b
    for j in range(CONV_W):
        acc = acc + w[j] * up[:, j:j + S]
    return acc


def setup_inputs(seed: int = 0) -> dict:
    key = jax.random.key(seed)
    ks = jax.random.split(key, 22)
    f32 = jnp.float32
    L = DEPTH

    def lin(k, shape, fan_in):
        return jax.random.normal(k, shape, f32) * fan_in ** -0.5

    def gain(k, shape):
        return 1.0 + 0.02 * jax.random.normal(k, shape, f32)

    return {
        "x": jax.random.normal(ks[0], (BATCH, SEQ, D_MODEL), f32),
        "rel_bias": 0.1 * jax.random.normal(ks[1], (REL_BUCKETS, A_HEADS), f32),
        "norm_mix": gain(ks[2], (L, D_MODEL)),
        "w_in": lin(ks[3], (L, D_MODEL, IN_COLS), D_MODEL),
        "a_q_norm": gain(ks[4], (L, A_HEAD_DIM)),
        "a_k_norm": gain(ks[5], (L, A_HEAD_DIM)),
        "b_cq_norm": gain(ks[6], (L, B_Q_LORA)),
        "b_ckv_norm": gain(ks[7], (L, B_KV_LORA)),
        "b_w_uq": lin(ks[8], (L, B_Q_LORA, B_HEADS * B_QK), B_Q_LORA),
        "b_w_ukv": lin(ks[9], (L, B_KV_LORA, B_HEADS * (B_NOPE + B_V)), B_KV_LORA),
        "b_q_norm": gain(ks[10], (L, B_QK)),
        "b_k_norm": gain(ks[11], (L, B_QK)),
        "w_proj_a": lin(ks[12], (L, A_WIDTH, D_MODEL), A_WIDTH),
        "w_proj_b": lin(ks[13], (L, B_WIDTH, D_MODEL), B_WIDTH),
        "b_gate": 0.01 * jax.random.normal(ks[14], (L, 2 * D_MODEL), f32),
        "w_out": lin(ks[15], (L, D_MODEL, D_MODEL), D_MODEL),
        "norm_ffn": gain(ks[16], (L, D_MODEL)),
        "w_up": lin(ks[17], (L, D_MODEL, 2 * D_FF), D_MODEL),
        "conv_w": lin(ks[18], (L, CONV_W, 2 * D_FF), CONV_W),
        "conv_b": 0.01 * jax.random.normal(ks[19], (L, 2 * D_FF), f32),
        "w_down": lin(ks[20], (L, D_FF, D_MODEL), D_FF),
    }


def reference(x, rel_bias, norm_mix, w_in, a_q_norm, a_k_norm, b_cq_norm, b_ckv_norm,
              b_w_uq, b_w_ukv, b_q_norm, b_k_norm, w_proj_a, w_proj_b, b_gate, w_out,
              norm_ffn, w_up, conv_w, conv_b, w_down):
    B, S, _ = x.shape
    topk = min(TOPK_MAX, S // 4)
    pos = jnp.arange(S, dtype=jnp.int32)
    for l in range(DEPTH):
        h = rms_norm(x, norm_mix[l])
        z = h @ w_in[l]
        qa, ka, va, qi, ki, wi, cq, ckv, kr, ga, gb = split_cols(z)

        qa = rms_norm(qa.reshape(B, S, A_HEADS, A_HEAD_DIM), a_q_norm[l])
        ka = rms_norm(ka.reshape(B, S, A_HEADS, A_HEAD_DIM), a_k_norm[l])
        va = va.reshape(B, S, A_HEADS, A_HEAD_DIM)
        qi = qi.reshape(B, S, IDX_HEADS, IDX_DIM)
        y_a = sparse_indexed_attention(qa, ka, va, qi, ki, wi, rel_bias, topk)

        cq = rms_norm(cq, b_cq_norm[l])
        qb = (cq @ b_w_uq[l]).reshape(B, S, B_HEADS, B_QK)
        qb = jnp.concatenate([qb[..., :B_NOPE], rope(qb[..., B_NOPE:], pos)], axis=-1)
        ckv = rms_norm(ckv, b_ckv_norm[l])
        kv = (ckv @ b_w_ukv[l]).reshape(B, S, B_HEADS, B_NOPE + B_V)
        k_rope = jnp.broadcast_to(rope(kr[:, :, None, :], pos), (B, S, B_HEADS, B_ROPE))
        kb = jnp.concatenate([kv[..., :B_NOPE], k_rope], axis=-1)
        vb = kv[..., B_NOPE:]
        qb = rms_norm(qb, b_q_norm[l])
        kb = rms_norm(kb, b_k_norm[l])
        y_b = latent_attention(qb, kb, vb)

        gate_a = jax.nn.sigmoid(ga + b_gate[l, :D_MODEL])
        gate_b = jax.nn.sigmoid(gb + b_gate[l, D_MODEL:])
        merged = gate_a * (y_a @ w_proj_a[l]) + gate_b * (y_b @ w_proj_b[l])
        x = x + merged @ w_out[l]

        h = rms_norm(x, norm_ffn[l])
        u = causal_dwconv(h @ w_up[l], conv_w[l], conv_b[l])
        val, gat = u[..., :D_FF], u[..., D_FF:]
        x = x + (jax.nn.silu(gat) * val) @ w_down[l]
    return x
```

```python
import math
from contextlib import ExitStack

import numpy as np
import ml_dtypes

import concourse.bass as bass
import concourse.mybir as mybir
from concourse.bass_utils import run_bass_kernel_spmd

F32 = mybir.dt.float32
BF16 = mybir.dt.bfloat16
FP8 = mybir.dt.float8e4
AF = mybir.ActivationFunctionType
ALU = mybir.AluOpType
AX = mybir.AxisListType
NPBF = ml_dtypes.bfloat16

D = 1024
NCORE = 8
SB = 512
IN_COLS = 4840
D_FF = 2816
NEG_IDX = -1.0e30
NEG_B = -30000.0
NIT = 16
EPS = 1e-6


class Sched:
    ENG = ("pe", "act", "dve", "pool", "sp")

    def __init__(self, nc, es):
        self.nc = nc
        self.es = es
        self.ops = {e: [] for e in self.ENG}
        self.sems = {}
        self.cnt = {}
        self.seen = {e: {} for e in self.ENG}
        self.lastw = {}
        self.readers = {}
        self.nops = 0
        self.nwaits = 0
        for e in self.ENG[:4]:
            self._mk(e)

    def _mk(self, key):
        self.sems[key] = self.es.enter_context(self.nc.semaphore("s_%d" % len(self.sems)))
        self.cnt[key] = 0

    def _deps(self, eng, reads, writes):
        need = {}

        def add(ref):
            if ref is None:
                return
            k, v = ref
            if k not in self.ENG:
                v = self.cnt[k]
            if need.get(k, 0) < v:
                need[k] = v

        for r in reads:
            add(self.lastw.get(r))
        for w in writes:
            add(self.lastw.get(w))
            for k, v in self.readers.get(w, {}).items():
                add((k, v))
        waits = []
        for k, v in need.items():
            if k == "pe" and eng == "pe":
                continue
            if self.seen[eng].get(k, 0) < v:
                self.seen[eng][k] = v
                waits.append((k, v))
        return waits

    def _commit(self, ref, reads, writes):
        k, v = ref
        for r in reads:
            d = self.readers.setdefault(r, {})
            if d.get(k, 0) < v:
                d[k] = v
        for w in writes:
            self.lastw[w] = ref
            self.readers[w] = {}

    def op(self, eng, fn, reads=(), writes=()):
        waits = self._deps(eng, reads, writes)
        self.cnt[eng] += 1
        self.ops[eng].append((waits, fn, eng, 1))
        self._commit((eng, self.cnt[eng]), reads, writes)
        self.nops += 1
        self.nwaits += len(waits)

    def dma(self, q, out, in_, semkey, reads=(), writes=(), **kw):
        if semkey not in self.sems:
            self._mk(semkey)
        waits = self._deps(q, reads, writes)
        self.cnt[semkey] += 16
        self.ops[q].append((waits, lambda e: e.dma_start(out=out, in_=in_, **kw), semkey, 16))
        self._commit((semkey, self.cnt[semkey]), reads, writes)
        self.nops += 1
        self.nwaits += len(waits)

    def coll(self, kind, ins, outs, groups, reads=(), writes=()):
        if "cc" not in self.sems:
            self._mk("cc")
        waits = self._deps("pool", reads, writes)
        self.cnt["cc"] += 1
        self.ops["pool"].append((waits, lambda e: e.collective_compute(kind, ALU.bypass, replica_groups=groups, ins=ins, outs=outs),
                                 "cc", 1))
        self._commit(("cc", self.cnt["cc"]), reads, writes)

    def barrier(self, skip_cc=False):
        for eng in self.ENG:
            waits = []
            for k, v in self.cnt.items():
                if skip_cc and k == "cc":
                    continue
                if v > 0 and self.seen[eng].get(k, 0) < v:
                    self.seen[eng][k] = v
                    waits.append((k, v))
            self.ops[eng].append((waits, None, None, 0))
        if skip_cc:
            self.lastw = {r: ref for r, ref in self.lastw.items() if ref[0] == "cc"}
        else:
            self.lastw = {}
        self.readers = {}

    def emit(self, skip_cc=False):
        nc = self.nc
        self.barrier(skip_cc)
        fin = []
        with nc.Block() as block:
            def run(e, name):
                for waits, fn, sk, amt in self.ops[name]:
                    for k, v in waits:
                        e.wait_ge(self.sems[k], v)
                    if fn is not None:
                        fn(e).then_inc(self.sems[sk], amt)

            @block.tensor
            def _(e):
                run(e, "pe")

            @block.scalar
            def _(e):
                run(e, "act")

            @block.vector
            def _(e):
                run(e, "dve")

            @block.gpsimd
            def _(e):
                run(e, "pool")

            @block.sync
            def _(e):
                run(e, "sp")
        self.ops = {e: [] for e in self.ENG}


class Pipe:
    def __init__(self, lag):
        self.lag = lag
        self.q = []

    def push(self, fns):
        self.q.append(fns)
        if len(self.q) > self.lag:
            for f in self.q.pop(0):
                f()

    def flush(self):
        while self.q:
            for f in self.q.pop(0):
                f()


LAG = 2
MASK_ENG = ("dve", "dve")


def wq(ap):
    return "sp" if ap.dtype == BF16 else "pool"


class Ctx:
    def __init__(self, nc, es, env=None):
        self.nc = nc
        self.es = es
        self.env = env
        self.n = env.n if env else 0

    def sb(self, shape, dt, name=None):
        self.n += 1
        if self.env:
            self.env.n = self.n
        return self.es.enter_context(self.nc.sbuf_tensor(name or ("t%d" % self.n), list(shape), dt))

    def ps(self, shape, dt, name=None):
        self.n += 1
        if self.env:
            self.env.n = self.n
        return self.es.enter_context(self.nc.psum_tensor(name or ("p%d" % self.n), list(shape), dt))

    def din(self, name, shape, dt):
        if self.env:
            ap = self.env.d[name]
            assert tuple(ap.shape) == tuple(shape), (name, ap.shape, shape)
            return ap
        return self.nc.dram_tensor(name, list(shape), dt, kind="ExternalInput").ap()

    def dout(self, name, shape, dt):
        if self.env:
            ap = self.env.d[name]
            assert tuple(ap.shape) == tuple(shape), (name, ap.shape, shape)
            return ap
        return self.nc.dram_tensor(name, list(shape), dt, kind="ExternalOutput").ap()


class Env:
    def __init__(self, nc, sc):
        self.nc = nc
        self.sc = sc
        self.d = {}
        self.n = 0


def t5_bucket_np(rel):
    nb = 16
    max_exact = 8
    side = np.where(rel > 0, nb, 0)
    n = np.abs(rel)
    nf = np.maximum(n, 1).astype(np.float32)
    large = max_exact + (np.log(nf / max_exact) / math.log(128 / max_exact) * (nb - max_exact)).astype(np.int32)
    large = np.minimum(large, nb - 1)
    return side + np.where(n < max_exact, n, large)


def consts_B(j):
    c = {}
    c["ident"] = np.eye(128, dtype=np.float32).astype(NPBF)
    mg = np.zeros((128, 8, 128), np.float32)
    for p in range(128):
        t16 = p // 8
        for g in range(8):
            mg[p, g, 16 * g + t16] = 1.0
    c["mg"] = mg.astype(NPBF)
    t = np.arange(512)[:, None]
    s = np.arange(512)[None, :]
    diag = np.where((s // 64) > (t // 64), NEG_IDX, 0.0).astype(np.float32)
    c["diagi"] = diag.reshape(4, 128, 512).transpose(1, 0, 2).astype(NPBF).copy()
    selcol = np.zeros((128, 8), np.float32)
    for m in range(4):
        selcol[:, m] = 1.0 if m == j else 0.0
        selcol[:, 4 + m] = NEG_IDX if m > j else 0.0
    c["selcol"] = selcol
    sel = np.zeros((128, 8, 128), np.float32)
    for m in range(4):
        if m == j:
            sel[:, m, :] = np.eye(128)
        if m == j:
            sel[:, 4 + m, :] = np.eye(128)
    c["sel"] = sel.astype(NPBF)
    negrow = np.zeros((1, 4, 128), np.float32)
    for m in range(4):
        if m > j:
            negrow[0, m, :] = NEG_B
    c["negrow"] = negrow.astype(NPBF)
    c["onesrow"] = np.ones((1, 512), np.float32).astype(NPBF)
    k = np.arange(128)[:, None, None]
    sub = np.arange(4)[None, :, None]
    tt = np.arange(512)[None, None, :]
    bd = np.where(((128 * sub + k) // 64) > (tt // 64), NEG_B, 0.0).astype(np.float32)
    c["bd"] = bd.astype(NPBF)
    e65 = np.zeros((65, 64), np.float32)
    e65[64, :] = 1.0
    c["e65"] = e65
    c["pow2"] = np.tile((2.0 ** -(np.arange(NIT) + 1.0))[None, :], (128, 1)).astype(np.float32)
    return c


def tb_bucket_idx():
    k = np.arange(128)[:, None]
    cc = np.arange(384)[None, :]
    return t5_bucket_np(k - cc + 128)


def build_B(S, NI, env=None):
    NT = NI * SB
    nc = env.nc if env else bass.Bass("TRN2", target_bir_lowering=False)
    es = ExitStack()
    with es:
        cx = Ctx(nc, es, env)
        sc = env.sc if env else Sched(nc, es)
        d_qiT = cx.din("qiT", [64, NT * 8], BF16)
        d_w2T = cx.din("w2T", [128, NT * 8 // 128], F32)
        d_qaT = cx.din("qaT", [512, NT], BF16)
        d_qbT = cx.din("qbT", [768, NT], BF16)
        d_kiT = cx.din("kiT_g", [4 * NI * 64, 512], BF16)
        d_kaT = cx.din("kaT_g", [4 * NI * 512, 512], BF16)
        d_va = cx.din("va_g", [4 * NT, 520], BF16)
        d_kbT = cx.din("kbT_g", [4 * NI * 768, 512], BF16)
        d_vb = cx.din("vb_g", [4 * NT, 520], BF16)
        d_ident = cx.din("ident", [128, 128], BF16)
        d_mg = cx.din("mg", [128, 8, 128], BF16)
        d_diagi = cx.din("diagi", [128, 4, 512], BF16)
        d_selcol = cx.din("selcol", [128, 8], F32)
        d_sel = cx.din("sel", [128, 8, 128], BF16)
        d_negrow = cx.din("negrow", [1, 4, 128], BF16)
        d_onesrow = cx.din("onesrow", [1, 512], BF16)
        d_bd = cx.din("bd", [128, 4, 512], BF16)
        d_e65 = cx.din("e65", [65, 64], F32)
        d_pow2 = cx.din("pow2", [128, NIT], F32)
        d_tbg = cx.din("tbg", [128, 8, 384], F32)
        d_far = cx.din("far", [128, 8], F32)
        d_yaT = cx.dout("yaT", [512, NT], BF16)
        d_ybT = cx.dout("ybT", [512, NT], BF16)

        ident = cx.sb([128, 128], BF16)
        mg = cx.sb([128, 8, 128], BF16)
        diagi = cx.sb([128, 4, 512], BF16)
        selcol = cx.sb([128, 8], F32)
        sel = cx.sb([128, 8, 128], BF16)
        negrow = cx.sb([1, 4, 128], BF16)
        onesrow = cx.sb([1, 512], BF16)
        bd = cx.sb([128, 4, 512], BF16)
        e65 = cx.sb([65, 64], F32)
        pow2 = cx.sb([128, NIT], F32)
        far = cx.sb([128, 8], F32)
        dtmp = cx.sb([128, 2048], F32)
        tbg3 = dtmp[:, 0:1536].rearrange("p (h c) -> p h c", h=4)
        tb = cx.sb([128, 8, 384], BF16)
        ki_t = [cx.sb([64, 512], BF16) for _ in range(2)]
        w2T = cx.sb([128, NT * 8 // 128], F32)
        for i_, (dst, src) in enumerate([(ident, d_ident), (mg, d_mg), (diagi, d_diagi), (selcol, d_selcol),
                                         (sel, d_sel), (negrow, d_negrow), (onesrow, d_onesrow), (bd, d_bd),
                                         (e65, d_e65), (pow2, d_pow2), (far, d_far), (far, d_far)]):
            sc.dma("sp", dst[:], src, ("c", i_), writes=[("c", i_)])
        for hq in range(2):
            sc.dma("sp", tbg3, d_tbg[:, 4 * hq:4 * hq + 4, :], "tbg", writes=["dtmp"])
            sc.op("dve", lambda e, hq=hq: e.tensor_tensor(out=tb[:, 4 * hq:4 * hq + 4, :], in0=tbg3,
                                                          in1=far[:, 4 * hq:4 * hq + 4].unsqueeze(2).to_broadcast([128, 4, 384]),
                                                          op=ALU.subtract),
                  reads=[("c", 11), "dtmp"], writes=["tb"])

        score = cx.sb([128, S], F32)
        maskT = cx.sb([128, S // 128, 512], FP8)
        junk = cx.sb([128, 2048], BF16)
        qi_sb = cx.sb([64, SB * 8], BF16)
        qa_sb = qi_sb[:, :].rearrange("d (h t) -> d h t", h=8)
        qb2_sb = cx.sb([96, 2, SB], BF16)
        w2g = cx.sb([128, 8, 128], BF16)
        NH = 3
        hid = [cx.sb([128, 512], BF16) for _ in range(NH)]
        mts = [cx.sb([128, 512], BF16) for _ in range(2)]
        NPT = 4
        pT = [cx.sb([128, 512], BF16) for _ in range(NPT)]
        kt_b = [cx.sb([96, 4, 512], BF16) for _ in range(2)]
        v_b = [cx.sb([128, 4, 260], BF16) for _ in range(2)]
        o_sb = [cx.sb([65, 512], F32) for _ in range(2)]
        rec = [cx.sb([64, 512], F32) for _ in range(1)]
        yT = [cx.sb([64, 512], BF16) for _ in range(2)]
        st = cx.sb([128, 16], F32)
        cnts = cx.sb([128, 8], F32)
        steps = cx.sb([128, NIT], F32)
        ps = [cx.ps([128, 512], F32) for _ in range(8)]

        cst = lambda *idx: [("c", k) for k in idx]
        state = dict(hid=0, pT=0, kv=0, o=0, mts=0, ki=0)

        for i in range(NI):
            nkt = 4 * i + 4
            tok0 = i * SB
            sc.dma("sp", qi_sb[:], d_qiT[:, tok0 * 8:(tok0 + SB) * 8], "q", writes=["qi"])
            sc.dma("sp", w2T[:, 32 * i:32 * i + 32], d_w2T[:, 32 * i:32 * i + 32], "q", writes=[("w2T", i)],
                   allow_slow_non_contiguous=True)

            def b_piece(hpair, i=i, nkt=nkt, tok0=tok0):
                sc.dma("sp", qb2_sb[:], d_qbT[192 * hpair:192 * hpair + 192, tok0:tok0 + SB].rearrange("(h d) t -> d h t", d=96),
                       "q2", writes=["qb2"])
                pipe = Pipe(LAG)
                u = 0
                for kt in range(nkt):
                    slot = state["kv"] % 2
                    state["kv"] += 1
                    ktile, vtile = kt_b[slot], v_b[slot]
                    sc.dma("sp", ktile[:, 0:2, :],
                           d_kbT[768 * kt + 192 * hpair:768 * kt + 192 * hpair + 192, :].rearrange("(h d) s -> d h s", d=96),
                           ("kv", slot), reads=[("g", "kbT", kt // 4)], writes=[("k", slot)])
                    sc.dma("sp", vtile[:, :, 0:130],
                           d_vb[512 * kt:512 * kt + 512, 130 * hpair:130 * hpair + 130].rearrange("(u p) c -> p u c", p=128),
                           ("kv", slot), reads=[("g", "vb", kt // 4)], writes=[("v", slot)])
                    diag = kt >= 4 * i
                    m = kt - 4 * i
                    for sub in range(4):
                        for hh in range(2):
                            sb_i = 4 + (u % 4)
                            u += 1
                            pss = ps[sb_i]
                            sc.op("pe", lambda e, pss=pss, ktile=ktile, hh=hh, sub=sub, diag=diag: e.matmul(
                                pss[:, :], ktile[:, hh, sub * 128:(sub + 1) * 128], qb2_sb[:, hh, :], start=True, stop=not diag),
                                reads=[("k", slot), "qb2"], writes=[("ps", sb_i)])
                            if diag:
                                sc.op("pe", lambda e, pss=pss, m=m, sub=sub: e.matmul(
                                    pss[:, :], sel[:, m, :], bd[:, sub, :], start=False, stop=False),
                                    reads=cst(4, 7), writes=[("ps", sb_i)])
                                sc.op("pe", lambda e, pss=pss, m=m: e.matmul(
                                    pss[:, :], negrow[0:1, m, :], onesrow[0:1, :], start=False, stop=True),
                                    reads=cst(5, 6), writes=[("ps", sb_i)])
                            pslot = state["pT"] % NPT
                            state["pT"] += 1
                            pt = pT[pslot]
                            sc.op("act", lambda e, pt=pt, pss=pss: e.activation(out=pt[:, :], in_=pss[:, :], func=AF.Exp),
                                  reads=[("ps", sb_i)], writes=[("pT", pslot)])
                            first = (kt == 0 and sub == 0)
                            last = (kt == nkt - 1 and sub == 3)
                            pipe.push([lambda hh=hh, vtile=vtile, sub=sub, pt=pt, first=first, last=last, slot=slot, pslot=pslot: sc.op(
                                "pe", lambda e: e.matmul(
                                    ps[hh][0:65, :], vtile[:, sub, hh * 65:(hh + 1) * 65], pt[:, :], start=first, stop=last),
                                reads=[("v", slot), ("pT", pslot)], writes=[("ps", hh)])])
                pipe.flush()
                for hh in range(2):
                    h = 2 * hpair + hh
                    _finalize(sc, state, ps[hh], ps[4 + hh], o_sb, rec, yT, e65, 512,
                              d_ybT[h * 64:(h + 1) * 64, tok0:tok0 + SB], hh, 4 + hh)

            for half in range(1):
                for qb_i in range(4):
                    qq = qb_i
                    tq = qb_i * 128
                    gcol = (tok0 + tq) * 8 // 128
                    sc.op("dve", lambda e, gcol=gcol: e.tensor_tensor(
                        out=w2g[:], in0=mg[:], in1=w2T[:, gcol:gcol + 8].unsqueeze(2).to_broadcast([128, 8, 128]),
                        op=ALU.mult), reads=cst(1) + [("w2T", i)], writes=["w2g"])
                    pipe = Pipe(LAG)
                    for kt in range(nkt):
                        psc_i = 4 + (kt % 2)
                        psc = ps[psc_i]
                        kis = state["ki"] % 2
                        state["ki"] += 1
                        kit = ki_t[kis]
                        sc.dma("sp", kit[:, :], d_kiT[64 * kt:64 * kt + 64, :], ("ki", kis), reads=[("g", "kiT", kt // 4)], writes=[("ki", kis)])
                        for g in range(8):
                            ph_i = (state["hid"]) % NH
                            state["hid"] += 1
                            ph = ps[ph_i]
                            hd = hid[ph_i]
                            c0 = (tq + 16 * g) * 8
                            sc.op("pe", lambda e, ph=ph, c0=c0, kit=kit: e.matmul(
                                ph[:, :], qi_sb[:, c0:c0 + 128], kit[:, :], start=True, stop=True),
                                reads=["qi", ("ki", kis)], writes=[("ps", ph_i)])
                            sc.op("act", lambda e, hd=hd, ph=ph: e.activation(out=hd[:, :], in_=ph[:, :], func=AF.Relu),
                                  reads=[("ps", ph_i)], writes=[("hid", ph_i)])
                            fns = [lambda psc=psc, g=g, hd=hd, ph_i=ph_i, psc_i=psc_i: sc.op("pe", lambda e: e.matmul(
                                psc[:, :], w2g[:, g, :], hd[:, :], start=(g == 0), stop=(g == 7)),
                                reads=["w2g", ("hid", ph_i)], writes=[("ps", psc_i)])]
                            if g == 7:
                                sco = score[:, kt * 512:(kt + 1) * 512]
                                if kt >= 4 * i:
                                    m = kt - 4 * i
                                    fns.append(lambda sco=sco, psc=psc, m=m, qb_i=qb_i, psc_i=psc_i, kt=kt: sc.op(
                                        "dve", lambda e: e.scalar_tensor_tensor(
                                            out=sco, in0=diagi[:, qb_i, :], scalar=selcol[:, m:m + 1], in1=psc[:, :],
                                            op0=ALU.mult, op1=ALU.add), reads=[("ps", psc_i)] + cst(2, 3), writes=[("sc", kt)]))
                                    fns.append(lambda sco=sco, m=m, kt=kt: sc.op("dve", lambda e: e.tensor_scalar(
                                        out=sco, in0=sco, scalar1=selcol[:, 4 + m:5 + m], scalar2=None, op0=ALU.add),
                                        reads=[("sc", kt)] + cst(3), writes=[("sc", kt)]))
                                else:
                                    fns.append(lambda sco=sco, psc=psc, psc_i=psc_i, kt=kt: sc.op(
                                        "dve", lambda e: e.tensor_copy(out=sco, in_=psc[:, :]),
                                        reads=[("ps", psc_i)], writes=[("sc", kt)]))
                            pipe.push(fns)
                    pipe.flush()
                    n = nkt * 512
                    nf = 4 * i * 512
                    allsc = [("sc", kt) for kt in range(nkt)]
                    sc.op("dve", lambda e, n=n: e.tensor_reduce(out=st[:, 0:1], in_=score[:, 0:n], axis=AX.X, op=ALU.max),
                          reads=allsc, writes=["st0"])
                    sc.op("dve", lambda e, nf=nf, n=n: e.tensor_scalar(
                        out=dtmp[:, 0:2048], in0=score[:, nf:n], scalar1=-1.0e29, scalar2=4.0e30, op0=ALU.is_lt, op1=ALU.mult),
                        reads=allsc, writes=["dtmp"])
                    sc.op("dve", lambda e, nf=nf, n=n: e.tensor_tensor(
                        out=dtmp[:, 0:2048], in0=dtmp[:, 0:2048], in1=score[:, nf:n], op=ALU.add),
                        reads=allsc + ["dtmp"], writes=["dtmp"])
                    sc.op("dve", lambda e: e.tensor_reduce(out=st[:, 3:4], in_=dtmp[:, 0:2048], axis=AX.X, op=ALU.min),
                          reads=["dtmp"], writes=["st3"])
                    if i > 0:
                        sc.op("dve", lambda e, nf=nf: e.tensor_reduce(out=st[:, 1:2], in_=score[:, 0:nf], axis=AX.X, op=ALU.min),
                              reads=allsc, writes=["st1"])
                        sc.op("dve", lambda e: e.tensor_tensor(out=st[:, 3:4], in0=st[:, 3:4], in1=st[:, 1:2], op=ALU.min),
                              reads=["st1", "st3"], writes=["st3"])
                    sc.op("dve", lambda e: e.tensor_tensor(out=st[:, 4:5], in0=st[:, 0:1], in1=st[:, 3:4], op=ALU.subtract),
                          reads=["st0", "st3"], writes=["st4"])
                    sc.op("dve", lambda e: e.tensor_scalar(out=steps[:, :], in0=pow2[:, :], scalar1=st[:, 4:5], scalar2=None,
                                                           op0=ALU.mult), reads=["st4"] + cst(9), writes=["steps"])
                    sc.op("dve", lambda e: e.scalar_tensor_tensor(out=st[:, 5:6], in0=st[:, 4:5], scalar=0.5, in1=st[:, 3:4],
                                                                  op0=ALU.mult, op1=ALU.add),
                          reads=["st4", "st3"], writes=["thr"])
                    nch = (n + 2047) // 2048
                    for it in range(NIT):
                        for ch in range(nch):
                            a0 = ch * 2048
                            a1 = min(n, a0 + 2048)
                            sc.op("dve", lambda e, a0=a0, a1=a1, ch=ch: e.tensor_scalar(
                                out=junk[:, 0:a1 - a0], in0=score[:, a0:a1], scalar1=st[:, 5:6], scalar2=0.0,
                                op0=ALU.is_ge, op1=ALU.add, accum_out=cnts[:, ch:ch + 1]),
                                reads=allsc + ["thr"], writes=["junk", ("cnt", ch)])
                        if nch > 1:
                            sc.op("dve", lambda e, nch=nch: e.tensor_reduce(out=st[:, 8:9], in_=cnts[:, 0:nch], axis=AX.X, op=ALU.add),
                                  reads=[("cnt", ch) for ch in range(nch)], writes=["ctot"])
                            csrc = st[:, 8:9]
                            crd = ["ctot"]
                        else:
                            csrc = cnts[:, 0:1]
                            crd = [("cnt", 0)]
                        sc.op("dve", lambda e, csrc=csrc: e.tensor_scalar(out=st[:, 6:7], in0=csrc, scalar1=255.5, scalar2=0.5,
                                                                         op0=ALU.is_ge, op1=ALU.subtract),
                              reads=crd, writes=["g"])
                        sc.op("dve", lambda e, it=it: e.scalar_tensor_tensor(out=st[:, 5:6], in0=st[:, 6:7], scalar=steps[:, it:it + 1],
                                                                             in1=st[:, 5:6], op0=ALU.mult, op1=ALU.add),
                              reads=["g", "steps", "thr"], writes=["thr"])
                    sc.op("dve", lambda e: e.scalar_tensor_tensor(out=st[:, 7:8], in0=st[:, 4:5], scalar=-(2.0 ** -(NIT - 1)),
                                                                  in1=st[:, 5:6], op0=ALU.mult, op1=ALU.add),
                          reads=["st4", "thr"], writes=["thrF"])
                    b_piece(qb_i)
                    for kt in range(nkt):
                        ms = state["mts"] % 2
                        state["mts"] += 1
                        mt = mts[ms]
                        sc.op("dve", lambda e, mt=mt, kt=kt: e.tensor_scalar(
                            out=mt[:, :], in0=score[:, kt * 512:(kt + 1) * 512], scalar1=st[:, 7:8], scalar2=None, op0=ALU.is_ge),
                            reads=[("sc", kt), "thrF"], writes=[("mts", ms)])
                        pti = 6 + ms
                        ptb = ps[pti][:, :].bitcast(BF16)
                        for sub in range(4):
                            sc.op("pe", lambda e, ptb=ptb, mt=mt, sub=sub: e.transpose(
                                ptb[:, sub * 128:(sub + 1) * 128], mt[:, sub * 128:(sub + 1) * 128], ident[:, :]),
                                reads=[("mts", ms)] + cst(0), writes=[("ps", pti)])
                        sc.op("act", lambda e, ptb=ptb, kt=kt, qq=qq: e.activation(
                            out=maskT[:, kt * 4:kt * 4 + 4, qq * 128:(qq + 1) * 128],
                            in_=ptb[:, 0:512].rearrange("p (u t) -> p u t", u=4), func=AF.Copy, saturate=False),
                            reads=[("ps", pti)], writes=[("mT", kt, qq)])
            sc.dma("sp", qa_sb, d_qaT[:, tok0:tok0 + SB].rearrange("(h d) t -> d h t", d=64), "q", writes=["qi"])
            for hg in range(2):
                pipe = Pipe(LAG)
                u = 0
                for kt in range(nkt):
                    slot = state["kv"] % 2
                    state["kv"] += 1
                    ktile, vtile = kt_b[slot], v_b[slot]
                    sc.dma("sp", ktile[0:64, :, :],
                           d_kaT[512 * kt + 256 * hg:512 * kt + 256 * hg + 256, :].rearrange("(h d) s -> d h s", d=64),
                           ("kv", slot), reads=[("g", "kaT", kt // 4)], writes=[("k", slot)])
                    sc.dma("sp", vtile[:, :, :],
                           d_va[512 * kt:512 * kt + 512, 260 * hg:260 * hg + 260].rearrange("(u p) c -> p u c", p=128),
                           ("kv", slot), reads=[("g", "va", kt // 4)], writes=[("v", slot)])
                    for sub in range(4):
                        roles = []
                        if kt >= 4 * i:
                            roles.append((kt - 4 * i, sub))
                        if sub == 3 and kt >= 4 * i - 1 and kt - 4 * i + 1 <= 3:
                            roles.append((4 + (kt - 4 * i + 1), -1))
                        mms = []
                        for (si, sg) in roles:
                            tlo = max(128 * sg - 128, 0)
                            thi = min(128 * sg + 256, 512)
                            if thi > tlo:
                                mms.append((si, tlo, thi, tlo - 128 * sg + 128, thi - 128 * sg + 128))
                        for hh in range(4):
                            h = 4 * hg + hh
                            sb_i = 4 + (u % 4)
                            u += 1
                            pss = ps[sb_i]
                            sc.op("pe", lambda e, pss=pss, ktile=ktile, hh=hh, h=h, sub=sub, nm=len(mms): e.matmul(
                                pss[:, :], ktile[0:64, hh, sub * 128:(sub + 1) * 128], qa_sb[:, h, :], start=True, stop=(nm == 0),
                                skip_group_check=True),
                                reads=[("k", slot), "qi"], writes=[("ps", sb_i)])
                            for idx, (si, tlo, thi, clo, chi) in enumerate(mms):
                                sc.op("pe", lambda e, pss=pss, si=si, tlo=tlo, thi=thi, clo=clo, chi=chi, h=h,
                                      lastm=(idx == len(mms) - 1): e.matmul(
                                    pss[:, tlo:thi], sel[:, si, :], tb[:, h, clo:chi], start=False, stop=lastm, skip_group_check=True),
                                    reads=cst(4) + ["tb"], writes=[("ps", sb_i)])
                            pslot = state["pT"] % NPT
                            state["pT"] += 1
                            pt = pT[pslot]
                            sc.op("act", lambda e, pt=pt, pss=pss: e.activation(out=pt[:, :], in_=pss[:, :], func=AF.Exp),
                                  reads=[("ps", sb_i)], writes=[("pT", pslot)])
                            meng = MASK_ENG[u % len(MASK_ENG)]
                            sc.op(meng, lambda e, pt=pt, kt=kt, sub=sub: e.tensor_tensor(
                                out=pt[:, :], in0=pt[:, :], in1=maskT[:, kt * 4 + sub, :], op=ALU.mult),
                                reads=[("pT", pslot)] + [("mT", kt, q_) for q_ in range(4)], writes=[("pT", pslot)])
                            first = (kt == 0 and sub == 0)
                            last = (kt == nkt - 1 and sub == 3)
                            pipe.push([lambda hh=hh, vtile=vtile, sub=sub, pt=pt, first=first, last=last, slot=slot, pslot=pslot: sc.op(
                                "pe", lambda e: e.matmul(
                                    ps[hh][0:65, :], vtile[:, sub, hh * 65:(hh + 1) * 65], pt[:, :], start=first, stop=last),
                                reads=[("v", slot), ("pT", pslot)], writes=[("ps", hh)])])
                pipe.flush()
                for hh in range(4):
                    h = 4 * hg + hh
                    _finalize(sc, state, ps[hh], ps[4 + hh], o_sb, rec, yT, e65, 512,
                              d_yaT[h * 64:(h + 1) * 64, tok0:tok0 + SB], hh, 4 + hh)
        sc.emit()
        print("B program: ops", sc.nops, "waits", sc.nwaits)
    return nc


def _finalize(sc, state, acc, pden, o_sb, rec, yT, e65, n, dst, acc_i, den_i, outs=None):
    k = state["o"] % 2
    state["o"] += 1
    kr = k % len(rec)
    o, r, y = o_sb[k], rec[kr], yT[k]
    sc.op("act", lambda e: e.activation(out=o[:, 0:n], in_=acc[0:65, 0:n], func=AF.Copy),
          reads=[("ps", acc_i)], writes=[("o", k)])
    sc.op("pe", lambda e: e.matmul(pden[0:64, 0:n], e65[:, :], o[:, 0:n], start=True, stop=True),
          reads=[("o", k), ("c", 8)], writes=[("ps", den_i)])
    sc.op("dve", lambda e: e.reciprocal(out=r[:, 0:n], in_=pden[0:64, 0:n]),
          reads=[("ps", den_i)], writes=[("rec", kr)])
    sc.op("dve", lambda e: e.tensor_tensor(out=y[:, 0:n], in0=o[0:64, 0:n], in1=r[:, 0:n], op=ALU.mult),
          reads=[("o", k), ("rec", kr)], writes=[("y", k)])
    if outs is None:
        outs = [(dst, 0, n)]
    for (d, a, b) in outs:
        sc.dma("sp", d, y[:, a:b], ("yo", k), reads=[("y", k)], writes=[])


def core_tokens(NI, j):
    return np.concatenate([np.arange(SB * (4 * i + j), SB * (4 * i + j) + SB) for i in range(NI)])


def maps_B(rA, rel_bias, NI):
    bidx = tb_bucket_idx()
    tbg = np.ascontiguousarray(np.asarray(rel_bias, np.float32)[bidx].transpose(0, 2, 1))
    far = np.ascontiguousarray(np.broadcast_to(np.asarray(rel_bias, np.float32)[15][None, :], (128, 8)))
    maps = []
    for c in range(NCORE):
        b, j = c // 4, c % 4
        NT = NI * SB
        m = dict(consts_B(j))
        m["tbg"] = tbg
        m["far"] = far
        for k in ("qiT", "qaT", "qbT"):
            m[k] = rA[c][k]
        m["w2T"] = np.ascontiguousarray(rA[c]["w2"].reshape(NT * 8 // 128, 128).T)
        for k, rows in (("kiT", 64), ("kaT", 512), ("kbT", 768), ("va", 512), ("vb", 512)):
            m[k + "_g"] = np.concatenate([rA[4 * b + kt % 4][k][rows * (kt // 4):rows * (kt // 4) + rows] for kt in range(4 * NI)], axis=0)
        maps.append(m)
    return maps


def split_G(G, S, NI):
    out = []
    for c in range(NCORE):
        b, j = c // 4, c % 4
        tok = core_tokens(NI, j)
        NT = len(tok)
        r = {}
        r["qiT"] = np.ascontiguousarray(G["qiT"][b].reshape(64, S, 8)[:, tok, :].reshape(64, NT * 8))
        r["w2"] = np.ascontiguousarray(G["w2"][b][tok])
        for k in ("qaT", "qbT"):
            r[k] = np.ascontiguousarray(G[k][b][:, :, tok].reshape(-1, NT))
        for k in ("kaT", "kbT"):
            a_ = G[k][b][:, :, tok].reshape(-1, NI, SB)
            r[k] = np.ascontiguousarray(a_.transpose(1, 0, 2).reshape(-1, SB))
        r["kiT"] = np.ascontiguousarray(G["kiT"][b][:, tok].reshape(64, NI, SB).transpose(1, 0, 2).reshape(-1, SB))
        for k in ("va", "vb"):
            r[k] = np.ascontiguousarray(G[k][b][tok])
        out.append(r)
    return out


def gather_B(results, S, NI):
    yaT = np.zeros((2, 8, 64, S), NPBF)
    ybT = np.zeros((2, 8, 64, S), NPBF)
    for c in range(NCORE):
        b, j = c // 4, c % 4
        tok = core_tokens(NI, j)
        yaT[b][:, :, tok] = results[c]["yaT"].reshape(8, 64, -1)
        ybT[b][:, :, tok] = results[c]["ybT"].reshape(8, 64, -1)
    return yaT, ybT


OFF = dict(qa=0, ka=512, va=1024, qi=1536, ki=2048, wi=2112, cq=2120, ckv=2504, kr=2760, ga=2792)


def build_A(NT, env=None):
    nc = env.nc if env else bass.Bass("TRN2", target_bir_lowering=False)
    es = ExitStack()
    with es:
        cx = Ctx(nc, es, env)
        sc = env.sc if env else Sched(nc, es)
        d_x = cx.din("x", [NT, D], F32)
        d_win = cx.din("w_in", [D, IN_COLS], F32)
        d_wuq = cx.din("w_uq", [384, 768], F32)
        d_wukv = cx.din("w_ukv", [256, 1024], F32)
        d_gmix = cx.din("gmix", [128, 1024], F32)
        d_gqk = cx.din("gqk", [128, 1024], F32)
        d_gcq = cx.din("gcq", [128, 384], F32)
        d_gckv = cx.din("gckv", [128, 256], F32)
        d_gbq = cx.din("gbq", [128, 768], F32)
        d_gbk = cx.din("gbk", [128, 768], F32)
        d_bgate = cx.din("bgate", [128, 2048], F32)
        d_cos = cx.din("cos", [NT, 16], F32)
        d_sin = cx.din("sin", [NT, 16], F32)
        d_ident = cx.din("ident", [128, 128], BF16)
        o_qiT = cx.dout("qiT", [64, NT * 8], BF16)
        o_w2 = cx.dout("w2", [NT, 8], F32)
        o_qaT = cx.dout("qaT", [512, NT], BF16)
        o_qbT = cx.dout("qbT", [768, NT], BF16)
        o_kiT = cx.dout("kiT", [NT // 512 * 64, 512], BF16)
        o_kaT = cx.dout("kaT", [NT, 512], BF16)
        o_va = cx.dout("va", [NT, 520], BF16)
        o_kbT = cx.dout("kbT", [NT // 512 * 768, 512], BF16)
        o_vb = cx.dout("vb", [NT, 520], BF16)
        o_gate = cx.dout("gate", [NT, 2048], F32)

        win = cx.sb([128, 8, IN_COLS], BF16)
        wuq = cx.sb([128, 3, 768], BF16)
        wukv = cx.sb([128, 2, 1024], BF16)
        gmix = cx.sb([128, 1024], F32)
        gqk = cx.sb([128, 1024], F32)
        gcq = cx.sb([128, 384], F32)
        gckv = cx.sb([128, 256], F32)
        gbq = cx.sb([128, 768], F32)
        gbk = cx.sb([128, 768], F32)
        bgate = cx.sb([128, 2048], F32)
        ident = cx.sb([128, 128], BF16)
        for kc in range(8):
            sc.dma(wq(d_win), win[:, kc, :], d_win[kc * 128:(kc + 1) * 128, :], ("w", kc), writes=[("win", kc)])
        sc.dma(wq(d_wuq), wuq[:], d_wuq.rearrange("(k p) c -> p k c", p=128), "wuq", writes=["wuq"])
        sc.dma(wq(d_wukv), wukv[:], d_wukv.rearrange("(k p) c -> p k c", p=128), "wukv", writes=["wukv"])
        for nm, t_, d_ in (("gmix", gmix, d_gmix), ("gqk", gqk, d_gqk), ("gcq", gcq, d_gcq), ("gckv", gckv, d_gckv),
                           ("gbq", gbq, d_gbq), ("gbk", gbk, d_gbk), ("bgate", bgate, d_bgate), ("ident", ident, d_ident)):
            sc.dma("sp", t_[:], d_, nm, writes=[nm])
        sc.op("dve", lambda e: e.tensor_scalar(out=gqk[:, 0:512], in0=gqk[:, 0:512], scalar1=0.125, scalar2=None, op0=ALU.mult),
              reads=["gqk"], writes=["gqk"])
        sc.op("dve", lambda e: e.tensor_scalar(out=gbq[:, :], in0=gbq[:, :], scalar1=96.0 ** -0.5, scalar2=None, op0=ALU.mult),
              reads=["gbq"], writes=["gbq"])

        xt = [cx.sb([128, 1024], F32) for _ in range(2)]
        junkb = cx.sb([128, 1024], BF16)
        xg = cx.sb([128, 1024], BF16)
        xT = cx.sb([128, 1024], BF16)
        zz = [cx.sb([128, IN_COLS], F32) for _ in range(2)]
        tmpf = cx.sb([128, 1024], F32)
        qk_b = cx.sb([128, 1024], BF16)
        qkT = cx.sb([64, 16, 128], BF16)
        va_sb = cx.sb([128, 8, 65], BF16)
        vb_sb = cx.sb([128, 8, 65], BF16)
        qi_b = cx.sb([128, 512], BF16)
        qiT_sb = cx.sb([64, 1024], BF16)
        ki_b = cx.sb([128, 64], BF16)
        kiT_sb = cx.sb([64, 128], BF16)
        w2_sb = cx.sb([128, 8], F32)
        cq_b = cx.sb([128, 384], BF16)
        cqT = cx.sb([128, 3, 128], BF16)
        ckv_b = cx.sb([128, 256], BF16)
        ckvT = cx.sb([128, 2, 128], BF16)
        qb_f = cx.sb([128, 8, 96], F32)
        kv_f = cx.sb([128, 8, 128], F32)
        kb_f = cx.sb([128, 8, 96], F32)
        qb_b = cx.sb([128, 8, 96], BF16)
        kb_b = cx.sb([128, 8, 96], BF16)
        qbT_sb = cx.sb([96, 8, 128], BF16)
        kbT_sb = cx.sb([96, 8, 128], BF16)
        gate_sb = cx.sb([128, 1024], F32)
        cs = cx.sb([128, 32], F32)
        rt = cx.sb([128, 8, 64], F32)
        krr = cx.sb([128, 32], F32)
        st = cx.sb([128, 64], F32)
        ps = [cx.ps([128, 512], F32) for _ in range(8)]
        sc.op("dve", lambda e: e.memset(va_sb[:], 1.0), writes=["va"])
        sc.op("dve", lambda e: e.memset(vb_sb[:], 1.0), writes=["vb"])

        def rs_of(src, dst, dim, rd, wr):
            sc.op("dve", lambda e: e.tensor_scalar(out=dst, in0=src, scalar1=1.0 / dim, scalar2=EPS, op0=ALU.mult, op1=ALU.add),
                  reads=rd, writes=[wr])
            sc.op("act", lambda e: e.activation(out=dst, in_=dst, func=AF.Sqrt), reads=[wr], writes=[wr])
            sc.op("dve", lambda e: e.reciprocal(out=dst, in_=dst), reads=[wr], writes=[wr])

        def transposes(src_fn, n, rows, pbank, rd):
            ptb = ps[pbank][:, :].bitcast(BF16)
            for k in range(n):
                sc.op("pe", lambda e, k=k: e.transpose(ptb[0:rows, k * 128:(k + 1) * 128], src_fn(k), ident[:, :]),
                      reads=rd + ["ident"], writes=[("ps", pbank)])
            return ptb

        ntile = NT // 128
        def front(ti):
                r0 = ti * 128
                par = ti % 2
                z_ = zz[par]
                x_ = xt[ti % 2]
                xk = ("x", ti % 2)
                sc.dma("sp", x_[:], d_x[r0:r0 + 128, :], xk, writes=[xk])
                sc.op("act", lambda e, x_=x_: e.activation(out=junkb[:, :], in_=x_[:, :], func=AF.Square, accum_out=st[:, 60 + 2 * par:61 + 2 * par]),
                      reads=[xk], writes=["junkb", ("ss", par)])
                rs_of(st[:, 60 + 2 * par:61 + 2 * par], st[:, 61 + 2 * par:62 + 2 * par], 1024.0, [("ss", par)], ("rstd", par))
                sc.op("dve", lambda e, x_=x_: e.tensor_tensor(out=xg[:, :], in0=x_[:, :], in1=gmix[:, :], op=ALU.mult),
                      reads=[xk, "gmix"], writes=["xg"])
                ptb = transposes(lambda k: xg[:, k * 128:(k + 1) * 128], 8, 128, 0, ["xg"])
                sc.op("act", lambda e, ptb=ptb: e.activation(out=xT[:, :], in_=ptb[:, 0:1024], func=AF.Copy),
                      reads=[("ps", 0)], writes=["xT"])
                ncol = (IN_COLS + 511) // 512
                for ct in range(ncol):
                    c0 = ct * 512
                    w = min(512, IN_COLS - c0)
                    pb = 1 + (ct % 3)
                    for kc in range(8):
                        sc.op("pe", lambda e, pb=pb, kc=kc, c0=c0, w=w: e.matmul(
                            ps[pb][:, 0:w], xT[:, kc * 128:(kc + 1) * 128], win[:, kc, c0:c0 + w], start=(kc == 0), stop=(kc == 7)),
                            reads=["xT", ("win", kc)], writes=[("ps", pb)])
                    if ct % 2 == 0:
                        sc.op("act", lambda e, pb=pb, c0=c0, w=w: e.activation(out=z_[:, c0:c0 + w], in_=ps[pb][:, 0:w], func=AF.Copy,
                                                                               scale=st[:, 61 + 2 * par:62 + 2 * par]),
                              reads=[("ps", pb), ("rstd", par)], writes=[("z", par, ct)])
                    else:
                        sc.op("dve", lambda e, pb=pb, c0=c0, w=w: e.tensor_scalar(out=z_[:, c0:c0 + w], in0=ps[pb][:, 0:w], scalar1=st[:, 61 + 2 * par:62 + 2 * par],
                                                                                 scalar2=None, op0=ALU.mult),
                              reads=[("ps", pb), ("rstd", par)], writes=[("z", par, ct)])

        def back(ti):
                r0 = ti * 128
                par = ti % 2
                z_ = zz[par]
                sc.dma("sp", cs[:, 0:16], d_cos[r0:r0 + 128, :], "cs", writes=["cos"])
                sc.dma("sp", cs[:, 16:32], d_sin[r0:r0 + 128, :], "cs", writes=["sin"])
                zr = lambda a, b: [("z", par, c) for c in range(a // 512, (b - 1) // 512 + 1)]
                sc.op("act", lambda e: e.activation(out=tmpf[:, 0:1024], in_=z_[:, 0:1024], func=AF.Square),
                      reads=zr(0, 1024), writes=["tmpf"])
                sc.op("dve", lambda e: e.tensor_reduce(out=st[:, 8:24], in_=tmpf[:, 0:1024].rearrange("p (h d) -> p h d", d=64),
                                                       axis=AX.X, op=ALU.add), reads=["tmpf"], writes=["ss16"])
                rs_of(st[:, 8:24], st[:, 24:40], 64.0, ["ss16"], "rs16")
                sc.op("dve", lambda e: e.tensor_tensor(out=tmpf[:, 0:1024].rearrange("p (h d) -> p h d", d=64),
                                                       in0=z_[:, 0:1024].rearrange("p (h d) -> p h d", d=64),
                                                       in1=st[:, 24:40].unsqueeze(2).to_broadcast([128, 16, 64]), op=ALU.mult),
                      reads=zr(0, 1024) + ["rs16", "tmpf"], writes=["tmpf"])
                sc.op("dve", lambda e: e.tensor_tensor(out=qk_b[:, :], in0=tmpf[:, 0:1024], in1=gqk[:, :], op=ALU.mult),
                      reads=["tmpf", "gqk"], writes=["qk_b"])
                for half in range(2):
                    ptb = transposes(lambda k, half=half: qk_b[:, (half * 8 + k) * 64:(half * 8 + k + 1) * 64], 8, 64, 4 + half, ["qk_b"])
                    sc.op("act", lambda e, ptb=ptb, half=half: e.activation(
                        out=qkT[:, half * 8:(half + 1) * 8, :], in_=ptb[0:64, 0:1024].rearrange("p (h t) -> p h t", h=8), func=AF.Copy),
                        reads=[("ps", 4 + half)], writes=[("qkT", half)])
                    if half == 0:
                        dst = o_qaT[:, r0:r0 + 128].rearrange("(h d) t -> d h t", d=64)
                    else:
                        dst = o_kaT[512 * (r0 // 512):512 * (r0 // 512) + 512, r0 % 512:r0 % 512 + 128].rearrange("(h d) t -> d h t", d=64)
                    sc.dma("sp", dst, qkT[:, half * 8:(half + 1) * 8, :], ("oqk", half), reads=[("qkT", half)],
                           writes=([("dr", "kaT", r0 // 512)] if half == 1 else []))
                sc.op("act", lambda e: e.activation(out=va_sb[:, :, 0:64], in_=z_[:, 1024:1536].rearrange("p (h d) -> p h d", d=64), func=AF.Copy),
                      reads=zr(1024, 1536) + ["va"], writes=["va"])
                sc.dma("sp", o_va[r0:r0 + 128, :], va_sb[:].rearrange("p h d -> p (h d)"), "ova", reads=["va"], writes=[("dr", "va", r0 // 512)])
                sc.op("act", lambda e: e.activation(out=qi_b[:, :], in_=z_[:, 1536:2048], func=AF.Copy), reads=zr(1536, 2048), writes=["qi_b"])
                ptb = transposes(lambda k: qi_b[:, k * 64:(k + 1) * 64], 8, 64, 6, ["qi_b"])
                sc.op("act", lambda e, ptb=ptb: e.activation(out=qiT_sb[:, :].rearrange("p (t h) -> p h t", h=8),
                                                            in_=ptb[0:64, 0:1024].rearrange("p (h t) -> p h t", h=8), func=AF.Copy),
                      reads=[("ps", 6)], writes=["qiT"])
                sc.dma("sp", o_qiT[:, r0 * 8:(r0 + 128) * 8], qiT_sb[:, :], "oqi", reads=["qiT"])
                sc.op("act", lambda e: e.activation(out=ki_b[:, :], in_=z_[:, 2048:2112], func=AF.Copy), reads=zr(2048, 2112), writes=["ki_b"])
                ptb = transposes(lambda k: ki_b[:, :], 1, 64, 7, ["ki_b"])
                sc.op("act", lambda e, ptb=ptb: e.activation(out=kiT_sb[:, :], in_=ptb[0:64, 0:128], func=AF.Copy),
                      reads=[("ps", 7)], writes=["kiT"])
                sc.dma("sp", o_kiT[64 * (r0 // 512):64 * (r0 // 512) + 64, r0 % 512:r0 % 512 + 128], kiT_sb[:, :], "oki", reads=["kiT"],
                       writes=[("dr", "kiT", r0 // 512)])
                sc.op("dve", lambda e: e.tensor_scalar(out=w2_sb[:, :], in0=z_[:, 2112:2120], scalar1=(8 ** -0.5) * (64 ** -0.5), scalar2=None,
                                                       op0=ALU.mult), reads=zr(2112, 2120), writes=["w2"])
                sc.dma("sp", o_w2[r0:r0 + 128, :], w2_sb[:, :], "ow2", reads=["w2"])
                sc.op("act", lambda e: e.activation(out=junkb[:, 0:384], in_=z_[:, 2120:2504], func=AF.Square, accum_out=st[:, 2:3]),
                      reads=zr(2120, 2504) + ["junkb"], writes=["junkb", "sscq"])
                rs_of(st[:, 2:3], st[:, 3:4], 384.0, ["sscq"], "rscq")
                sc.op("dve", lambda e: e.scalar_tensor_tensor(out=cq_b[:, :], in0=z_[:, 2120:2504], scalar=st[:, 3:4], in1=gcq[:, :],
                                                              op0=ALU.mult, op1=ALU.mult), reads=zr(2120, 2504) + ["rscq", "gcq"], writes=["cq_b"])
                ptb = transposes(lambda k: cq_b[:, k * 128:(k + 1) * 128], 3, 128, 0, ["cq_b"])
                sc.op("act", lambda e, ptb=ptb: e.activation(out=cqT[:, :, :], in_=ptb[:, 0:384].rearrange("p (k t) -> p k t", k=3), func=AF.Copy),
                      reads=[("ps", 0)], writes=["cqT"])
                for (c0, w, pb) in ((0, 512, 1), (512, 256, 2)):
                    for kc in range(3):
                        sc.op("pe", lambda e, pb=pb, kc=kc, c0=c0, w=w: e.matmul(ps[pb][:, 0:w], cqT[:, kc, :], wuq[:, kc, c0:c0 + w],
                                                                                 start=(kc == 0), stop=(kc == 2)),
                              reads=["cqT", "wuq"], writes=[("ps", pb)])
                    sc.op("act", lambda e, pb=pb, c0=c0, w=w: e.activation(out=qb_f[:].rearrange("p h d -> p (h d)")[:, c0:c0 + w],
                                                                           in_=ps[pb][:, 0:w], func=AF.Copy),
                          reads=[("ps", pb)], writes=[("qb_f", c0)])
                sc.op("act", lambda e: e.activation(out=junkb[:, 0:256], in_=z_[:, 2504:2760], func=AF.Square, accum_out=st[:, 4:5]),
                      reads=zr(2504, 2760) + ["junkb"], writes=["junkb", "ssckv"])
                rs_of(st[:, 4:5], st[:, 5:6], 256.0, ["ssckv"], "rsckv")
                sc.op("dve", lambda e: e.scalar_tensor_tensor(out=ckv_b[:, :], in0=z_[:, 2504:2760], scalar=st[:, 5:6], in1=gckv[:, :],
                                                              op0=ALU.mult, op1=ALU.mult), reads=zr(2504, 2760) + ["rsckv", "gckv"], writes=["ckv_b"])
                ptb = transposes(lambda k: ckv_b[:, k * 128:(k + 1) * 128], 2, 128, 3, ["ckv_b"])
                sc.op("act", lambda e, ptb=ptb: e.activation(out=ckvT[:, :, :], in_=ptb[:, 0:256].rearrange("p (k t) -> p k t", k=2), func=AF.Copy),
                      reads=[("ps", 3)], writes=["ckvT"])
                for (c0, pb) in ((0, 4), (512, 5)):
                    for kc in range(2):
                        sc.op("pe", lambda e, pb=pb, kc=kc, c0=c0: e.matmul(ps[pb][:, 0:512], ckvT[:, kc, :], wukv[:, kc, c0:c0 + 512],
                                                                            start=(kc == 0), stop=(kc == 1)),
                              reads=["ckvT", "wukv"], writes=[("ps", pb)])
                    sc.op("act", lambda e, pb=pb, c0=c0: e.activation(out=kv_f[:].rearrange("p h d -> p (h d)")[:, c0:c0 + 512],
                                                                      in_=ps[pb][:, 0:512], func=AF.Copy),
                          reads=[("ps", pb)], writes=[("kv_f", c0)])
                cosb = cs[:, 0:16].unsqueeze(1).to_broadcast([128, 8, 16])
                sinb = cs[:, 16:32].unsqueeze(1).to_broadcast([128, 8, 16])
                qbr = [("qb_f", 0), ("qb_f", 512)]
                x1 = qb_f[:, :, 64:80]
                x2 = qb_f[:, :, 80:96]
                for k_, (a_, b_) in enumerate(((x1, cosb), (x2, sinb), (x1, sinb), (x2, cosb))):
                    sc.op("dve", lambda e, k_=k_, a_=a_, b_=b_: e.tensor_tensor(out=rt[:, :, k_ * 16:(k_ + 1) * 16], in0=a_, in1=b_, op=ALU.mult),
                          reads=qbr + ["cos", "sin"], writes=[("rt", k_)])
                sc.op("dve", lambda e: e.tensor_tensor(out=qb_f[:, :, 64:80], in0=rt[:, :, 0:16], in1=rt[:, :, 16:32], op=ALU.subtract),
                      reads=[("rt", 0), ("rt", 1)] + qbr, writes=qbr)
                sc.op("dve", lambda e: e.tensor_tensor(out=qb_f[:, :, 80:96], in0=rt[:, :, 32:48], in1=rt[:, :, 48:64], op=ALU.add),
                      reads=[("rt", 2), ("rt", 3)] + qbr, writes=qbr)
                k1 = z_[:, 2760:2776]
                k2 = z_[:, 2776:2792]
                for k_, (a_, b_) in enumerate(((k1, cs[:, 0:16]), (k2, cs[:, 16:32]), (k1, cs[:, 16:32]), (k2, cs[:, 0:16]))):
                    sc.op("dve", lambda e, k_=k_, a_=a_, b_=b_: e.tensor_tensor(out=rt[:, 0, k_ * 16:(k_ + 1) * 16], in0=a_, in1=b_, op=ALU.mult),
                          reads=zr(2760, 2792) + ["cos", "sin"] + [("rt", q) for q in range(4)], writes=[("rt", k_)])
                sc.op("dve", lambda e: e.tensor_tensor(out=krr[:, 0:16], in0=rt[:, 0, 0:16], in1=rt[:, 0, 16:32], op=ALU.subtract),
                      reads=[("rt", 0), ("rt", 1)], writes=["krr"])
                sc.op("dve", lambda e: e.tensor_tensor(out=krr[:, 16:32], in0=rt[:, 0, 32:48], in1=rt[:, 0, 48:64], op=ALU.add),
                      reads=[("rt", 2), ("rt", 3), "krr"], writes=["krr"])
                kvr = [("kv_f", 0), ("kv_f", 512)]
                sc.op("act", lambda e: e.activation(out=kb_f[:, :, 0:64], in_=kv_f[:, :, 0:64], func=AF.Copy), reads=kvr, writes=["kb_f"])
                sc.op("dve", lambda e: e.tensor_copy(out=kb_f[:, :, 64:96], in_=krr[:, :].unsqueeze(1).to_broadcast([128, 8, 32])),
                      reads=["krr", "kb_f"], writes=["kb_f"])
                sc.op("act", lambda e: e.activation(out=vb_sb[:, :, 0:64], in_=kv_f[:, :, 64:128], func=AF.Copy), reads=kvr + ["vb"], writes=["vb"])
                sc.dma("sp", o_vb[r0:r0 + 128, :], vb_sb[:].rearrange("p h d -> p (h d)"), "ovb", reads=["vb"], writes=[("dr", "vb", r0 // 512)])
                for nm, src, rdk, gt, dstb, dstT, outd, pbk in (("q", qb_f, qbr, gbq, qb_b, qbT_sb, o_qbT, 6), ("k", kb_f, ["kb_f"], gbk, kb_b, kbT_sb, o_kbT, 7)):
                    flat = src[:].rearrange("p h d -> p (h d)")
                    sc.op("act", lambda e, flat=flat: e.activation(out=tmpf[:, 0:768], in_=flat, func=AF.Square), reads=rdk + ["tmpf"], writes=["tmpf"])
                    sc.op("dve", lambda e: e.tensor_reduce(out=st[:, 40:48], in_=tmpf[:, 0:768].rearrange("p (h d) -> p h d", d=96),
                                                           axis=AX.X, op=ALU.add), reads=["tmpf"], writes=["ss8"])
                    rs_of(st[:, 40:48], st[:, 48:56], 96.0, ["ss8"], "rs8")
                    sc.op("dve", lambda e, src=src: e.tensor_tensor(out=tmpf[:, 0:768].rearrange("p (h d) -> p h d", d=96), in0=src[:, :, :],
                                                                   in1=st[:, 48:56].unsqueeze(2).to_broadcast([128, 8, 96]), op=ALU.mult),
                          reads=rdk + ["rs8", "tmpf"], writes=["tmpf"])
                    sc.op("dve", lambda e, dstb=dstb, gt=gt: e.tensor_tensor(out=dstb[:].rearrange("p h d -> p (h d)"), in0=tmpf[:, 0:768], in1=gt[:, :],
                                                                            op=ALU.mult), reads=["tmpf", "gb" + nm], writes=["b_" + nm])
                    ptb = transposes(lambda k, dstb=dstb: dstb[:, k, :], 8, 96, pbk, ["b_" + nm])
                    sc.op("act", lambda e, ptb=ptb, dstT=dstT: e.activation(out=dstT[:, :, :], in_=ptb[0:96, 0:1024].rearrange("p (h t) -> p h t", h=8),
                                                                           func=AF.Copy), reads=[("ps", pbk)], writes=["T_" + nm])
                    if nm == "q":
                        dstd = outd[:, r0:r0 + 128]
                    else:
                        dstd = outd[768 * (r0 // 512):768 * (r0 // 512) + 768, r0 % 512:r0 % 512 + 128]
                    sc.dma("sp", dstd.rearrange("(h d) t -> d h t", d=96), dstT[:, :, :], "oT" + nm, reads=["T_" + nm],
                           writes=([("dr", "kbT", r0 // 512)] if nm == "k" else []))
                for gh in range(2):
                    sc.op("dve", lambda e, gh=gh: e.tensor_tensor(out=tmpf[:, :], in0=z_[:, 2792 + 1024 * gh:2792 + 1024 * (gh + 1)],
                                                                 in1=bgate[:, 1024 * gh:1024 * (gh + 1)], op=ALU.add),
                          reads=zr(2792, 4840) + ["bgate", "tmpf"], writes=["tmpf"])
                    sc.op("act", lambda e: e.activation(out=gate_sb[:, :], in_=tmpf[:, :], func=AF.Sigmoid), reads=["tmpf"], writes=["gate"])
                    sc.dma("sp", o_gate[r0:r0 + 128, 1024 * gh:1024 * (gh + 1)], gate_sb[:, :], "ogate", reads=["gate"])
                if env is not None and getattr(env, "hook_sb", None) and ti % 4 == 3:
                    env.hook_sb(ti // 4)


        front(0)
        for ti in range(ntile):
            if ti + 1 < ntile:
                front(ti + 1)
            back(ti)
        sc.emit(skip_cc=bool(env is not None and getattr(env, "skip_cc", False)))
        print("A program: ops", sc.nops, "waits", sc.nwaits)
    return nc


def rope_tables(pos):
    inv = (np.float32(10000.0) ** (-np.arange(16, dtype=np.float32) / np.float32(16))).astype(np.float32)
    ang = pos.astype(np.float32)[:, None] * inv[None, :]
    return np.cos(ang).astype(np.float32), np.sin(ang).astype(np.float32)


def rep(v, n=128):
    return np.ascontiguousarray(np.broadcast_to(np.asarray(v, np.float32)[None, :], (n, v.shape[0])))


def shard_A(x, P, l, NI):
    maps = []
    ident = np.eye(128, dtype=np.float32).astype(NPBF)
    for c in range(NCORE):
        b, j = c // 4, c % 4
        tok = core_tokens(NI, j)
        cos, sin = rope_tables(tok)
        m = dict(x=np.ascontiguousarray(x[b][tok]), w_in=P["w_in"][l], w_uq=P["b_w_uq"][l], w_ukv=P["b_w_ukv"][l],
                 gmix=rep(P["norm_mix"][l]),
                 gqk=rep(np.concatenate([np.tile(P["a_q_norm"][l], 8), np.tile(P["a_k_norm"][l], 8)])),
                 gcq=rep(P["b_cq_norm"][l]), gckv=rep(P["b_ckv_norm"][l]),
                 gbq=rep(np.tile(P["b_q_norm"][l], 8)), gbk=rep(np.tile(P["b_k_norm"][l], 8)),
                 bgate=rep(P["b_gate"][l]), cos=cos, sin=sin, ident=ident)
        maps.append(m)
    return maps


def gather_A(results, S, NI):
    B = 2
    G = dict(qiT=np.zeros((B, 64, S * 8), NPBF), w2=np.zeros((B, S, 8), np.float32), qaT=np.zeros((B, 8, 64, S), NPBF),
             qbT=np.zeros((B, 8, 96, S), NPBF), kiT=np.zeros((B, 64, S), NPBF), kaT=np.zeros((B, 8, 64, S), NPBF),
             va=np.zeros((B, S, 520), NPBF), kbT=np.zeros((B, 8, 96, S), NPBF), vb=np.zeros((B, S, 520), NPBF),
             gate=np.zeros((B, S, 2048), np.float32))
    for c in range(NCORE):
        b, j = c // 4, c % 4
        tok = core_tokens(NI, j)
        r = results[c]
        G["qiT"][b].reshape(64, S, 8)[:, tok, :] = r["qiT"].reshape(64, len(tok), 8)
        G["w2"][b][tok] = r["w2"]
        for k in ("qaT", "qbT"):
            G[k][b][:, :, tok] = r[k].reshape(8, -1, len(tok))
        for k, dh in (("kaT", 64), ("kbT", 96)):
            G[k][b][:, :, tok] = r[k].reshape(NI, 8, dh, SB).transpose(1, 2, 0, 3).reshape(8, dh, len(tok))
        G["kiT"][b][:, tok] = r["kiT"].reshape(NI, 64, SB).transpose(1, 0, 2).reshape(64, len(tok))
        for k in ("va", "vb", "gate"):
            G[k][b][tok] = r[k]
    return G


def build_C1(NT, env=None):
    nc = env.nc if env else bass.Bass("TRN2", target_bir_lowering=False)
    es = ExitStack()
    with es:
        cx = Ctx(nc, es, env)
        sc = env.sc if env else Sched(nc, es)
        d_x = cx.din("x", [NT, D], F32)
        d_yaT = cx.din("yaT", [512, NT], BF16)
        d_ybT = cx.din("ybT", [512, NT], BF16)
        d_gate = cx.din("gate", [NT, 2048], F32)
        d_wpa = cx.din("w_pa", [512, D], F32)
        d_wpb = cx.din("w_pb", [512, D], F32)
        d_wout = cx.din("w_out", [D, D], F32)
        d_ident = cx.din("ident", [128, 128], BF16)
        o_x = cx.dout("xmid", [NT, D], F32)
        wpa = cx.sb([64, 8, D], BF16)
        wpb = cx.sb([64, 8, D], BF16)
        wout = cx.sb([128, 8, D], BF16)
        ident = cx.sb([128, 128], BF16)
        sc.dma(wq(d_wpa), wpa[:], d_wpa.rearrange("(h d) c -> d h c", d=64), "wpa", writes=["wpa"])
        sc.dma(wq(d_wpb), wpb[:], d_wpb.rearrange("(h d) c -> d h c", d=64), "wpb", writes=["wpb"])
        sc.dma(wq(d_wout), wout[:], d_wout.rearrange("(k p) c -> p k c", p=128), "wout", writes=["wout"])
        sc.dma("sp", ident[:], d_ident, "ident", writes=["ident"])
        yT = [[cx.sb([64, 8, 128], BF16) for _ in range(2)] for _ in range(2)]
        gt = [cx.sb([128, 2048], F32) for _ in range(2)]
        xt = [cx.sb([128, D], F32) for _ in range(2)]
        t1 = cx.sb([128, D], F32)
        t2 = cx.sb([128, D], F32)
        mb = cx.sb([128, D], BF16)
        mT = cx.sb([128, D], BF16)
        xo = [cx.sb([128, D], F32) for _ in range(2)]
        ps = [cx.ps([128, 512], F32) for _ in range(8)]
        for ti in range(NT // 128):
            r0 = ti * 128
            s = ti % 2
            sc.dma("sp", yT[0][s][:], d_yaT[:, r0:r0 + 128].rearrange("(h d) t -> d h t", d=64), ("ld", s), writes=[("ya", s)])
            sc.dma("sp", yT[1][s][:], d_ybT[:, r0:r0 + 128].rearrange("(h d) t -> d h t", d=64), ("ld", s), writes=[("yb", s)])
            sc.dma("sp", gt[s][:], d_gate[r0:r0 + 128, :], ("ld", s), writes=[("g", s)])
            sc.dma("sp", xt[s][:], d_x[r0:r0 + 128, :], ("ld", s), writes=[("x", s)])
            for br, (w_, wn, yk) in enumerate(((wpa, "wpa", "ya"), (wpb, "wpb", "yb"))):
                for ct in range(2):
                    pb = br * 2 + ct
                    for h in range(8):
                        sc.op("pe", lambda e, pb=pb, h=h, w_=w_, br=br, ct=ct, s=s: e.matmul(
                            ps[pb][:, :], yT[br][s][:, h, :], w_[:, h, ct * 512:(ct + 1) * 512], start=(h == 0), stop=(h == 7)),
                            reads=[(yk, s), wn], writes=[("ps", pb)])
            for ct in range(2):
                cs_ = slice(ct * 512, (ct + 1) * 512)
                sc.op("dve", lambda e, ct=ct, cs_=cs_, s=s: e.tensor_tensor(out=t1[:, cs_], in0=ps[ct][:, :], in1=gt[s][:, cs_], op=ALU.mult),
                      reads=[("ps", ct), ("g", s)], writes=[("t1", ct)])
                sc.op("dve", lambda e, ct=ct, cs_=cs_, s=s: e.tensor_tensor(out=t2[:, cs_], in0=ps[2 + ct][:, :],
                                                                           in1=gt[s][:, 1024 + ct * 512:1024 + (ct + 1) * 512], op=ALU.mult),
                      reads=[("ps", 2 + ct), ("g", s)], writes=[("t2", ct)])
                sc.op("dve", lambda e, cs_=cs_: e.tensor_tensor(out=mb[:, cs_], in0=t1[:, cs_], in1=t2[:, cs_], op=ALU.add),
                      reads=[("t1", ct), ("t2", ct)], writes=[("mb", ct)])
            ptb = ps[4][:, :].bitcast(BF16)
            for k in range(8):
                sc.op("pe", lambda e, k=k, ptb=ptb: e.transpose(ptb[:, k * 128:(k + 1) * 128], mb[:, k * 128:(k + 1) * 128], ident[:, :]),
                      reads=[("mb", k // 4), "ident"], writes=[("ps", 4)])
            sc.op("act", lambda e, ptb=ptb: e.activation(out=mT[:, :], in_=ptb[:, 0:1024], func=AF.Copy), reads=[("ps", 4)], writes=["mT"])
            for ct in range(2):
                pb = 5 + ct
                for kc in range(8):
                    sc.op("pe", lambda e, pb=pb, kc=kc, ct=ct: e.matmul(ps[pb][:, :], mT[:, kc * 128:(kc + 1) * 128],
                                                                       wout[:, kc, ct * 512:(ct + 1) * 512], start=(kc == 0), stop=(kc == 7)),
                          reads=["mT", "wout"], writes=[("ps", pb)])
                sc.op("dve", lambda e, pb=pb, ct=ct, s=s: e.tensor_tensor(out=xo[s][:, ct * 512:(ct + 1) * 512], in0=ps[pb][:, :],
                                                                         in1=xt[s][:, ct * 512:(ct + 1) * 512], op=ALU.add),
                      reads=[("ps", pb), ("x", s)], writes=[("xo", s, ct)])
            sc.dma("sp", o_x[r0:r0 + 128, :], xo[s][:, :], ("st", s), reads=[("xo", s, 0), ("xo", s, 1)])
        sc.emit()
        print("C1 program: ops", sc.nops, "waits", sc.nwaits)
    return nc


def build_C2(NI, env=None):
    NT = NI * SB
    NCT = 2 * D_FF // 128
    NV = D_FF // 128
    nc = env.nc if env else bass.Bass("TRN2", target_bir_lowering=False)
    es = ExitStack()
    with es:
        cx = Ctx(nc, es, env)
        sc = env.sc if env else Sched(nc, es)
        d_x = cx.din("xmid", [NT, D], F32)
        d_halo = cx.din("halo", [NI, 2, D], F32)
        d_g = cx.din("gffn", [128, D], F32)
        d_wup = cx.din("w_up", [D, 2 * D_FF], F32)
        d_wdn = cx.din("w_down", [D_FF, D], F32)
        d_cw = cx.din("convw", [128, NCT, 3], F32)
        d_cb = cx.din("convb", [128, NCT], F32)
        d_ident = cx.din("ident", [128, 128], BF16)
        o_x = cx.dout("xout", [NT, D], F32)
        gffn = cx.sb([128, D], F32)
        wdn = cx.sb([128, NV, D], BF16)
        cw = cx.sb([128, NCT, 3], F32)
        cb = cx.sb([128, NCT], F32)
        ident = cx.sb([128, 128], BF16)
        sc.dma("sp", gffn[:], d_g, "gffn", writes=["gffn"])
        sc.dma("sp", cw[:], d_cw, "cw", writes=["cw"])
        sc.dma("sp", cb[:], d_cb, "cb", writes=["cb"])
        sc.dma("sp", ident[:], d_ident, "ident", writes=["ident"])
        for v in range(NV):
            sc.dma(wq(d_wdn), wdn[:, v, :], d_wdn[v * 128:(v + 1) * 128, :], ("wdn", v % 2), writes=[("wdn", v)])
        xt = [cx.sb([128, D], F32) for _ in range(4)]
        xh = cx.sb([2, D], F32)
        junkb = cx.sb([128, D], BF16)
        hb = cx.sb([128, D], BF16)
        h2T = cx.sb([128, 8, 514], BF16)
        wup = [[cx.sb([128, 8, 128], BF16) for _ in range(2)] for _ in range(2)]
        u2 = [[cx.sb([128, 514], F32) for _ in range(2)] for _ in range(2)]
        acc2 = [[cx.sb([128, 512], F32) for _ in range(2)] for _ in range(2)]
        sg2 = [cx.sb([128, 512], F32) for _ in range(2)]
        aT = cx.sb([128, NV, 512], BF16)
        xo = [cx.sb([128, D], F32) for _ in range(2)]
        st = cx.sb([128, 8], F32)
        ps = [cx.ps([128, 512], F32) for _ in range(8)]
        wslot = 0
        for i in range(NI):
            tok0 = i * SB
            sc.dma("sp", xh[:, :], d_halo[i, :, :], "xh", writes=["xh"])
            tiles = [("h", xh, 2, 0)] + [(k, xt[k], 128, 2 + 128 * k) for k in range(4)]
            for (tk, xx, np_, c0) in tiles:
                key = ("xt", tk)
                if tk != "h":
                    sc.dma("sp", xx[:, :], d_x[tok0 + tk * 128:tok0 + (tk + 1) * 128, :], key, writes=[key])
                else:
                    key = "xh"
                sc.op("act", lambda e, xx=xx, np_=np_: e.activation(out=junkb[0:np_, :], in_=xx[0:np_, :], func=AF.Square, accum_out=st[0:np_, 0:1]),
                      reads=[key, "junkb"], writes=["junkb", "ss"])
                sc.op("dve", lambda e, np_=np_: e.tensor_scalar(out=st[0:np_, 1:2], in0=st[0:np_, 0:1], scalar1=1.0 / D, scalar2=EPS,
                                                               op0=ALU.mult, op1=ALU.add), reads=["ss"], writes=["rs"])
                sc.op("act", lambda e, np_=np_: e.activation(out=st[0:np_, 1:2], in_=st[0:np_, 1:2], func=AF.Sqrt), reads=["rs"], writes=["rs"])
                sc.op("dve", lambda e, np_=np_: e.reciprocal(out=st[0:np_, 1:2], in_=st[0:np_, 1:2]), reads=["rs"], writes=["rs"])
                sc.op("dve", lambda e, xx=xx, np_=np_: e.scalar_tensor_tensor(out=hb[0:np_, :], in0=xx[0:np_, :], scalar=st[0:np_, 1:2],
                                                                             in1=gffn[0:np_, :], op0=ALU.mult, op1=ALU.mult),
                      reads=[key, "rs", "gffn"], writes=["hb"])
                ptb = ps[7][:, :].bitcast(BF16)
                for k in range(8):
                    sc.op("pe", lambda e, k=k, ptb=ptb, np_=np_: e.transpose(ptb[:, k * 128:k * 128 + np_], hb[0:np_, k * 128:(k + 1) * 128],
                                                                           ident[0:np_, 0:np_]),
                          reads=["hb", "ident"], writes=[("ps", 7)])
                sc.op("act", lambda e, ptb=ptb, np_=np_, c0=c0: e.activation(
                    out=h2T[:, :, c0:c0 + np_], in_=ptb[:, 0:1024].rearrange("p (k t) -> p k t", k=8)[:, :, 0:np_], func=AF.Copy),
                    reads=[("ps", 7)], writes=[("h2T", tk)])
            h2r = [("h2T", tk) for tk in ("h", 0, 1, 2, 3)]
            for v in range(NV):
                ws = wslot % 2
                wslot += 1
                par = v % 2
                u = [(par, 0), (par, 1)]
                for gi, ct in enumerate((v, NV + v)):
                    sc.dma(wq(d_wup), wup[ws][gi][:], d_wup[:, ct * 128:(ct + 1) * 128].rearrange("(k p) c -> p k c", p=128),
                           ("wup", ws, gi), writes=[("wup", ws, gi)])
                for gi, ct in enumerate((v, NV + v)):
                    pm = 3 * par + gi
                    ph = 3 * par + 2
                    hc = 2 * gi
                    ug = u2[par][gi]
                    ag = acc2[par][gi]
                    for kc in range(8):
                        sc.op("pe", lambda e, pm=pm, kc=kc, ws=ws, gi=gi: e.matmul(ps[pm][:, :], wup[ws][gi][:, kc, :], h2T[:, kc, 2:514],
                                                                                  start=(kc == 0), stop=(kc == 7)),
                              reads=[("wup", ws, gi)] + h2r, writes=[("ps", pm)])
                    for kc in range(8):
                        sc.op("pe", lambda e, ph=ph, kc=kc, ws=ws, gi=gi, hc=hc: e.matmul(ps[ph][:, hc:hc + 2], wup[ws][gi][:, kc, :], h2T[:, kc, 0:2],
                                                                                  start=(kc == 0), stop=(kc == 7)),
                              reads=[("wup", ws, gi)] + h2r, writes=[("ps", ph)])
                    uk = ("u", par, gi)
                    ak = ("acc", par, gi)
                    sc.op("act", lambda e, pm=pm, ug=ug: e.activation(out=ug[:, 2:514], in_=ps[pm][:, :], func=AF.Copy),
                          reads=[("ps", pm)], writes=[uk])
                    sc.op("act", lambda e, ph=ph, ug=ug, hc=hc: e.activation(out=ug[:, 0:2], in_=ps[ph][:, hc:hc + 2], func=AF.Copy),
                          reads=[("ps", ph), uk], writes=[uk])
                    sc.op("dve", lambda e, ug=ug, ag=ag, ct=ct: e.tensor_scalar(out=ag[:, :], in0=ug[:, 2:514], scalar1=cw[:, ct, 2:3],
                                                                               scalar2=cb[:, ct:ct + 1], op0=ALU.mult, op1=ALU.add),
                          reads=[uk, "cw", "cb"], writes=[ak])
                    sc.op("dve", lambda e, ug=ug, ag=ag, ct=ct: e.scalar_tensor_tensor(out=ag[:, :], in0=ug[:, 1:513], scalar=cw[:, ct, 1:2],
                                                                                      in1=ag[:, :], op0=ALU.mult, op1=ALU.add),
                          reads=[uk, "cw", ak], writes=[ak])
                    sc.op("dve", lambda e, ug=ug, ag=ag, ct=ct: e.scalar_tensor_tensor(out=ag[:, :], in0=ug[:, 0:512], scalar=cw[:, ct, 0:1],
                                                                                      in1=ag[:, :], op0=ALU.mult, op1=ALU.add),
                          reads=[uk, "cw", ak], writes=[ak])
                sc.op("act", lambda e, par=par: e.activation(out=sg2[par][:, :], in_=acc2[par][1][:, :], func=AF.Silu),
                      reads=[("acc", par, 1)], writes=[("sg", par)])
                sc.op("pool", lambda e, v=v, par=par: e.tensor_tensor(out=aT[:, v, :], in0=sg2[par][:, :], in1=acc2[par][0][:, :], op=ALU.mult),
                      reads=[("sg", par), ("acc", par, 0)], writes=[("aT", v)])
            for tk in range(4):
                s = tk % 2
                for ct in range(2):
                    pb = 6 + ct
                    for v in range(NV):
                        sc.op("pe", lambda e, pb=pb, v=v, tk=tk, ct=ct: e.matmul(ps[pb][:, :], aT[:, v, tk * 128:(tk + 1) * 128],
                                                                                wdn[:, v, ct * 512:(ct + 1) * 512], start=(v == 0), stop=(v == NV - 1)),
                              reads=[("aT", v), ("wdn", v)], writes=[("ps", pb)])
                    sc.op("dve", lambda e, pb=pb, ct=ct, tk=tk, s=s: e.tensor_tensor(out=xo[s][:, ct * 512:(ct + 1) * 512], in0=ps[pb][:, :],
                                                                                   in1=xt[tk][:, ct * 512:(ct + 1) * 512], op=ALU.add),
                          reads=[("ps", pb), ("xt", tk)], writes=[("xo", s, ct)])
                sc.dma("sp", o_x[tok0 + tk * 128:tok0 + (tk + 1) * 128, :], xo[s][:, :], ("st", s), reads=[("xo", s, 0), ("xo", s, 1)])
        sc.emit()
        print("C2 program: ops", sc.nops, "waits", sc.nwaits)
    return nc


def maps_C1(xloc, rB, rA, P, l):
    ident = np.eye(128, dtype=np.float32).astype(NPBF)
    return [dict(x=xloc[c], yaT=rB[c]["yaT"], ybT=rB[c]["ybT"], gate=rA[c]["gate"],
                 w_pa=P["w_proj_a"][l], w_pb=P["w_proj_b"][l], w_out=P["w_out"][l], ident=ident) for c in range(NCORE)]


def shard_C1(x, yaT, ybT, gate, P, l, NI):
    xloc, rB, rA = [], [], []
    for c in range(NCORE):
        b, j = c // 4, c % 4
        tok = core_tokens(NI, j)
        xloc.append(np.ascontiguousarray(x[b][tok]))
        rB.append(dict(yaT=np.ascontiguousarray(yaT[b][:, :, tok].reshape(512, -1)), ybT=np.ascontiguousarray(ybT[b][:, :, tok].reshape(512, -1))))
        rA.append(dict(gate=np.ascontiguousarray(gate[b][tok])))
    return maps_C1(xloc, rB, rA, P, l)


def shard_C2(xmid, P, l, NI):
    ident = np.eye(128, dtype=np.float32).astype(NPBF)
    NCT = 2 * D_FF // 128
    cw = np.ascontiguousarray(np.asarray(P["conv_w"][l], np.float32).reshape(3, NCT, 128).transpose(2, 1, 0))
    cb = np.ascontiguousarray(np.asarray(P["conv_b"][l], np.float32).reshape(NCT, 128).T)
    maps = []
    for c in range(NCORE):
        b, j = c // 4, c % 4
        tok = core_tokens(NI, j)
        halo = np.zeros((NI, 2, D), np.float32)
        for i in range(NI):
            g0 = SB * (4 * i + j)
            if g0 >= 2:
                halo[i] = xmid[b][g0 - 2:g0]
        maps.append(dict(xmid=np.ascontiguousarray(xmid[b][tok]), halo=halo, gffn=rep(P["norm_ffn"][l]),
                         w_up=P["w_up"][l], w_down=P["w_down"][l], convw=cw, convb=cb, ident=ident))
    return maps


def gather_tok(results, key, S, NI):
    out = np.zeros((2, S, D), np.float32)
    for c in range(NCORE):
        b, j = c // 4, c % 4
        out[b][core_tokens(NI, j)] = results[c][key]
    return out


_PROG = {}


def _prog(key, fn, *args):
    if key not in _PROG:
        _PROG[key] = fn(*args)
    return _PROG[key]


def _run(nc, maps):
    return run_bass_kernel_spmd(nc, maps, core_ids=list(range(NCORE))).results


def kernel_unfused(**inputs):
    P = {k: np.asarray(v) for k, v in inputs.items()}
    x = np.asarray(P["x"], np.float32)
    S = x.shape[1]
    NI = S // (4 * SB)
    NT = NI * SB
    for l in range(2):
        rA = _run(_prog(("A", NT), build_A, NT), shard_A(x, P, l, NI))
        rB = _run(_prog(("B", S, NI), build_B, S, NI), maps_B(rA, P["rel_bias"], NI))
        xloc = [np.ascontiguousarray(x[c // 4][core_tokens(NI, c % 4)]) for c in range(NCORE)]
        xmid = gather_tok(_run(_prog(("C1", NT), build_C1, NT), maps_C1(xloc, rB, rA, P, l)), "xmid", S, NI)
        x = gather_tok(_run(_prog(("C2", NI), build_C2, NI), shard_C2(xmid, P, l, NI)), "xout", S, NI)
    return x.astype(np.float32)


GROUPS = [[0, 1, 2, 3], [4, 5, 6, 7]]
NCT_ = 2 * D_FF // 128


def build_fused(S, NI):
    NT = NI * SB
    NH2 = NI * 2
    nc = bass.Bass("TRN2", target_bir_lowering=False)
    es = ExitStack()
    with es:
        sc = Sched(nc, es)
        env = Env(nc, sc)

        def ein(name, shape, dt):
            return nc.dram_tensor(name, list(shape), dt, kind="ExternalInput").ap()

        def scr(name, shape, dt):
            return nc.dram_tensor(name, list(shape), dt).ap()

        X = dict(
            x=ein("x", [NT, D], F32), w_in=ein("w_in", [2, D, IN_COLS], F32), w_uq=ein("w_uq", [2, 384, 768], F32),
            w_ukv=ein("w_ukv", [2, 256, 1024], F32), gmix=ein("gmix", [2, 128, 1024], F32), gqk=ein("gqk", [2, 128, 1024], F32),
            gcq=ein("gcq", [2, 128, 384], F32), gckv=ein("gckv", [2, 128, 256], F32), gbq=ein("gbq", [2, 128, 768], F32),
            gbk=ein("gbk", [2, 128, 768], F32), bgate=ein("bgate", [2, 128, 2048], F32), cos=ein("cos", [NT, 16], F32),
            sin=ein("sin", [NT, 16], F32), ident=ein("ident", [128, 128], BF16),
            mg=ein("mg", [128, 8, 128], BF16), diagi=ein("diagi", [128, 4, 512], BF16), selcol=ein("selcol", [128, 8], F32),
            sel=ein("sel", [128, 8, 128], BF16), negrow=ein("negrow", [1, 4, 128], BF16), onesrow=ein("onesrow", [1, 512], BF16),
            bd=ein("bd", [128, 4, 512], BF16), e65=ein("e65", [65, 64], F32), pow2=ein("pow2", [128, NIT], F32),
            tbg=ein("tbg", [128, 8, 384], F32), far=ein("far", [128, 8], F32),
            w_pa=ein("w_pa", [2, 512, D], F32), w_pb=ein("w_pb", [2, 512, D], F32), w_out=ein("w_out", [2, D, D], F32),
            gffn=ein("gffn", [2, 128, D], F32), w_up=ein("w_up", [2, D, 2 * D_FF], F32), w_down=ein("w_down", [2, D_FF, D], F32),
            convw=ein("convw", [2, 128, NCT_, 3], F32), convb=ein("convb", [2, 128, NCT_], F32),
            halosel=ein("halosel", [4 * NH2, NH2], F32),
        )
        xout = nc.dram_tensor("xout", [NT, D], F32, kind="ExternalOutput").ap()
        T = dict(
            qiT=scr("s_qiT", [64, NT * 8], BF16), w2=scr("s_w2", [NT, 8], F32), qaT=scr("s_qaT", [512, NT], BF16),
            qbT=scr("s_qbT", [768, NT], BF16), kiT=scr("s_kiT", [NI * 64, 512], BF16), kaT=scr("s_kaT", [NT, 512], BF16),
            va=scr("s_va", [NT, 520], BF16), kbT=scr("s_kbT", [NI * 768, 512], BF16), vb=scr("s_vb", [NT, 520], BF16),
            gate=scr("s_gate", [NT, 2048], F32),
            kiT_g=scr("g_kiT", [4 * NI * 64, 512], BF16), kaT_g=scr("g_kaT", [4 * NI * 512, 512], BF16), va_g=scr("g_va", [4 * NT, 520], BF16),
            kbT_g=scr("g_kbT", [4 * NI * 768, 512], BF16), vb_g=scr("g_vb", [4 * NT, 520], BF16),
            yaT=scr("s_yaT", [512, NT], BF16), ybT=scr("s_ybT", [512, NT], BF16), xmid=scr("s_xmid", [NT, D], F32),
            x1=scr("s_x1", [NT, D], F32), halo_loc=scr("s_hloc", [NH2, D], F32), halo_g=scr("g_halo", [4 * NH2, D], F32),
            halo=scr("s_halo", [NI, 2, D], F32),
        )
        WB = {k: scr("b_" + k, list(X[k].shape), BF16) for k in ("w_in", "w_uq", "w_ukv", "w_pa", "w_pb", "w_out", "w_up", "w_down")}

        def conv(k, l):
            rows = X[k].shape[1]
            for r0 in range(0, rows, 128):
                r1 = min(rows, r0 + 128)
                sc.dma("pool", WB[k][l, r0:r1, :], X[k][l, r0:r1, :], "wconv")
        for k in ("w_in", "w_uq", "w_ukv"):
            conv(k, 0)
        sc.emit()
        for k, l in (("w_pa", 0), ("w_pb", 0), ("w_out", 0), ("w_down", 0), ("w_up", 0), ("w_in", 1), ("w_uq", 1), ("w_ukv", 1),
                     ("w_pa", 1), ("w_pb", 1), ("w_out", 1), ("w_down", 1), ("w_up", 1)):
            conv(k, l)
        for l in range(2):
            xin = X["x"] if l == 0 else T["x1"]
            env.d = dict(x=xin, w_in=WB["w_in"][l], w_uq=WB["w_uq"][l], w_ukv=WB["w_ukv"][l], gmix=X["gmix"][l], gqk=X["gqk"][l],
                         gcq=X["gcq"][l], gckv=X["gckv"][l], gbq=X["gbq"][l], gbk=X["gbk"][l], bgate=X["bgate"][l],
                         cos=X["cos"], sin=X["sin"], ident=X["ident"],
                         qiT=T["qiT"], w2=T["w2"], qaT=T["qaT"], qbT=T["qbT"], kiT=T["kiT"], kaT=T["kaT"], va=T["va"],
                         kbT=T["kbT"], vb=T["vb"], gate=T["gate"])
            def hook_sb(i):
                for k, rows in (("kiT", 64), ("kaT", 512), ("kbT", 768), ("va", 512), ("vb", 512)):
                    sc.coll("AllGather", [T[k][rows * i:rows * (i + 1), :]], [T[k + "_g"][4 * rows * i:4 * rows * (i + 1), :]], GROUPS,
                            reads=[("dr", k, i)], writes=[("g", k, i)])
            env.hook_sb = hook_sb
            env.skip_cc = True
            build_A(NT, env)
            env.hook_sb = None
            env.skip_cc = False
            env.d = dict(qiT=T["qiT"], w2T=T["w2"].rearrange("t h -> (t h)").rearrange("(c p) -> p c", p=128),
                         qaT=T["qaT"], qbT=T["qbT"], kiT_g=T["kiT_g"], kaT_g=T["kaT_g"], va_g=T["va_g"], kbT_g=T["kbT_g"],
                         vb_g=T["vb_g"], yaT=T["yaT"], ybT=T["ybT"],
                         **{k: X[k] for k in ("ident", "mg", "diagi", "selcol", "sel", "negrow", "onesrow", "bd", "e65", "pow2", "tbg", "far")})
            build_B(S, NI, env)
            env.d = dict(x=xin, yaT=T["yaT"], ybT=T["ybT"], gate=T["gate"], w_pa=WB["w_pa"][l], w_pb=WB["w_pb"][l], w_out=WB["w_out"][l],
                         ident=X["ident"], xmid=T["xmid"])
            build_C1(NT, env)
            for i in range(NI):
                sc.dma("sp", T["halo_loc"][2 * i:2 * i + 2, :], T["xmid"][SB * i + SB - 2:SB * i + SB, :], "hl")
            sc.emit()
            sc.coll("AllGather", [T["halo_loc"]], [T["halo_g"]], GROUPS)
            sc.emit()
            with ExitStack() as es2:
                cx = Ctx(nc, es2, env)
                hg = cx.sb([4 * NH2, D], F32)
                hs = cx.sb([4 * NH2, NH2], F32)
                ho = cx.sb([NH2, D], F32)
                pp = [cx.ps([128, 512], F32) for _ in range(2)]
                sc.dma("sp", hg[:], T["halo_g"], "hx", writes=["hg"])
                sc.dma("sp", hs[:], X["halosel"], "hx", writes=["hs"])
                for ct in range(2):
                    sc.op("pe", lambda e, ct=ct: e.matmul(pp[ct][0:NH2, :], hs[:, :], hg[:, ct * 512:(ct + 1) * 512], start=True, stop=True),
                          reads=["hg", "hs"], writes=[("pp", ct)])
                    sc.op("act", lambda e, ct=ct: e.activation(out=ho[:, ct * 512:(ct + 1) * 512], in_=pp[ct][0:NH2, :], func=AF.Copy),
                          reads=[("pp", ct)], writes=[("ho", ct)])
                sc.dma("sp", T["halo"].rearrange("i k d -> (i k) d"), ho[:], "hy", reads=[("ho", 0), ("ho", 1)])
                sc.emit()
            env.d = dict(xmid=T["xmid"], halo=T["halo"], gffn=X["gffn"][l], w_up=WB["w_up"][l], w_down=WB["w_down"][l],
                         convw=X["convw"][l], convb=X["convb"][l], ident=X["ident"], xout=(T["x1"] if l == 0 else xout))
            build_C2(NI, env)
        print("fused program: ops", sc.nops, "waits", sc.nwaits)
    return nc


def maps_fused(P, S, NI):
    x = np.asarray(P["x"], np.float32)
    NH2 = NI * 2
    ident = np.eye(128, dtype=np.float32).astype(NPBF)
    bidx = tb_bucket_idx()
    rb = np.asarray(P["rel_bias"], np.float32)
    tbg = np.ascontiguousarray(rb[bidx].transpose(0, 2, 1))
    far = np.ascontiguousarray(np.broadcast_to(rb[15][None, :], (128, 8)))
    st = lambda f: np.stack([f(l) for l in range(2)])
    shared = dict(
        w_in=np.asarray(P["w_in"], np.float32), w_uq=np.asarray(P["b_w_uq"], np.float32), w_ukv=np.asarray(P["b_w_ukv"], np.float32),
        gmix=st(lambda l: rep(P["norm_mix"][l])),
        gqk=st(lambda l: rep(np.concatenate([np.tile(P["a_q_norm"][l], 8), np.tile(P["a_k_norm"][l], 8)]))),
        gcq=st(lambda l: rep(P["b_cq_norm"][l])), gckv=st(lambda l: rep(P["b_ckv_norm"][l])),
        gbq=st(lambda l: rep(np.tile(P["b_q_norm"][l], 8))), gbk=st(lambda l: rep(np.tile(P["b_k_norm"][l], 8))),
        bgate=st(lambda l: rep(P["b_gate"][l])), ident=ident, tbg=tbg, far=far,
        w_pa=np.asarray(P["w_proj_a"], np.float32), w_pb=np.asarray(P["w_proj_b"], np.float32), w_out=np.asarray(P["w_out"], np.float32),
        gffn=st(lambda l: rep(P["norm_ffn"][l])), w_up=np.asarray(P["w_up"], np.float32), w_down=np.asarray(P["w_down"], np.float32),
        convw=st(lambda l: np.ascontiguousarray(np.asarray(P["conv_w"][l], np.float32).reshape(3, NCT_, 128).transpose(2, 1, 0))),
        convb=st(lambda l: np.ascontiguousarray(np.asarray(P["conv_b"][l], np.float32).reshape(NCT_, 128).T)),
    )
    maps = []
    for c in range(NCORE):
        b, j = c // 4, c % 4
        tok = core_tokens(NI, j)
        cos, sin = rope_tables(tok)
        hs = np.zeros((4 * NH2, NH2), np.float32)
        for i in range(NI):
            jj, ii = (j - 1, i) if j > 0 else (3, i - 1)
            if ii >= 0:
                for k in range(2):
                    hs[jj * NH2 + 2 * ii + k, 2 * i + k] = 1.0
        m = dict(shared)
        m.update(consts_B(j))
        m.update(x=np.ascontiguousarray(x[b][tok]), cos=cos, sin=sin, halosel=hs)
        maps.append(m)
    return maps


def kernel_fused(**inputs):
    P = {k: np.asarray(v) for k, v in inputs.items()}
    S = P["x"].shape[1]
    NI = S // (4 * SB)
    res = _run(_prog(("F", S, NI), build_fused, S, NI), maps_fused(P, S, NI))
    return gather_tok(res, "xout", S, NI).astype(np.float32)


def kernel(**inputs):
    return kernel_fused(**inputs)
```

```python
import math
from contextlib import ExitStack

import numpy as np
import ml_dtypes

import concourse.bass as bass
import concourse.mybir as mybir
from concourse.bass_utils import run_bass_kernel_spmd

F32 = mybir.dt.float32
BF16 = mybir.dt.bfloat16
FP8 = mybir.dt.float8e4
AF = mybir.ActivationFunctionType
ALU = mybir.AluOpType
AX = mybir.AxisListType
NPBF = ml_dtypes.bfloat16

D = 1024
NCORE = 8
SB = 512
IN_COLS = 4840
D_FF = 2816
NEG_IDX = -1.0e30
NEG_B = -30000.0
NIT = 16
EPS = 1e-6


class Sched:
    ENG = ("pe", "act", "dve", "pool", "sp")

    def __init__(self, nc, es):
        self.nc = nc
        self.es = es
        self.ops = {e: [] for e in self.ENG}
        self.sems = {}
        self.cnt = {}
        self.seen = {e: {} for e in self.ENG}
        self.lastw = {}
        self.readers = {}
        self.nops = 0
        self.nwaits = 0
        for e in self.ENG[:4]:
            self._mk(e)

    def _mk(self, key):
        self.sems[key] = self.es.enter_context(self.nc.semaphore("s_%d" % len(self.sems)))
        self.cnt[key] = 0

    def _deps(self, eng, reads, writes):
        need = {}

        def add(ref):
            if ref is None:
                return
            k, v = ref
            if k not in self.ENG:
                v = self.cnt[k]
            if need.get(k, 0) < v:
                need[k] = v

        for r in reads:
            add(self.lastw.get(r))
        for w in writes:
            add(self.lastw.get(w))
            for k, v in self.readers.get(w, {}).items():
                add((k, v))
        waits = []
        for k, v in need.items():
            if k == "pe" and eng == "pe":
                continue
            if self.seen[eng].get(k, 0) < v:
                self.seen[eng][k] = v
                waits.append((k, v))
        return waits

    def _commit(self, ref, reads, writes):
        k, v = ref
        for r in reads:
            d = self.readers.setdefault(r, {})
            if d.get(k, 0) < v:
                d[k] = v
        for w in writes:
            self.lastw[w] = ref
            self.readers[w] = {}

    def record(self, f):
        rec = []
        self._rec = rec
        try:
            f()
        finally:
            self._rec = None
        return rec

    def interleave(self, a, b):
        na, nb = len(a), len(b)
        ia = ib = 0
        while ia < na or ib < nb:
            if ib >= nb or (ia < na and ia * nb <= ib * na):
                a[ia]()
                ia += 1
            else:
                b[ib]()
                ib += 1

    def op(self, eng, fn, reads=(), writes=()):
        if getattr(self, "_rec", None) is not None:
            rec = self._rec
            rec.append(lambda: self._op(eng, fn, reads, writes))
            return
        self._op(eng, fn, reads, writes)

    def _op(self, eng, fn, reads=(), writes=()):
        waits = self._deps(eng, reads, writes)
        self.cnt[eng] += 1
        self.ops[eng].append((waits, fn, eng, 1))
        self._commit((eng, self.cnt[eng]), reads, writes)
        self.nops += 1
        self.nwaits += len(waits)

    def dma(self, q, out, in_, semkey, reads=(), writes=(), **kw):
        if getattr(self, "_rec", None) is not None:
            rec = self._rec
            rec.append(lambda: self._dma(q, out, in_, semkey, reads, writes, **kw))
            return
        self._dma(q, out, in_, semkey, reads, writes, **kw)

    def _dma(self, q, out, in_, semkey, reads=(), writes=(), **kw):
        if semkey not in self.sems:
            self._mk(semkey)
        waits = self._deps(q, reads, writes)
        self.cnt[semkey] += 16
        self.ops[q].append((waits, lambda e: e.dma_start(out=out, in_=in_, **kw), semkey, 16))
        self._commit((semkey, self.cnt[semkey]), reads, writes)
        self.nops += 1
        self.nwaits += len(waits)

    def coll(self, kind, ins, outs, groups, reads=(), writes=()):
        if getattr(self, "_rec", None) is not None:
            rec = self._rec
            rec.append(lambda: self._coll(kind, ins, outs, groups, reads, writes))
            return
        self._coll(kind, ins, outs, groups, reads, writes)

    def _coll(self, kind, ins, outs, groups, reads=(), writes=()):
        if "cc" not in self.sems:
            self._mk("cc")
        waits = self._deps("pool", reads, writes)
        self.cnt["cc"] += 1
        self.ops["pool"].append((waits, lambda e: e.collective_compute(kind, ALU.bypass, replica_groups=groups, ins=ins, outs=outs),
                                 "cc", 1))
        self._commit(("cc", self.cnt["cc"]), reads, writes)

    def barrier(self, skip_cc=False):
        for eng in self.ENG:
            waits = []
            for k, v in self.cnt.items():
                if skip_cc and k == "cc":
                    continue
                if v > 0 and self.seen[eng].get(k, 0) < v:
                    self.seen[eng][k] = v
                    waits.append((k, v))
            self.ops[eng].append((waits, None, None, 0))
        if skip_cc:
            self.lastw = {r: ref for r, ref in self.lastw.items() if ref[0] == "cc"}
        else:
            self.lastw = {}
        self.readers = {}

    def emit(self, skip_cc=False):
        nc = self.nc
        self.barrier(skip_cc)
        fin = []
        with nc.Block() as block:
            def run(e, name):
                for waits, fn, sk, amt in self.ops[name]:
                    for k, v in waits:
                        e.wait_ge(self.sems[k], v)
                    if fn is not None:
                        fn(e).then_inc(self.sems[sk], amt)

            @block.tensor
            def _(e):
                run(e, "pe")

            @block.scalar
            def _(e):
                run(e, "act")

            @block.vector
            def _(e):
                run(e, "dve")

            @block.gpsimd
            def _(e):
                run(e, "pool")

            @block.sync
            def _(e):
                run(e, "sp")
        self.ops = {e: [] for e in self.ENG}


class Pipe:
    def __init__(self, lag):
        self.lag = lag
        self.q = []

    def push(self, fns):
        self.q.append(fns)
        if len(self.q) > self.lag:
            for f in self.q.pop(0):
                f()

    def flush(self):
        while self.q:
            for f in self.q.pop(0):
                f()


LAG = 2
MASK_ENG = ("dve", "dve")


def wq(ap):
    return "sp" if ap.dtype == BF16 else "pool"


class Ctx:
    def __init__(self, nc, es, env=None):
        self.nc = nc
        self.es = es
        self.env = env
        self.n = env.n if env else 0

    def sb(self, shape, dt, name=None):
        self.n += 1
        if self.env:
            self.env.n = self.n
        return self.es.enter_context(self.nc.sbuf_tensor(name or ("t%d" % self.n), list(shape), dt))

    def ps(self, shape, dt, name=None):
        self.n += 1
        if self.env:
            self.env.n = self.n
        return self.es.enter_context(self.nc.psum_tensor(name or ("p%d" % self.n), list(shape), dt))

    def din(self, name, shape, dt):
        if self.env:
            ap = self.env.d[name]
            assert tuple(ap.shape) == tuple(shape), (name, ap.shape, shape)
            return ap
        return self.nc.dram_tensor(name, list(shape), dt, kind="ExternalInput").ap()

    def dout(self, name, shape, dt):
        if self.env:
            ap = self.env.d[name]
            assert tuple(ap.shape) == tuple(shape), (name, ap.shape, shape)
            return ap
        return self.nc.dram_tensor(name, list(shape), dt, kind="ExternalOutput").ap()


class Env:
    def __init__(self, nc, sc):
        self.nc = nc
        self.sc = sc
        self.d = {}
        self.n = 0


def t5_bucket_np(rel):
    nb = 16
    max_exact = 8
    side = np.where(rel > 0, nb, 0)
    n = np.abs(rel)
    nf = np.maximum(n, 1).astype(np.float32)
    large = max_exact + (np.log(nf / max_exact) / math.log(128 / max_exact) * (nb - max_exact)).astype(np.int32)
    large = np.minimum(large, nb - 1)
    return side + np.where(n < max_exact, n, large)


def consts_B(j):
    c = {}
    c["ident"] = np.eye(128, dtype=np.float32).astype(NPBF)
    mg = np.zeros((128, 8, 128), np.float32)
    for p in range(128):
        t16 = p // 8
        for g in range(8):
            mg[p, g, 16 * g + t16] = 1.0
    c["mg"] = mg.astype(NPBF)
    t = np.arange(512)[:, None]
    s = np.arange(512)[None, :]
    diag = np.where((s // 64) > (t // 64), NEG_IDX, 0.0).astype(np.float32)
    c["diagi"] = diag.reshape(4, 128, 512).transpose(1, 0, 2).astype(NPBF).copy()
    selcol = np.zeros((128, 8), np.float32)
    for m in range(4):
        selcol[:, m] = 1.0 if m == j else 0.0
        selcol[:, 4 + m] = NEG_IDX if m > j else 0.0
    c["selcol"] = selcol
    sel = np.zeros((128, 8, 128), np.float32)
    for m in range(4):
        if m == j:
            sel[:, m, :] = np.eye(128)
        if m == j:
            sel[:, 4 + m, :] = np.eye(128)
    c["sel"] = sel.astype(NPBF)
    negrow = np.zeros((1, 4, 128), np.float32)
    for m in range(4):
        if m > j:
            negrow[0, m, :] = NEG_B
    c["negrow"] = negrow.astype(NPBF)
    c["onesrow"] = np.ones((1, 512), np.float32).astype(NPBF)
    k = np.arange(128)[:, None, None]
    sub = np.arange(4)[None, :, None]
    tt = np.arange(512)[None, None, :]
    bd = np.where(((128 * sub + k) // 64) > (tt // 64), NEG_B, 0.0).astype(np.float32)
    c["bd"] = bd.astype(NPBF)
    e65 = np.zeros((65, 64), np.float32)
    e65[64, :] = 1.0
    c["e65"] = e65
    c["pow2"] = np.tile((2.0 ** -(np.arange(NIT) + 1.0))[None, :], (128, 1)).astype(np.float32)
    return c


def tb_bucket_idx():
    k = np.arange(128)[:, None]
    cc = np.arange(384)[None, :]
    return t5_bucket_np(k - cc + 128)


def build_B(S, NI, env=None):
    NT = NI * SB
    nc = env.nc if env else bass.Bass("TRN2", target_bir_lowering=False)
    es = ExitStack()
    with es:
        cx = Ctx(nc, es, env)
        sc = env.sc if env else Sched(nc, es)
        d_qiT = cx.din("qiT", [64, NT * 8], BF16)
        d_w2T = cx.din("w2T", [128, NT * 8 // 128], F32)
        d_qaT = cx.din("qaT", [512, NT], BF16)
        d_qbT = cx.din("qbT", [768, NT], BF16)
        d_kiT = cx.din("kiT_g", [4 * NI * 64, 512], BF16)
        d_kaT = cx.din("kaT_g", [4 * NI * 512, 512], BF16)
        d_va = cx.din("va_g", [4 * NT, 520], BF16)
        d_kbT = cx.din("kbT_g", [4 * NI * 768, 512], BF16)
        d_vb = cx.din("vb_g", [4 * NT, 520], BF16)
        d_ident = cx.din("ident", [128, 128], BF16)
        d_mg = cx.din("mg", [128, 8, 128], BF16)
        d_diagi = cx.din("diagi", [128, 4, 512], BF16)
        d_selcol = cx.din("selcol", [128, 8], F32)
        d_sel = cx.din("sel", [128, 8, 128], BF16)
        d_negrow = cx.din("negrow", [1, 4, 128], BF16)
        d_onesrow = cx.din("onesrow", [1, 512], BF16)
        d_bd = cx.din("bd", [128, 4, 512], BF16)
        d_e65 = cx.din("e65", [65, 64], F32)
        d_pow2 = cx.din("pow2", [128, NIT], F32)
        d_tbg = cx.din("tbg", [128, 8, 384], F32)
        d_far = cx.din("far", [128, 8], F32)
        d_yaT = cx.dout("yaT", [512, NT], BF16)
        d_ybT = cx.dout("ybT", [512, NT], BF16)

        ident = cx.sb([128, 128], BF16)
        mg = cx.sb([128, 8, 128], BF16)
        diagi = cx.sb([128, 4, 512], BF16)
        selcol = cx.sb([128, 8], F32)
        sel = cx.sb([128, 8, 128], BF16)
        negrow = cx.sb([1, 4, 128], BF16)
        onesrow = cx.sb([1, 512], BF16)
        bd = cx.sb([128, 4, 512], BF16)
        e65 = cx.sb([65, 64], F32)
        pow2 = cx.sb([128, NIT], F32)
        far = cx.sb([128, 8], F32)
        dtmp = cx.sb([128, 2048], F32)
        tbg3 = dtmp[:, 0:1536].rearrange("p (h c) -> p h c", h=4)
        tb = cx.sb([128, 8, 384], BF16)
        ki_t = [cx.sb([64, 512], BF16) for _ in range(2)]
        w2T = cx.sb([128, NT * 8 // 128], F32)
        for i_, (dst, src) in enumerate([(ident, d_ident), (mg, d_mg), (diagi, d_diagi), (selcol, d_selcol),
                                         (sel, d_sel), (negrow, d_negrow), (onesrow, d_onesrow), (bd, d_bd),
                                         (e65, d_e65), (pow2, d_pow2), (far, d_far), (far, d_far)]):
            sc.dma("sp", dst[:], src, ("c", i_), writes=[("c", i_)])
        for hq in range(2):
            sc.dma("sp", tbg3, d_tbg[:, 4 * hq:4 * hq + 4, :], "tbg", writes=["dtmp"])
            sc.op("dve", lambda e, hq=hq: e.tensor_tensor(out=tb[:, 4 * hq:4 * hq + 4, :], in0=tbg3,
                                                          in1=far[:, 4 * hq:4 * hq + 4].unsqueeze(2).to_broadcast([128, 4, 384]),
                                                          op=ALU.subtract),
                  reads=[("c", 11), "dtmp"], writes=["tb"])

        score = cx.sb([128, S], F32)
        maskT = cx.sb([128, S // 128, 512], FP8)
        junk = cx.sb([128, 2048], BF16)
        qi_sb = cx.sb([64, SB * 8], BF16)
        qa_sb = qi_sb[:, :].rearrange("d (h t) -> d h t", h=8)
        qb2_sb = cx.sb([96, 2, SB], BF16)
        w2g = cx.sb([128, 8, 128], BF16)
        NH = 3
        hid = [cx.sb([128, 512], BF16) for _ in range(NH)]
        mts = [cx.sb([128, 512], BF16) for _ in range(2)]
        NPT = 4
        pT = [cx.sb([128, 512], BF16) for _ in range(NPT)]
        kt_b = [cx.sb([96, 4, 512], BF16) for _ in range(2)]
        v_b = [cx.sb([128, 4, 260], BF16) for _ in range(2)]
        o_sb = [cx.sb([65, 512], F32) for _ in range(2)]
        rec = [cx.sb([64, 512], F32) for _ in range(1)]
        yT = [cx.sb([64, 512], BF16) for _ in range(2)]
        st = cx.sb([128, 16], F32)
        cnts = cx.sb([128, 8], F32)
        steps = cx.sb([128, NIT], F32)
        ps = [cx.ps([128, 512], F32) for _ in range(8)]

        cst = lambda *idx: [("c", k) for k in idx]
        state = dict(hid=0, pT=0, kv=0, o=0, mts=0, ki=0)

        for i in range(NI):
            nkt = 4 * i + 4
            tok0 = i * SB
            sc.dma("sp", qi_sb[:], d_qiT[:, tok0 * 8:(tok0 + SB) * 8], "q", writes=["qi"])
            sc.dma("sp", w2T[:, 32 * i:32 * i + 32], d_w2T[:, 32 * i:32 * i + 32], "q", writes=[("w2T", i)],
                   allow_slow_non_contiguous=True)

            def b_piece(hpair, i=i, nkt=nkt, tok0=tok0):
                sc.dma("sp", qb2_sb[:], d_qbT[192 * hpair:192 * hpair + 192, tok0:tok0 + SB].rearrange("(h d) t -> d h t", d=96),
                       "q2", writes=["qb2"])
                pipe = Pipe(LAG)
                u = 0
                for kt in range(nkt):
                    slot = state["kv"] % 2
                    state["kv"] += 1
                    ktile, vtile = kt_b[slot], v_b[slot]
                    sc.dma("sp", ktile[:, 0:2, :],
                           d_kbT[768 * kt + 192 * hpair:768 * kt + 192 * hpair + 192, :].rearrange("(h d) s -> d h s", d=96),
                           ("kv", slot), reads=[("g", "kbT", kt // 4)], writes=[("k", slot)])
                    sc.dma("sp", vtile[:, :, 0:130],
                           d_vb[512 * kt:512 * kt + 512, 130 * hpair:130 * hpair + 130].rearrange("(u p) c -> p u c", p=128),
                           ("kv", slot), reads=[("g", "vb", kt // 4)], writes=[("v", slot)])
                    diag = kt >= 4 * i
                    m = kt - 4 * i
                    for sub in range(4):
                        for hh in range(2):
                            sb_i = 4 + (u % 4)
                            u += 1
                            pss = ps[sb_i]
                            sc.op("pe", lambda e, pss=pss, ktile=ktile, hh=hh, sub=sub, diag=diag: e.matmul(
                                pss[:, :], ktile[:, hh, sub * 128:(sub + 1) * 128], qb2_sb[:, hh, :], start=True, stop=not diag),
                                reads=[("k", slot), "qb2"], writes=[("ps", sb_i)])
                            if diag:
                                sc.op("pe", lambda e, pss=pss, m=m, sub=sub: e.matmul(
                                    pss[:, :], sel[:, m, :], bd[:, sub, :], start=False, stop=False),
                                    reads=cst(4, 7), writes=[("ps", sb_i)])
                                sc.op("pe", lambda e, pss=pss, m=m: e.matmul(
                                    pss[:, :], negrow[0:1, m, :], onesrow[0:1, :], start=False, stop=True),
                                    reads=cst(5, 6), writes=[("ps", sb_i)])
                            pslot = state["pT"] % NPT
                            state["pT"] += 1
                            pt = pT[pslot]
                            sc.op("act", lambda e, pt=pt, pss=pss: e.activation(out=pt[:, :], in_=pss[:, :], func=AF.Exp),
                                  reads=[("ps", sb_i)], writes=[("pT", pslot)])
                            first = (kt == 0 and sub == 0)
                            last = (kt == nkt - 1 and sub == 3)
                            pipe.push([lambda hh=hh, vtile=vtile, sub=sub, pt=pt, first=first, last=last, slot=slot, pslot=pslot: sc.op(
                                "pe", lambda e: e.matmul(
                                    ps[hh][0:65, :], vtile[:, sub, hh * 65:(hh + 1) * 65], pt[:, :], start=first, stop=last),
                                reads=[("v", slot), ("pT", pslot)], writes=[("ps", hh)])])
                pipe.flush()
                for hh in range(2):
                    h = 2 * hpair + hh
                    _finalize(sc, state, ps[hh], ps[4 + hh], o_sb, rec, yT, e65, 512,
                              d_ybT[h * 64:(h + 1) * 64, tok0:tok0 + SB], hh, 4 + hh)

            for half in range(1):
                for qb_i in range(4):
                    qq = qb_i
                    tq = qb_i * 128
                    gcol = (tok0 + tq) * 8 // 128
                    sc.op("dve", lambda e, gcol=gcol: e.tensor_tensor(
                        out=w2g[:], in0=mg[:], in1=w2T[:, gcol:gcol + 8].unsqueeze(2).to_broadcast([128, 8, 128]),
                        op=ALU.mult), reads=cst(1) + [("w2T", i)], writes=["w2g"])
                    pipe = Pipe(LAG)
                    for kt in range(nkt):
                        psc_i = 4 + (kt % 2)
                        psc = ps[psc_i]
                        kis = state["ki"] % 2
                        state["ki"] += 1
                        kit = ki_t[kis]
                        sc.dma("sp", kit[:, :], d_kiT[64 * kt:64 * kt + 64, :], ("ki", kis), reads=[("g", "kiT", kt // 4)], writes=[("ki", kis)])
                        for g in range(8):
                            ph_i = (state["hid"]) % NH
                            state["hid"] += 1
                            ph = ps[ph_i]
                            hd = hid[ph_i]
                            c0 = (tq + 16 * g) * 8
                            sc.op("pe", lambda e, ph=ph, c0=c0, kit=kit: e.matmul(
                                ph[:, :], qi_sb[:, c0:c0 + 128], kit[:, :], start=True, stop=True),
                                reads=["qi", ("ki", kis)], writes=[("ps", ph_i)])
                            sc.op("act", lambda e, hd=hd, ph=ph: e.activation(out=hd[:, :], in_=ph[:, :], func=AF.Relu),
                                  reads=[("ps", ph_i)], writes=[("hid", ph_i)])
                            fns = [lambda psc=psc, g=g, hd=hd, ph_i=ph_i, psc_i=psc_i: sc.op("pe", lambda e: e.matmul(
                                psc[:, :], w2g[:, g, :], hd[:, :], start=(g == 0), stop=(g == 7)),
                                reads=["w2g", ("hid", ph_i)], writes=[("ps", psc_i)])]
                            if g == 7:
                                sco = score[:, kt * 512:(kt + 1) * 512]
                                if kt >= 4 * i:
                                    m = kt - 4 * i
                                    fns.append(lambda sco=sco, psc=psc, m=m, qb_i=qb_i, psc_i=psc_i, kt=kt: sc.op(
                                        "dve", lambda e: e.scalar_tensor_tensor(
                                            out=sco, in0=diagi[:, qb_i, :], scalar=selcol[:, m:m + 1], in1=psc[:, :],
                                            op0=ALU.mult, op1=ALU.add), reads=[("ps", psc_i)] + cst(2, 3), writes=[("sc", kt)]))
                                    fns.append(lambda sco=sco, m=m, kt=kt: sc.op("dve", lambda e: e.tensor_scalar(
                                        out=sco, in0=sco, scalar1=selcol[:, 4 + m:5 + m], scalar2=None, op0=ALU.add),
                                        reads=[("sc", kt)] + cst(3), writes=[("sc", kt)]))
                                else:
                                    fns.append(lambda sco=sco, psc=psc, psc_i=psc_i, kt=kt: sc.op(
                                        "dve", lambda e: e.tensor_copy(out=sco, in_=psc[:, :]),
                                        reads=[("ps", psc_i)], writes=[("sc", kt)]))
                            pipe.push(fns)
                    pipe.flush()
                    n = nkt * 512
                    nf = 4 * i * 512
                    allsc = [("sc", kt) for kt in range(nkt)]
                    sc.op("dve", lambda e, n=n: e.tensor_reduce(out=st[:, 0:1], in_=score[:, 0:n], axis=AX.X, op=ALU.max),
                          reads=allsc, writes=["st0"])
                    sc.op("dve", lambda e, nf=nf, n=n: e.tensor_scalar(
                        out=dtmp[:, 0:2048], in0=score[:, nf:n], scalar1=-1.0e29, scalar2=4.0e30, op0=ALU.is_lt, op1=ALU.mult),
                        reads=allsc, writes=["dtmp"])
                    sc.op("dve", lambda e, nf=nf, n=n: e.tensor_tensor(
                        out=dtmp[:, 0:2048], in0=dtmp[:, 0:2048], in1=score[:, nf:n], op=ALU.add),
                        reads=allsc + ["dtmp"], writes=["dtmp"])
                    sc.op("dve", lambda e: e.tensor_reduce(out=st[:, 3:4], in_=dtmp[:, 0:2048], axis=AX.X, op=ALU.min),
                          reads=["dtmp"], writes=["st3"])
                    if i > 0:
                        sc.op("dve", lambda e, nf=nf: e.tensor_reduce(out=st[:, 1:2], in_=score[:, 0:nf], axis=AX.X, op=ALU.min),
                              reads=allsc, writes=["st1"])
                        sc.op("dve", lambda e: e.tensor_tensor(out=st[:, 3:4], in0=st[:, 3:4], in1=st[:, 1:2], op=ALU.min),
                              reads=["st1", "st3"], writes=["st3"])
                    sc.op("dve", lambda e: e.tensor_tensor(out=st[:, 4:5], in0=st[:, 0:1], in1=st[:, 3:4], op=ALU.subtract),
                          reads=["st0", "st3"], writes=["st4"])
                    sc.op("dve", lambda e: e.tensor_scalar(out=steps[:, :], in0=pow2[:, :], scalar1=st[:, 4:5], scalar2=None,
                                                           op0=ALU.mult), reads=["st4"] + cst(9), writes=["steps"])
                    sc.op("dve", lambda e: e.scalar_tensor_tensor(out=st[:, 5:6], in0=st[:, 4:5], scalar=0.5, in1=st[:, 3:4],
                                                                  op0=ALU.mult, op1=ALU.add),
                          reads=["st4", "st3"], writes=["thr"])
                    nch = (n + 2047) // 2048
                    for it in range(NIT):
                        for ch in range(nch):
                            a0 = ch * 2048
                            a1 = min(n, a0 + 2048)
                            sc.op("dve", lambda e, a0=a0, a1=a1, ch=ch: e.tensor_scalar(
                                out=junk[:, 0:a1 - a0], in0=score[:, a0:a1], scalar1=st[:, 5:6], scalar2=0.0,
                                op0=ALU.is_ge, op1=ALU.add, accum_out=cnts[:, ch:ch + 1]),
                                reads=allsc + ["thr"], writes=["junk", ("cnt", ch)])
                        if nch > 1:
                            sc.op("dve", lambda e, nch=nch: e.tensor_reduce(out=st[:, 8:9], in_=cnts[:, 0:nch], axis=AX.X, op=ALU.add),
                                  reads=[("cnt", ch) for ch in range(nch)], writes=["ctot"])
                            csrc = st[:, 8:9]
                            crd = ["ctot"]
                        else:
                            csrc = cnts[:, 0:1]
                            crd = [("cnt", 0)]
                        sc.op("dve", lambda e, csrc=csrc: e.tensor_scalar(out=st[:, 6:7], in0=csrc, scalar1=255.5, scalar2=0.5,
                                                                         op0=ALU.is_ge, op1=ALU.subtract),
                              reads=crd, writes=["g"])
                        sc.op("dve", lambda e, it=it: e.scalar_tensor_tensor(out=st[:, 5:6], in0=st[:, 6:7], scalar=steps[:, it:it + 1],
                                                                             in1=st[:, 5:6], op0=ALU.mult, op1=ALU.add),
                              reads=["g", "steps", "thr"], writes=["thr"])
                    sc.op("dve", lambda e: e.scalar_tensor_tensor(out=st[:, 7:8], in0=st[:, 4:5], scalar=-(2.0 ** -(NIT - 1)),
                                                                  in1=st[:, 5:6], op0=ALU.mult, op1=ALU.add),
                          reads=["st4", "thr"], writes=["thrF"])
                    b_piece(qb_i)
                    for kt in range(nkt):
                        ms = state["mts"] % 2
                        state["mts"] += 1
                        mt = mts[ms]
                        sc.op("dve", lambda e, mt=mt, kt=kt: e.tensor_scalar(
                            out=mt[:, :], in0=score[:, kt * 512:(kt + 1) * 512], scalar1=st[:, 7:8], scalar2=None, op0=ALU.is_ge),
                            reads=[("sc", kt), "thrF"], writes=[("mts", ms)])
                        pti = 6 + ms
                        ptb = ps[pti][:, :].bitcast(BF16)
                        for sub in range(4):
                            sc.op("pe", lambda e, ptb=ptb, mt=mt, sub=sub: e.transpose(
                                ptb[:, sub * 128:(sub + 1) * 128], mt[:, sub * 128:(sub + 1) * 128], ident[:, :]),
                                reads=[("mts", ms)] + cst(0), writes=[("ps", pti)])
                        sc.op("act", lambda e, ptb=ptb, kt=kt, qq=qq: e.activation(
                            out=maskT[:, kt * 4:kt * 4 + 4, qq * 128:(qq + 1) * 128],
                            in_=ptb[:, 0:512].rearrange("p (u t) -> p u t", u=4), func=AF.Copy, saturate=False),
                            reads=[("ps", pti)], writes=[("mT", kt, qq)])
            sc.dma("sp", qa_sb, d_qaT[:, tok0:tok0 + SB].rearrange("(h d) t -> d h t", d=64), "q", writes=["qi"])
            for hg in range(2):
                pipe = Pipe(LAG)
                u = 0
                for kt in range(nkt):
                    slot = state["kv"] % 2
                    state["kv"] += 1
                    ktile, vtile = kt_b[slot], v_b[slot]
                    sc.dma("sp", ktile[0:64, :, :],
                           d_kaT[512 * kt + 256 * hg:512 * kt + 256 * hg + 256, :].rearrange("(h d) s -> d h s", d=64),
                           ("kv", slot), reads=[("g", "kaT", kt // 4)], writes=[("k", slot)])
                    sc.dma("sp", vtile[:, :, :],
                           d_va[512 * kt:512 * kt + 512, 260 * hg:260 * hg + 260].rearrange("(u p) c -> p u c", p=128),
                           ("kv", slot), reads=[("g", "va", kt // 4)], writes=[("v", slot)])
                    for sub in range(4):
                        roles = []
                        if kt >= 4 * i:
                            roles.append((kt - 4 * i, sub))
                        if sub == 3 and kt >= 4 * i - 1 and kt - 4 * i + 1 <= 3:
                            roles.append((4 + (kt - 4 * i + 1), -1))
                        mms = []
                        for (si, sg) in roles:
                            tlo = max(128 * sg - 128, 0)
                            thi = min(128 * sg + 256, 512)
                            if thi > tlo:
                                mms.append((si, tlo, thi, tlo - 128 * sg + 128, thi - 128 * sg + 128))
                        for hh in range(4):
                            h = 4 * hg + hh
                            sb_i = 4 + (u % 4)
                            u += 1
                            pss = ps[sb_i]
                            sc.op("pe", lambda e, pss=pss, ktile=ktile, hh=hh, h=h, sub=sub, nm=len(mms): e.matmul(
                                pss[:, :], ktile[0:64, hh, sub * 128:(sub + 1) * 128], qa_sb[:, h, :], start=True, stop=(nm == 0),
                                skip_group_check=True),
                                reads=[("k", slot), "qi"], writes=[("ps", sb_i)])
                            for idx, (si, tlo, thi, clo, chi) in enumerate(mms):
                                sc.op("pe", lambda e, pss=pss, si=si, tlo=tlo, thi=thi, clo=clo, chi=chi, h=h,
                                      lastm=(idx == len(mms) - 1): e.matmul(
                                    pss[:, tlo:thi], sel[:, si, :], tb[:, h, clo:chi], start=False, stop=lastm, skip_group_check=True),
                                    reads=cst(4) + ["tb"], writes=[("ps", sb_i)])
                            pslot = state["pT"] % NPT
                            state["pT"] += 1
                            pt = pT[pslot]
                            sc.op("act", lambda e, pt=pt, pss=pss: e.activation(out=pt[:, :], in_=pss[:, :], func=AF.Exp),
                                  reads=[("ps", sb_i)], writes=[("pT", pslot)])
                            meng = MASK_ENG[u % len(MASK_ENG)]
                            sc.op(meng, lambda e, pt=pt, kt=kt, sub=sub: e.tensor_tensor(
                                out=pt[:, :], in0=pt[:, :], in1=maskT[:, kt * 4 + sub, :], op=ALU.mult),
                                reads=[("pT", pslot)] + [("mT", kt, q_) for q_ in range(4)], writes=[("pT", pslot)])
                            first = (kt == 0 and sub == 0)
                            last = (kt == nkt - 1 and sub == 3)
                            pipe.push([lambda hh=hh, vtile=vtile, sub=sub, pt=pt, first=first, last=last, slot=slot, pslot=pslot: sc.op(
                                "pe", lambda e: e.matmul(
                                    ps[hh][0:65, :], vtile[:, sub, hh * 65:(hh + 1) * 65], pt[:, :], start=first, stop=last),
                                reads=[("v", slot), ("pT", pslot)], writes=[("ps", hh)])])
                pipe.flush()
                for hh in range(4):
                    h = 4 * hg + hh
                    _finalize(sc, state, ps[hh], ps[4 + hh], o_sb, rec, yT, e65, 512,
                              d_yaT[h * 64:(h + 1) * 64, tok0:tok0 + SB], hh, 4 + hh)
        sc.emit()
        print("B program: ops", sc.nops, "waits", sc.nwaits)
    return nc


def _finalize(sc, state, acc, pden, o_sb, rec, yT, e65, n, dst, acc_i, den_i, outs=None):
    k = state["o"] % 2
    state["o"] += 1
    kr = k % len(rec)
    o, r, y = o_sb[k], rec[kr], yT[k]
    sc.op("act", lambda e: e.activation(out=o[:, 0:n], in_=acc[0:65, 0:n], func=AF.Copy),
          reads=[("ps", acc_i)], writes=[("o", k)])
    sc.op("pe", lambda e: e.matmul(pden[0:64, 0:n], e65[:, :], o[:, 0:n], start=True, stop=True),
          reads=[("o", k), ("c", 8)], writes=[("ps", den_i)])
    sc.op("dve", lambda e: e.reciprocal(out=r[:, 0:n], in_=pden[0:64, 0:n]),
          reads=[("ps", den_i)], writes=[("rec", kr)])
    sc.op("dve", lambda e: e.tensor_tensor(out=y[:, 0:n], in0=o[0:64, 0:n], in1=r[:, 0:n], op=ALU.mult),
          reads=[("o", k), ("rec", kr)], writes=[("y", k)])
    if outs is None:
        outs = [(dst, 0, n)]
    for (d, a, b) in outs:
        sc.dma("sp", d, y[:, a:b], ("yo", k), reads=[("y", k)], writes=[])


def core_tokens(NI, j):
    return np.concatenate([np.arange(SB * (4 * i + j), SB * (4 * i + j) + SB) for i in range(NI)])


def maps_B(rA, rel_bias, NI):
    bidx = tb_bucket_idx()
    tbg = np.ascontiguousarray(np.asarray(rel_bias, np.float32)[bidx].transpose(0, 2, 1))
    far = np.ascontiguousarray(np.broadcast_to(np.asarray(rel_bias, np.float32)[15][None, :], (128, 8)))
    maps = []
    for c in range(NCORE):
        b, j = c // 4, c % 4
        NT = NI * SB
        m = dict(consts_B(j))
        m["tbg"] = tbg
        m["far"] = far
        for k in ("qiT", "qaT", "qbT"):
            m[k] = rA[c][k]
        m["w2T"] = np.ascontiguousarray(rA[c]["w2"].reshape(NT * 8 // 128, 128).T)
        for k, rows in (("kiT", 64), ("kaT", 512), ("kbT", 768), ("va", 512), ("vb", 512)):
            m[k + "_g"] = np.concatenate([rA[4 * b + kt % 4][k][rows * (kt // 4):rows * (kt // 4) + rows] for kt in range(4 * NI)], axis=0)
        maps.append(m)
    return maps


def split_G(G, S, NI):
    out = []
    for c in range(NCORE):
        b, j = c // 4, c % 4
        tok = core_tokens(NI, j)
        NT = len(tok)
        r = {}
        r["qiT"] = np.ascontiguousarray(G["qiT"][b].reshape(64, S, 8)[:, tok, :].reshape(64, NT * 8))
        r["w2"] = np.ascontiguousarray(G["w2"][b][tok])
        for k in ("qaT", "qbT"):
            r[k] = np.ascontiguousarray(G[k][b][:, :, tok].reshape(-1, NT))
        for k in ("kaT", "kbT"):
            a_ = G[k][b][:, :, tok].reshape(-1, NI, SB)
            r[k] = np.ascontiguousarray(a_.transpose(1, 0, 2).reshape(-1, SB))
        r["kiT"] = np.ascontiguousarray(G["kiT"][b][:, tok].reshape(64, NI, SB).transpose(1, 0, 2).reshape(-1, SB))
        for k in ("va", "vb"):
            r[k] = np.ascontiguousarray(G[k][b][tok])
        out.append(r)
    return out


def gather_B(results, S, NI):
    yaT = np.zeros((2, 8, 64, S), NPBF)
    ybT = np.zeros((2, 8, 64, S), NPBF)
    for c in range(NCORE):
        b, j = c // 4, c % 4
        tok = core_tokens(NI, j)
        yaT[b][:, :, tok] = results[c]["yaT"].reshape(8, 64, -1)
        ybT[b][:, :, tok] = results[c]["ybT"].reshape(8, 64, -1)
    return yaT, ybT


OFF = dict(qa=0, ka=512, va=1024, qi=1536, ki=2048, wi=2112, cq=2120, ckv=2504, kr=2760, ga=2792)


def build_A(NT, env=None):
    nc = env.nc if env else bass.Bass("TRN2", target_bir_lowering=False)
    es = ExitStack()
    with es:
        cx = Ctx(nc, es, env)
        sc = env.sc if env else Sched(nc, es)
        d_x = cx.din("x", [NT, D], F32)
        d_win = cx.din("w_in", [D, IN_COLS], F32)
        d_wuq = cx.din("w_uq", [384, 768], F32)
        d_wukv = cx.din("w_ukv", [256, 1024], F32)
        d_gmix = cx.din("gmix", [128, 1024], F32)
        d_gqk = cx.din("gqk", [128, 1024], F32)
        d_gcq = cx.din("gcq", [128, 384], F32)
        d_gckv = cx.din("gckv", [128, 256], F32)
        d_gbq = cx.din("gbq", [128, 768], F32)
        d_gbk = cx.din("gbk", [128, 768], F32)
        d_bgate = cx.din("bgate", [128, 2048], F32)
        d_cos = cx.din("cos", [NT, 16], F32)
        d_sin = cx.din("sin", [NT, 16], F32)
        d_ident = cx.din("ident", [128, 128], BF16)
        o_qiT = cx.dout("qiT", [64, NT * 8], BF16)
        o_w2 = cx.dout("w2", [NT, 8], F32)
        o_qaT = cx.dout("qaT", [512, NT], BF16)
        o_qbT = cx.dout("qbT", [768, NT], BF16)
        o_kiT = cx.dout("kiT", [NT // 512 * 64, 512], BF16)
        o_kaT = cx.dout("kaT", [NT, 512], BF16)
        o_va = cx.dout("va", [NT, 520], BF16)
        o_kbT = cx.dout("kbT", [NT // 512 * 768, 512], BF16)
        o_vb = cx.dout("vb", [NT, 520], BF16)
        o_gate = cx.dout("gate", [NT, 2048], F32)

        win = cx.sb([128, 8, IN_COLS], BF16)
        wuq = cx.sb([128, 3, 768], BF16)
        wukv = cx.sb([128, 2, 1024], BF16)
        gmix = cx.sb([128, 1024], F32)
        gqk = cx.sb([128, 1024], F32)
        gcq = cx.sb([128, 384], F32)
        gckv = cx.sb([128, 256], F32)
        gbq = cx.sb([128, 768], F32)
        gbk = cx.sb([128, 768], F32)
        bgate = cx.sb([128, 2048], F32)
        ident = cx.sb([128, 128], BF16)
        for kc in range(8):
            sc.dma(wq(d_win), win[:, kc, :], d_win[kc * 128:(kc + 1) * 128, :], ("w", kc), writes=[("win", kc)])
        sc.dma(wq(d_wuq), wuq[:], d_wuq.rearrange("(k p) c -> p k c", p=128), "wuq", writes=["wuq"])
        sc.dma(wq(d_wukv), wukv[:], d_wukv.rearrange("(k p) c -> p k c", p=128), "wukv", writes=["wukv"])
        for nm, t_, d_ in (("gmix", gmix, d_gmix), ("gqk", gqk, d_gqk), ("gcq", gcq, d_gcq), ("gckv", gckv, d_gckv),
                           ("gbq", gbq, d_gbq), ("gbk", gbk, d_gbk), ("bgate", bgate, d_bgate), ("ident", ident, d_ident)):
            sc.dma("sp", t_[:], d_, nm, writes=[nm])
        sc.op("dve", lambda e: e.tensor_scalar(out=gqk[:, 0:512], in0=gqk[:, 0:512], scalar1=0.125, scalar2=None, op0=ALU.mult),
              reads=["gqk"], writes=["gqk"])
        sc.op("dve", lambda e: e.tensor_scalar(out=gbq[:, :], in0=gbq[:, :], scalar1=96.0 ** -0.5, scalar2=None, op0=ALU.mult),
              reads=["gbq"], writes=["gbq"])

        xt = [cx.sb([128, 1024], F32) for _ in range(2)]
        junkb = cx.sb([128, 1024], BF16)
        xg = cx.sb([128, 1024], BF16)
        xT = cx.sb([128, 1024], BF16)
        zz = [cx.sb([128, IN_COLS], F32) for _ in range(2)]
        tmpf = cx.sb([128, 1024], F32)
        qk_b = cx.sb([128, 1024], BF16)
        qkT = cx.sb([64, 16, 128], BF16)
        va_sb = cx.sb([128, 8, 65], BF16)
        vb_sb = cx.sb([128, 8, 65], BF16)
        qi_b = cx.sb([128, 512], BF16)
        qiT_sb = cx.sb([64, 1024], BF16)
        ki_b = cx.sb([128, 64], BF16)
        kiT_sb = cx.sb([64, 128], BF16)
        w2_sb = cx.sb([128, 8], F32)
        cq_b = cx.sb([128, 384], BF16)
        cqT = cx.sb([128, 3, 128], BF16)
        ckv_b = cx.sb([128, 256], BF16)
        ckvT = cx.sb([128, 2, 128], BF16)
        qb_f = cx.sb([128, 8, 96], F32)
        kv_f = cx.sb([128, 8, 128], F32)
        kb_f = cx.sb([128, 8, 96], F32)
        qb_b = cx.sb([128, 8, 96], BF16)
        kb_b = cx.sb([128, 8, 96], BF16)
        qbT_sb = cx.sb([96, 8, 128], BF16)
        kbT_sb = cx.sb([96, 8, 128], BF16)
        gate_sb = cx.sb([128, 1024], F32)
        cs = cx.sb([128, 32], F32)
        rt = cx.sb([128, 8, 64], F32)
        krr = cx.sb([128, 32], F32)
        st = cx.sb([128, 64], F32)
        ps = [cx.ps([128, 512], F32) for _ in range(8)]
        sc.op("dve", lambda e: e.memset(va_sb[:], 1.0), writes=["va"])
        sc.op("dve", lambda e: e.memset(vb_sb[:], 1.0), writes=["vb"])

        def rs_of(src, dst, dim, rd, wr):
            sc.op("dve", lambda e: e.tensor_scalar(out=dst, in0=src, scalar1=1.0 / dim, scalar2=EPS, op0=ALU.mult, op1=ALU.add),
                  reads=rd, writes=[wr])
            sc.op("act", lambda e: e.activation(out=dst, in_=dst, func=AF.Sqrt), reads=[wr], writes=[wr])
            sc.op("dve", lambda e: e.reciprocal(out=dst, in_=dst), reads=[wr], writes=[wr])

        def transposes(src_fn, n, rows, pbank, rd):
            ptb = ps[pbank][:, :].bitcast(BF16)
            for k in range(n):
                sc.op("pe", lambda e, k=k: e.transpose(ptb[0:rows, k * 128:(k + 1) * 128], src_fn(k), ident[:, :]),
                      reads=rd + ["ident"], writes=[("ps", pbank)])
            return ptb

        ntile = NT // 128
        def front(ti):
                r0 = ti * 128
                par = ti % 2
                z_ = zz[par]
                x_ = xt[ti % 2]
                xk = ("x", ti % 2)
                sc.dma("sp", x_[:], d_x[r0:r0 + 128, :], xk, writes=[xk])
                sc.op("act", lambda e, x_=x_: e.activation(out=junkb[:, :], in_=x_[:, :], func=AF.Square, accum_out=st[:, 60 + 2 * par:61 + 2 * par]),
                      reads=[xk], writes=["junkb", ("ss", par)])
                rs_of(st[:, 60 + 2 * par:61 + 2 * par], st[:, 61 + 2 * par:62 + 2 * par], 1024.0, [("ss", par)], ("rstd", par))
                sc.op("dve", lambda e, x_=x_: e.tensor_tensor(out=xg[:, :], in0=x_[:, :], in1=gmix[:, :], op=ALU.mult),
                      reads=[xk, "gmix"], writes=["xg"])
                ptb = transposes(lambda k: xg[:, k * 128:(k + 1) * 128], 8, 128, 0, ["xg"])
                sc.op("act", lambda e, ptb=ptb: e.activation(out=xT[:, :], in_=ptb[:, 0:1024], func=AF.Copy),
                      reads=[("ps", 0)], writes=["xT"])
                ncol = (IN_COLS + 511) // 512
                for ct in range(ncol):
                    c0 = ct * 512
                    w = min(512, IN_COLS - c0)
                    pb = 1 + (ct % 3)
                    for kc in range(8):
                        sc.op("pe", lambda e, pb=pb, kc=kc, c0=c0, w=w: e.matmul(
                            ps[pb][:, 0:w], xT[:, kc * 128:(kc + 1) * 128], win[:, kc, c0:c0 + w], start=(kc == 0), stop=(kc == 7)),
                            reads=["xT", ("win", kc)], writes=[("ps", pb)])
                    if ct % 2 == 0:
                        sc.op("act", lambda e, pb=pb, c0=c0, w=w: e.activation(out=z_[:, c0:c0 + w], in_=ps[pb][:, 0:w], func=AF.Copy,
                                                                               scale=st[:, 61 + 2 * par:62 + 2 * par]),
                              reads=[("ps", pb), ("rstd", par)], writes=[("z", par, ct)])
                    else:
                        sc.op("dve", lambda e, pb=pb, c0=c0, w=w: e.tensor_scalar(out=z_[:, c0:c0 + w], in0=ps[pb][:, 0:w], scalar1=st[:, 61 + 2 * par:62 + 2 * par],
                                                                                 scalar2=None, op0=ALU.mult),
                              reads=[("ps", pb), ("rstd", par)], writes=[("z", par, ct)])

        def back(ti):
                r0 = ti * 128
                par = ti % 2
                z_ = zz[par]
                sc.dma("sp", cs[:, 0:16], d_cos[r0:r0 + 128, :], "cs", writes=["cos"])
                sc.dma("sp", cs[:, 16:32], d_sin[r0:r0 + 128, :], "cs", writes=["sin"])
                zr = lambda a, b: [("z", par, c) for c in range(a // 512, (b - 1) // 512 + 1)]
                sc.op("act", lambda e: e.activation(out=tmpf[:, 0:1024], in_=z_[:, 0:1024], func=AF.Square),
                      reads=zr(0, 1024), writes=["tmpf"])
                sc.op("dve", lambda e: e.tensor_reduce(out=st[:, 8:24], in_=tmpf[:, 0:1024].rearrange("p (h d) -> p h d", d=64),
                                                       axis=AX.X, op=ALU.add), reads=["tmpf"], writes=["ss16"])
                rs_of(st[:, 8:24], st[:, 24:40], 64.0, ["ss16"], "rs16")
                sc.op("dve", lambda e: e.tensor_tensor(out=tmpf[:, 0:1024].rearrange("p (h d) -> p h d", d=64),
                                                       in0=z_[:, 0:1024].rearrange("p (h d) -> p h d", d=64),
                                                       in1=st[:, 24:40].unsqueeze(2).to_broadcast([128, 16, 64]), op=ALU.mult),
                      reads=zr(0, 1024) + ["rs16", "tmpf"], writes=["tmpf"])
                sc.op("dve", lambda e: e.tensor_tensor(out=qk_b[:, :], in0=tmpf[:, 0:1024], in1=gqk[:, :], op=ALU.mult),
                      reads=["tmpf", "gqk"], writes=["qk_b"])
                for half in range(2):
                    ptb = transposes(lambda k, half=half: qk_b[:, (half * 8 + k) * 64:(half * 8 + k + 1) * 64], 8, 64, 4 + half, ["qk_b"])
                    sc.op("act", lambda e, ptb=ptb, half=half: e.activation(
                        out=qkT[:, half * 8:(half + 1) * 8, :], in_=ptb[0:64, 0:1024].rearrange("p (h t) -> p h t", h=8), func=AF.Copy),
                        reads=[("ps", 4 + half)], writes=[("qkT", half)])
                    if half == 0:
                        dst = o_qaT[:, r0:r0 + 128].rearrange("(h d) t -> d h t", d=64)
                    else:
                        dst = o_kaT[512 * (r0 // 512):512 * (r0 // 512) + 512, r0 % 512:r0 % 512 + 128].rearrange("(h d) t -> d h t", d=64)
                    sc.dma("sp", dst, qkT[:, half * 8:(half + 1) * 8, :], ("oqk", half), reads=[("qkT", half)],
                           writes=([("dr", "kaT", r0 // 512)] if half == 1 else []))
                sc.op("act", lambda e: e.activation(out=va_sb[:, :, 0:64], in_=z_[:, 1024:1536].rearrange("p (h d) -> p h d", d=64), func=AF.Copy),
                      reads=zr(1024, 1536) + ["va"], writes=["va"])
                sc.dma("sp", o_va[r0:r0 + 128, :], va_sb[:].rearrange("p h d -> p (h d)"), "ova", reads=["va"], writes=[("dr", "va", r0 // 512)])
                sc.op("act", lambda e: e.activation(out=qi_b[:, :], in_=z_[:, 1536:2048], func=AF.Copy), reads=zr(1536, 2048), writes=["qi_b"])
                ptb = transposes(lambda k: qi_b[:, k * 64:(k + 1) * 64], 8, 64, 6, ["qi_b"])
                sc.op("act", lambda e, ptb=ptb: e.activation(out=qiT_sb[:, :].rearrange("p (t h) -> p h t", h=8),
                                                            in_=ptb[0:64, 0:1024].rearrange("p (h t) -> p h t", h=8), func=AF.Copy),
                      reads=[("ps", 6)], writes=["qiT"])
                sc.dma("sp", o_qiT[:, r0 * 8:(r0 + 128) * 8], qiT_sb[:, :], "oqi", reads=["qiT"])
                sc.op("act", lambda e: e.activation(out=ki_b[:, :], in_=z_[:, 2048:2112], func=AF.Copy), reads=zr(2048, 2112), writes=["ki_b"])
                ptb = transposes(lambda k: ki_b[:, :], 1, 64, 7, ["ki_b"])
                sc.op("act", lambda e, ptb=ptb: e.activation(out=kiT_sb[:, :], in_=ptb[0:64, 0:128], func=AF.Copy),
                      reads=[("ps", 7)], writes=["kiT"])
                sc.dma("sp", o_kiT[64 * (r0 // 512):64 * (r0 // 512) + 64, r0 % 512:r0 % 512 + 128], kiT_sb[:, :], "oki", reads=["kiT"],
                       writes=[("dr", "kiT", r0 // 512)])
                sc.op("dve", lambda e: e.tensor_scalar(out=w2_sb[:, :], in0=z_[:, 2112:2120], scalar1=(8 ** -0.5) * (64 ** -0.5), scalar2=None,
                                                       op0=ALU.mult), reads=zr(2112, 2120), writes=["w2"])
                sc.dma("sp", o_w2[r0:r0 + 128, :], w2_sb[:, :], "ow2", reads=["w2"])
                sc.op("act", lambda e: e.activation(out=junkb[:, 0:384], in_=z_[:, 2120:2504], func=AF.Square, accum_out=st[:, 2:3]),
                      reads=zr(2120, 2504) + ["junkb"], writes=["junkb", "sscq"])
                rs_of(st[:, 2:3], st[:, 3:4], 384.0, ["sscq"], "rscq")
                sc.op("dve", lambda e: e.scalar_tensor_tensor(out=cq_b[:, :], in0=z_[:, 2120:2504], scalar=st[:, 3:4], in1=gcq[:, :],
                                                              op0=ALU.mult, op1=ALU.mult), reads=zr(2120, 2504) + ["rscq", "gcq"], writes=["cq_b"])
                ptb = transposes(lambda k: cq_b[:, k * 128:(k + 1) * 128], 3, 128, 4, ["cq_b"])
                sc.op("act", lambda e, ptb=ptb: e.activation(out=cqT[:, :, :], in_=ptb[:, 0:384].rearrange("p (k t) -> p k t", k=3), func=AF.Copy),
                      reads=[("ps", 4)], writes=["cqT"])
                for (c0, w, pb) in ((0, 512, 5), (512, 256, 6)):
                    for kc in range(3):
                        sc.op("pe", lambda e, pb=pb, kc=kc, c0=c0, w=w: e.matmul(ps[pb][:, 0:w], cqT[:, kc, :], wuq[:, kc, c0:c0 + w],
                                                                                 start=(kc == 0), stop=(kc == 2)),
                              reads=["cqT", "wuq"], writes=[("ps", pb)])
                    sc.op("act", lambda e, pb=pb, c0=c0, w=w: e.activation(out=qb_f[:].rearrange("p h d -> p (h d)")[:, c0:c0 + w],
                                                                           in_=ps[pb][:, 0:w], func=AF.Copy),
                          reads=[("ps", pb)], writes=[("qb_f", c0)])
                sc.op("act", lambda e: e.activation(out=junkb[:, 0:256], in_=z_[:, 2504:2760], func=AF.Square, accum_out=st[:, 4:5]),
                      reads=zr(2504, 2760) + ["junkb"], writes=["junkb", "ssckv"])
                rs_of(st[:, 4:5], st[:, 5:6], 256.0, ["ssckv"], "rsckv")
                sc.op("dve", lambda e: e.scalar_tensor_tensor(out=ckv_b[:, :], in0=z_[:, 2504:2760], scalar=st[:, 5:6], in1=gckv[:, :],
                                                              op0=ALU.mult, op1=ALU.mult), reads=zr(2504, 2760) + ["rsckv", "gckv"], writes=["ckv_b"])
                ptb = transposes(lambda k: ckv_b[:, k * 128:(k + 1) * 128], 2, 128, 7, ["ckv_b"])
                sc.op("act", lambda e, ptb=ptb: e.activation(out=ckvT[:, :, :], in_=ptb[:, 0:256].rearrange("p (k t) -> p k t", k=2), func=AF.Copy),
                      reads=[("ps", 7)], writes=["ckvT"])
                for (c0, pb) in ((0, 4), (512, 5)):
                    for kc in range(2):
                        sc.op("pe", lambda e, pb=pb, kc=kc, c0=c0: e.matmul(ps[pb][:, 0:512], ckvT[:, kc, :], wukv[:, kc, c0:c0 + 512],
                                                                            start=(kc == 0), stop=(kc == 1)),
                              reads=["ckvT", "wukv"], writes=[("ps", pb)])
                    sc.op("act", lambda e, pb=pb, c0=c0: e.activation(out=kv_f[:].rearrange("p h d -> p (h d)")[:, c0:c0 + 512],
                                                                      in_=ps[pb][:, 0:512], func=AF.Copy),
                          reads=[("ps", pb)], writes=[("kv_f", c0)])
                cosb = cs[:, 0:16].unsqueeze(1).to_broadcast([128, 8, 16])
                sinb = cs[:, 16:32].unsqueeze(1).to_broadcast([128, 8, 16])
                qbr = [("qb_f", 0), ("qb_f", 512)]
                x1 = qb_f[:, :, 64:80]
                x2 = qb_f[:, :, 80:96]
                for k_, (a_, b_) in enumerate(((x1, cosb), (x2, sinb), (x1, sinb), (x2, cosb))):
                    sc.op("dve", lambda e, k_=k_, a_=a_, b_=b_: e.tensor_tensor(out=rt[:, :, k_ * 16:(k_ + 1) * 16], in0=a_, in1=b_, op=ALU.mult),
                          reads=qbr + ["cos", "sin"], writes=[("rt", k_)])
                sc.op("dve", lambda e: e.tensor_tensor(out=qb_f[:, :, 64:80], in0=rt[:, :, 0:16], in1=rt[:, :, 16:32], op=ALU.subtract),
                      reads=[("rt", 0), ("rt", 1)] + qbr, writes=qbr)
                sc.op("dve", lambda e: e.tensor_tensor(out=qb_f[:, :, 80:96], in0=rt[:, :, 32:48], in1=rt[:, :, 48:64], op=ALU.add),
                      reads=[("rt", 2), ("rt", 3)] + qbr, writes=qbr)
                k1 = z_[:, 2760:2776]
                k2 = z_[:, 2776:2792]
                for k_, (a_, b_) in enumerate(((k1, cs[:, 0:16]), (k2, cs[:, 16:32]), (k1, cs[:, 16:32]), (k2, cs[:, 0:16]))):
                    sc.op("dve", lambda e, k_=k_, a_=a_, b_=b_: e.tensor_tensor(out=rt[:, 0, k_ * 16:(k_ + 1) * 16], in0=a_, in1=b_, op=ALU.mult),
                          reads=zr(2760, 2792) + ["cos", "sin"] + [("rt", q) for q in range(4)], writes=[("rt", k_)])
                sc.op("dve", lambda e: e.tensor_tensor(out=krr[:, 0:16], in0=rt[:, 0, 0:16], in1=rt[:, 0, 16:32], op=ALU.subtract),
                      reads=[("rt", 0), ("rt", 1)], writes=["krr"])
                sc.op("dve", lambda e: e.tensor_tensor(out=krr[:, 16:32], in0=rt[:, 0, 32:48], in1=rt[:, 0, 48:64], op=ALU.add),
                      reads=[("rt", 2), ("rt", 3), "krr"], writes=["krr"])
                kvr = [("kv_f", 0), ("kv_f", 512)]
                sc.op("act", lambda e: e.activation(out=kb_f[:, :, 0:64], in_=kv_f[:, :, 0:64], func=AF.Copy), reads=kvr, writes=["kb_f"])
                sc.op("dve", lambda e: e.tensor_copy(out=kb_f[:, :, 64:96], in_=krr[:, :].unsqueeze(1).to_broadcast([128, 8, 32])),
                      reads=["krr", "kb_f"], writes=["kb_f"])
                sc.op("act", lambda e: e.activation(out=vb_sb[:, :, 0:64], in_=kv_f[:, :, 64:128], func=AF.Copy), reads=kvr + ["vb"], writes=["vb"])
                sc.dma("sp", o_vb[r0:r0 + 128, :], vb_sb[:].rearrange("p h d -> p (h d)"), "ovb", reads=["vb"], writes=[("dr", "vb", r0 // 512)])
                for nm, src, rdk, gt, dstb, dstT, outd, pbk in (("q", qb_f, qbr, gbq, qb_b, qbT_sb, o_qbT, 6), ("k", kb_f, ["kb_f"], gbk, kb_b, kbT_sb, o_kbT, 7)):
                    flat = src[:].rearrange("p h d -> p (h d)")
                    sc.op("act", lambda e, flat=flat: e.activation(out=tmpf[:, 0:768], in_=flat, func=AF.Square), reads=rdk + ["tmpf"], writes=["tmpf"])
                    sc.op("dve", lambda e: e.tensor_reduce(out=st[:, 40:48], in_=tmpf[:, 0:768].rearrange("p (h d) -> p h d", d=96),
                                                           axis=AX.X, op=ALU.add), reads=["tmpf"], writes=["ss8"])
                    rs_of(st[:, 40:48], st[:, 48:56], 96.0, ["ss8"], "rs8")
                    sc.op("dve", lambda e, src=src: e.tensor_tensor(out=tmpf[:, 0:768].rearrange("p (h d) -> p h d", d=96), in0=src[:, :, :],
                                                                   in1=st[:, 48:56].unsqueeze(2).to_broadcast([128, 8, 96]), op=ALU.mult),
                          reads=rdk + ["rs8", "tmpf"], writes=["tmpf"])
                    sc.op("dve", lambda e, dstb=dstb, gt=gt: e.tensor_tensor(out=dstb[:].rearrange("p h d -> p (h d)"), in0=tmpf[:, 0:768], in1=gt[:, :],
                                                                            op=ALU.mult), reads=["tmpf", "gb" + nm], writes=["b_" + nm])
                    ptb = transposes(lambda k, dstb=dstb: dstb[:, k, :], 8, 96, pbk, ["b_" + nm])
                    sc.op("act", lambda e, ptb=ptb, dstT=dstT: e.activation(out=dstT[:, :, :], in_=ptb[0:96, 0:1024].rearrange("p (h t) -> p h t", h=8),
                                                                           func=AF.Copy), reads=[("ps", pbk)], writes=["T_" + nm])
                    if nm == "q":
                        dstd = outd[:, r0:r0 + 128]
                    else:
                        dstd = outd[768 * (r0 // 512):768 * (r0 // 512) + 768, r0 % 512:r0 % 512 + 128]
                    sc.dma("sp", dstd.rearrange("(h d) t -> d h t", d=96), dstT[:, :, :], "oT" + nm, reads=["T_" + nm],
                           writes=([("dr", "kbT", r0 // 512)] if nm == "k" else []))
                for gh in range(2):
                    sc.op("dve", lambda e, gh=gh: e.tensor_tensor(out=tmpf[:, :], in0=z_[:, 2792 + 1024 * gh:2792 + 1024 * (gh + 1)],
                                                                 in1=bgate[:, 1024 * gh:1024 * (gh + 1)], op=ALU.add),
                          reads=zr(2792, 4840) + ["bgate", "tmpf"], writes=["tmpf"])
                    sc.op("act", lambda e: e.activation(out=gate_sb[:, :], in_=tmpf[:, :], func=AF.Sigmoid), reads=["tmpf"], writes=["gate"])
                    sc.dma("sp", o_gate[r0:r0 + 128, 1024 * gh:1024 * (gh + 1)], gate_sb[:, :], "ogate", reads=["gate"])
                if env is not None and getattr(env, "hook_sb", None) and ti % 4 == 3:
                    env.hook_sb(ti // 4)


        front(0)
        for ti in range(ntile):
            fr = sc.record(lambda: front(ti + 1)) if ti + 1 < ntile else []
            bk = sc.record(lambda: back(ti))
            sc.interleave(fr, bk)
        sc.emit(skip_cc=bool(env is not None and getattr(env, "skip_cc", False)))
        print("A program: ops", sc.nops, "waits", sc.nwaits)
    return nc


def rope_tables(pos):
    inv = (np.float32(10000.0) ** (-np.arange(16, dtype=np.float32) / np.float32(16))).astype(np.float32)
    ang = pos.astype(np.float32)[:, None] * inv[None, :]
    return np.cos(ang).astype(np.float32), np.sin(ang).astype(np.float32)


def rep(v, n=128):
    return np.ascontiguousarray(np.broadcast_to(np.asarray(v, np.float32)[None, :], (n, v.shape[0])))


def shard_A(x, P, l, NI):
    maps = []
    ident = np.eye(128, dtype=np.float32).astype(NPBF)
    for c in range(NCORE):
        b, j = c // 4, c % 4
        tok = core_tokens(NI, j)
        cos, sin = rope_tables(tok)
        m = dict(x=np.ascontiguousarray(x[b][tok]), w_in=P["w_in"][l], w_uq=P["b_w_uq"][l], w_ukv=P["b_w_ukv"][l],
                 gmix=rep(P["norm_mix"][l]),
                 gqk=rep(np.concatenate([np.tile(P["a_q_norm"][l], 8), np.tile(P["a_k_norm"][l], 8)])),
                 gcq=rep(P["b_cq_norm"][l]), gckv=rep(P["b_ckv_norm"][l]),
                 gbq=rep(np.tile(P["b_q_norm"][l], 8)), gbk=rep(np.tile(P["b_k_norm"][l], 8)),
                 bgate=rep(P["b_gate"][l]), cos=cos, sin=sin, ident=ident)
        maps.append(m)
    return maps


def gather_A(results, S, NI):
    B = 2
    G = dict(qiT=np.zeros((B, 64, S * 8), NPBF), w2=np.zeros((B, S, 8), np.float32), qaT=np.zeros((B, 8, 64, S), NPBF),
             qbT=np.zeros((B, 8, 96, S), NPBF), kiT=np.zeros((B, 64, S), NPBF), kaT=np.zeros((B, 8, 64, S), NPBF),
             va=np.zeros((B, S, 520), NPBF), kbT=np.zeros((B, 8, 96, S), NPBF), vb=np.zeros((B, S, 520), NPBF),
             gate=np.zeros((B, S, 2048), np.float32))
    for c in range(NCORE):
        b, j = c // 4, c % 4
        tok = core_tokens(NI, j)
        r = results[c]
        G["qiT"][b].reshape(64, S, 8)[:, tok, :] = r["qiT"].reshape(64, len(tok), 8)
        G["w2"][b][tok] = r["w2"]
        for k in ("qaT", "qbT"):
            G[k][b][:, :, tok] = r[k].reshape(8, -1, len(tok))
        for k, dh in (("kaT", 64), ("kbT", 96)):
            G[k][b][:, :, tok] = r[k].reshape(NI, 8, dh, SB).transpose(1, 2, 0, 3).reshape(8, dh, len(tok))
        G["kiT"][b][:, tok] = r["kiT"].reshape(NI, 64, SB).transpose(1, 0, 2).reshape(64, len(tok))
        for k in ("va", "vb", "gate"):
            G[k][b][tok] = r[k]
    return G


def build_C1(NT, env=None):
    nc = env.nc if env else bass.Bass("TRN2", target_bir_lowering=False)
    es = ExitStack()
    with es:
        cx = Ctx(nc, es, env)
        sc = env.sc if env else Sched(nc, es)
        d_x = cx.din("x", [NT, D], F32)
        d_yaT = cx.din("yaT", [512, NT], BF16)
        d_ybT = cx.din("ybT", [512, NT], BF16)
        d_gate = cx.din("gate", [NT, 2048], F32)
        d_wpa = cx.din("w_pa", [512, D], F32)
        d_wpb = cx.din("w_pb", [512, D], F32)
        d_wout = cx.din("w_out", [D, D], F32)
        d_ident = cx.din("ident", [128, 128], BF16)
        o_x = cx.dout("xmid", [NT, D], F32)
        wpa = cx.sb([64, 8, D], BF16)
        wpb = cx.sb([64, 8, D], BF16)
        wout = cx.sb([128, 8, D], BF16)
        ident = cx.sb([128, 128], BF16)
        sc.dma(wq(d_wpa), wpa[:], d_wpa.rearrange("(h d) c -> d h c", d=64), "wpa", writes=["wpa"])
        sc.dma(wq(d_wpb), wpb[:], d_wpb.rearrange("(h d) c -> d h c", d=64), "wpb", writes=["wpb"])
        sc.dma(wq(d_wout), wout[:], d_wout.rearrange("(k p) c -> p k c", p=128), "wout", writes=["wout"])
        sc.dma("sp", ident[:], d_ident, "ident", writes=["ident"])
        yT = [[cx.sb([64, 8, 128], BF16) for _ in range(2)] for _ in range(2)]
        gt = [cx.sb([128, 2048], F32) for _ in range(2)]
        xt = [cx.sb([128, D], F32) for _ in range(2)]
        t1 = cx.sb([128, D], F32)
        t2 = cx.sb([128, D], F32)
        mb = cx.sb([128, D], BF16)
        mT = cx.sb([128, D], BF16)
        xo = [cx.sb([128, D], F32) for _ in range(2)]
        ps = [cx.ps([128, 512], F32) for _ in range(8)]
        for ti in range(NT // 128):
            r0 = ti * 128
            s = ti % 2
            sc.dma("sp", yT[0][s][:], d_yaT[:, r0:r0 + 128].rearrange("(h d) t -> d h t", d=64), ("ld", s), writes=[("ya", s)])
            sc.dma("sp", yT[1][s][:], d_ybT[:, r0:r0 + 128].rearrange("(h d) t -> d h t", d=64), ("ld", s), writes=[("yb", s)])
            sc.dma("sp", gt[s][:], d_gate[r0:r0 + 128, :], ("ld", s), writes=[("g", s)])
            sc.dma("sp", xt[s][:], d_x[r0:r0 + 128, :], ("ld", s), writes=[("x", s)])
            for br, (w_, wn, yk) in enumerate(((wpa, "wpa", "ya"), (wpb, "wpb", "yb"))):
                for ct in range(2):
                    pb = br * 2 + ct
                    for h in range(8):
                        sc.op("pe", lambda e, pb=pb, h=h, w_=w_, br=br, ct=ct, s=s: e.matmul(
                            ps[pb][:, :], yT[br][s][:, h, :], w_[:, h, ct * 512:(ct + 1) * 512], start=(h == 0), stop=(h == 7)),
                            reads=[(yk, s), wn], writes=[("ps", pb)])
            for ct in range(2):
                cs_ = slice(ct * 512, (ct + 1) * 512)
                sc.op("dve", lambda e, ct=ct, cs_=cs_, s=s: e.tensor_tensor(out=t1[:, cs_], in0=ps[ct][:, :], in1=gt[s][:, cs_], op=ALU.mult),
                      reads=[("ps", ct), ("g", s)], writes=[("t1", ct)])
                sc.op("dve", lambda e, ct=ct, cs_=cs_, s=s: e.tensor_tensor(out=t2[:, cs_], in0=ps[2 + ct][:, :],
                                                                           in1=gt[s][:, 1024 + ct * 512:1024 + (ct + 1) * 512], op=ALU.mult),
                      reads=[("ps", 2 + ct), ("g", s)], writes=[("t2", ct)])
                sc.op("dve", lambda e, cs_=cs_: e.tensor_tensor(out=mb[:, cs_], in0=t1[:, cs_], in1=t2[:, cs_], op=ALU.add),
                      reads=[("t1", ct), ("t2", ct)], writes=[("mb", ct)])
            ptb = ps[4][:, :].bitcast(BF16)
            for k in range(8):
                sc.op("pe", lambda e, k=k, ptb=ptb: e.transpose(ptb[:, k * 128:(k + 1) * 128], mb[:, k * 128:(k + 1) * 128], ident[:, :]),
                      reads=[("mb", k // 4), "ident"], writes=[("ps", 4)])
            sc.op("act", lambda e, ptb=ptb: e.activation(out=mT[:, :], in_=ptb[:, 0:1024], func=AF.Copy), reads=[("ps", 4)], writes=["mT"])
            for ct in range(2):
                pb = 5 + ct
                for kc in range(8):
                    sc.op("pe", lambda e, pb=pb, kc=kc, ct=ct: e.matmul(ps[pb][:, :], mT[:, kc * 128:(kc + 1) * 128],
                                                                       wout[:, kc, ct * 512:(ct + 1) * 512], start=(kc == 0), stop=(kc == 7)),
                          reads=["mT", "wout"], writes=[("ps", pb)])
                sc.op("dve", lambda e, pb=pb, ct=ct, s=s: e.tensor_tensor(out=xo[s][:, ct * 512:(ct + 1) * 512], in0=ps[pb][:, :],
                                                                         in1=xt[s][:, ct * 512:(ct + 1) * 512], op=ALU.add),
                      reads=[("ps", pb), ("x", s)], writes=[("xo", s, ct)])
            sc.dma("sp", o_x[r0:r0 + 128, :], xo[s][:, :], ("st", s), reads=[("xo", s, 0), ("xo", s, 1)])
        sc.emit()
        print("C1 program: ops", sc.nops, "waits", sc.nwaits)
    return nc


def build_C2(NI, env=None):
    NT = NI * SB
    NCT = 2 * D_FF // 128
    NV = D_FF // 128
    nc = env.nc if env else bass.Bass("TRN2", target_bir_lowering=False)
    es = ExitStack()
    with es:
        cx = Ctx(nc, es, env)
        sc = env.sc if env else Sched(nc, es)
        d_x = cx.din("xmid", [NT, D], F32)
        d_halo = cx.din("halo", [NI, 2, D], F32)
        d_g = cx.din("gffn", [128, D], F32)
        d_wup = cx.din("w_up", [D, 2 * D_FF], F32)
        d_wdn = cx.din("w_down", [D_FF, D], F32)
        d_cw = cx.din("convw", [128, NCT, 3], F32)
        d_cb = cx.din("convb", [128, NCT], F32)
        d_ident = cx.din("ident", [128, 128], BF16)
        o_x = cx.dout("xout", [NT, D], F32)
        gffn = cx.sb([128, D], F32)
        wdn = cx.sb([128, NV, D], BF16)
        cw = cx.sb([128, NCT, 3], F32)
        cb = cx.sb([128, NCT], F32)
        ident = cx.sb([128, 128], BF16)
        sc.dma("sp", gffn[:], d_g, "gffn", writes=["gffn"])
        sc.dma("sp", cw[:], d_cw, "cw", writes=["cw"])
        sc.dma("sp", cb[:], d_cb, "cb", writes=["cb"])
        sc.dma("sp", ident[:], d_ident, "ident", writes=["ident"])
        for v in range(NV):
            sc.dma(wq(d_wdn), wdn[:, v, :], d_wdn[v * 128:(v + 1) * 128, :], ("wdn", v % 2), writes=[("wdn", v)])
        xt = [cx.sb([128, D], F32) for _ in range(4)]
        xh = cx.sb([2, D], F32)
        junkb = cx.sb([128, D], BF16)
        hb = cx.sb([128, D], BF16)
        h2T = cx.sb([128, 8, 514], BF16)
        wup = [[cx.sb([128, 8, 128], BF16) for _ in range(2)] for _ in range(2)]
        u2 = [[cx.sb([128, 514], F32) for _ in range(2)] for _ in range(2)]
        acc2 = [[cx.sb([128, 512], F32) for _ in range(2)] for _ in range(2)]
        sg2 = [cx.sb([128, 512], F32) for _ in range(2)]
        aT = cx.sb([128, NV, 512], BF16)
        xo = [cx.sb([128, D], F32) for _ in range(2)]
        st = cx.sb([128, 8], F32)
        ps = [cx.ps([128, 512], F32) for _ in range(8)]
        wslot = 0
        for i in range(NI):
            tok0 = i * SB
            sc.dma("sp", xh[:, :], d_halo[i, :, :], "xh", writes=["xh"])
            tiles = [("h", xh, 2, 0)] + [(k, xt[k], 128, 2 + 128 * k) for k in range(4)]
            for (tk, xx, np_, c0) in tiles:
                key = ("xt", tk)
                if tk != "h":
                    sc.dma("sp", xx[:, :], d_x[tok0 + tk * 128:tok0 + (tk + 1) * 128, :], key, writes=[key])
                else:
                    key = "xh"
                sc.op("act", lambda e, xx=xx, np_=np_: e.activation(out=junkb[0:np_, :], in_=xx[0:np_, :], func=AF.Square, accum_out=st[0:np_, 0:1]),
                      reads=[key, "junkb"], writes=["junkb", "ss"])
                sc.op("dve", lambda e, np_=np_: e.tensor_scalar(out=st[0:np_, 1:2], in0=st[0:np_, 0:1], scalar1=1.0 / D, scalar2=EPS,
                                                               op0=ALU.mult, op1=ALU.add), reads=["ss"], writes=["rs"])
                sc.op("act", lambda e, np_=np_: e.activation(out=st[0:np_, 1:2], in_=st[0:np_, 1:2], func=AF.Sqrt), reads=["rs"], writes=["rs"])
                sc.op("dve", lambda e, np_=np_: e.reciprocal(out=st[0:np_, 1:2], in_=st[0:np_, 1:2]), reads=["rs"], writes=["rs"])
                sc.op("dve", lambda e, xx=xx, np_=np_: e.scalar_tensor_tensor(out=hb[0:np_, :], in0=xx[0:np_, :], scalar=st[0:np_, 1:2],
                                                                             in1=gffn[0:np_, :], op0=ALU.mult, op1=ALU.mult),
                      reads=[key, "rs", "gffn"], writes=["hb"])
                ptb = ps[7][:, :].bitcast(BF16)
                for k in range(8):
                    sc.op("pe", lambda e, k=k, ptb=ptb, np_=np_: e.transpose(ptb[:, k * 128:k * 128 + np_], hb[0:np_, k * 128:(k + 1) * 128],
                                                                           ident[0:np_, 0:np_]),
                          reads=["hb", "ident"], writes=[("ps", 7)])
                sc.op("act", lambda e, ptb=ptb, np_=np_, c0=c0: e.activation(
                    out=h2T[:, :, c0:c0 + np_], in_=ptb[:, 0:1024].rearrange("p (k t) -> p k t", k=8)[:, :, 0:np_], func=AF.Copy),
                    reads=[("ps", 7)], writes=[("h2T", tk)])
            h2r = [("h2T", tk) for tk in ("h", 0, 1, 2, 3)]
            for v in range(NV):
                ws = wslot % 2
                wslot += 1
                par = v % 2
                u = [(par, 0), (par, 1)]
                for gi, ct in enumerate((v, NV + v)):
                    sc.dma(wq(d_wup), wup[ws][gi][:], d_wup[:, ct * 128:(ct + 1) * 128].rearrange("(k p) c -> p k c", p=128),
                           ("wup", ws, gi), writes=[("wup", ws, gi)])
                for gi, ct in enumerate((v, NV + v)):
                    pm = 3 * par + gi
                    ph = 3 * par + 2
                    hc = 2 * gi
                    ug = u2[par][gi]
                    ag = acc2[par][gi]
                    for kc in range(8):
                        sc.op("pe", lambda e, pm=pm, kc=kc, ws=ws, gi=gi: e.matmul(ps[pm][:, :], wup[ws][gi][:, kc, :], h2T[:, kc, 2:514],
                                                                                  start=(kc == 0), stop=(kc == 7)),
                              reads=[("wup", ws, gi)] + h2r, writes=[("ps", pm)])
                    for kc in range(8):
                        sc.op("pe", lambda e, ph=ph, kc=kc, ws=ws, gi=gi, hc=hc: e.matmul(ps[ph][:, hc:hc + 2], wup[ws][gi][:, kc, :], h2T[:, kc, 0:2],
                                                                                  start=(kc == 0), stop=(kc == 7)),
                              reads=[("wup", ws, gi)] + h2r, writes=[("ps", ph)])
                    uk = ("u", par, gi)
                    ak = ("acc", par, gi)
                    sc.op("act", lambda e, pm=pm, ug=ug: e.activation(out=ug[:, 2:514], in_=ps[pm][:, :], func=AF.Copy),
                          reads=[("ps", pm)], writes=[uk])
                    sc.op("act", lambda e, ph=ph, ug=ug, hc=hc: e.activation(out=ug[:, 0:2], in_=ps[ph][:, hc:hc + 2], func=AF.Copy),
                          reads=[("ps", ph), uk], writes=[uk])
                    sc.op("dve", lambda e, ug=ug, ag=ag, ct=ct: e.tensor_scalar(out=ag[:, :], in0=ug[:, 2:514], scalar1=cw[:, ct, 2:3],
                                                                               scalar2=cb[:, ct:ct + 1], op0=ALU.mult, op1=ALU.add),
                          reads=[uk, "cw", "cb"], writes=[ak])
                    sc.op("dve", lambda e, ug=ug, ag=ag, ct=ct: e.scalar_tensor_tensor(out=ag[:, :], in0=ug[:, 1:513], scalar=cw[:, ct, 1:2],
                                                                                      in1=ag[:, :], op0=ALU.mult, op1=ALU.add),
                          reads=[uk, "cw", ak], writes=[ak])
                    sc.op("dve", lambda e, ug=ug, ag=ag, ct=ct: e.scalar_tensor_tensor(out=ag[:, :], in0=ug[:, 0:512], scalar=cw[:, ct, 0:1],
                                                                                      in1=ag[:, :], op0=ALU.mult, op1=ALU.add),
                          reads=[uk, "cw", ak], writes=[ak])
                sc.op("act", lambda e, par=par: e.activation(out=sg2[par][:, :], in_=acc2[par][1][:, :], func=AF.Silu),
                      reads=[("acc", par, 1)], writes=[("sg", par)])
                sc.op("pool", lambda e, v=v, par=par: e.tensor_tensor(out=aT[:, v, :], in0=sg2[par][:, :], in1=acc2[par][0][:, :], op=ALU.mult),
                      reads=[("sg", par), ("acc", par, 0)], writes=[("aT", v)])
            for tk in range(4):
                s = tk % 2
                for ct in range(2):
                    pb = 6 + ct
                    for v in range(NV):
                        sc.op("pe", lambda e, pb=pb, v=v, tk=tk, ct=ct: e.matmul(ps[pb][:, :], aT[:, v, tk * 128:(tk + 1) * 128],
                                                                                wdn[:, v, ct * 512:(ct + 1) * 512], start=(v == 0), stop=(v == NV - 1)),
                              reads=[("aT", v), ("wdn", v)], writes=[("ps", pb)])
                    sc.op("dve", lambda e, pb=pb, ct=ct, tk=tk, s=s: e.tensor_tensor(out=xo[s][:, ct * 512:(ct + 1) * 512], in0=ps[pb][:, :],
                                                                                   in1=xt[tk][:, ct * 512:(ct + 1) * 512], op=ALU.add),
                          reads=[("ps", pb), ("xt", tk)], writes=[("xo", s, ct)])
                sc.dma("sp", o_x[tok0 + tk * 128:tok0 + (tk + 1) * 128, :], xo[s][:, :], ("st", s), reads=[("xo", s, 0), ("xo", s, 1)])
        sc.emit()
        print("C2 program: ops", sc.nops, "waits", sc.nwaits)
    return nc


def maps_C1(xloc, rB, rA, P, l):
    ident = np.eye(128, dtype=np.float32).astype(NPBF)
    return [dict(x=xloc[c], yaT=rB[c]["yaT"], ybT=rB[c]["ybT"], gate=rA[c]["gate"],
                 w_pa=P["w_proj_a"][l], w_pb=P["w_proj_b"][l], w_out=P["w_out"][l], ident=ident) for c in range(NCORE)]


def shard_C1(x, yaT, ybT, gate, P, l, NI):
    xloc, rB, rA = [], [], []
    for c in range(NCORE):
        b, j = c // 4, c % 4
        tok = core_tokens(NI, j)
        xloc.append(np.ascontiguousarray(x[b][tok]))
        rB.append(dict(yaT=np.ascontiguousarray(yaT[b][:, :, tok].reshape(512, -1)), ybT=np.ascontiguousarray(ybT[b][:, :, tok].reshape(512, -1))))
        rA.append(dict(gate=np.ascontiguousarray(gate[b][tok])))
    return maps_C1(xloc, rB, rA, P, l)


def shard_C2(xmid, P, l, NI):
    ident = np.eye(128, dtype=np.float32).astype(NPBF)
    NCT = 2 * D_FF // 128
    cw = np.ascontiguousarray(np.asarray(P["conv_w"][l], np.float32).reshape(3, NCT, 128).transpose(2, 1, 0))
    cb = np.ascontiguousarray(np.asarray(P["conv_b"][l], np.float32).reshape(NCT, 128).T)
    maps = []
    for c in range(NCORE):
        b, j = c // 4, c % 4
        tok = core_tokens(NI, j)
        halo = np.zeros((NI, 2, D), np.float32)
        for i in range(NI):
            g0 = SB * (4 * i + j)
            if g0 >= 2:
                halo[i] = xmid[b][g0 - 2:g0]
        maps.append(dict(xmid=np.ascontiguousarray(xmid[b][tok]), halo=halo, gffn=rep(P["norm_ffn"][l]),
                         w_up=P["w_up"][l], w_down=P["w_down"][l], convw=cw, convb=cb, ident=ident))
    return maps


def gather_tok(results, key, S, NI):
    out = np.zeros((2, S, D), np.float32)
    for c in range(NCORE):
        b, j = c // 4, c % 4
        out[b][core_tokens(NI, j)] = results[c][key]
    return out


_PROG = {}


def _prog(key, fn, *args):
    if key not in _PROG:
        _PROG[key] = fn(*args)
    return _PROG[key]


def _run(nc, maps):
    return run_bass_kernel_spmd(nc, maps, core_ids=list(range(NCORE))).results


def kernel_unfused(**inputs):
    P = {k: np.asarray(v) for k, v in inputs.items()}
    x = np.asarray(P["x"], np.float32)
    S = x.shape[1]
    NI = S // (4 * SB)
    NT = NI * SB
    for l in range(2):
        rA = _run(_prog(("A", NT), build_A, NT), shard_A(x, P, l, NI))
        rB = _run(_prog(("B", S, NI), build_B, S, NI), maps_B(rA, P["rel_bias"], NI))
        xloc = [np.ascontiguousarray(x[c // 4][core_tokens(NI, c % 4)]) for c in range(NCORE)]
        xmid = gather_tok(_run(_prog(("C1", NT), build_C1, NT), maps_C1(xloc, rB, rA, P, l)), "xmid", S, NI)
        x = gather_tok(_run(_prog(("C2", NI), build_C2, NI), shard_C2(xmid, P, l, NI)), "xout", S, NI)
    return x.astype(np.float32)


GROUPS = [[0, 1, 2, 3], [4, 5, 6, 7]]
NCT_ = 2 * D_FF // 128


def build_fused(S, NI):
    NT = NI * SB
    NH2 = NI * 2
    nc = bass.Bass("TRN2", target_bir_lowering=False)
    es = ExitStack()
    with es:
        sc = Sched(nc, es)
        env = Env(nc, sc)

        def ein(name, shape, dt):
            return nc.dram_tensor(name, list(shape), dt, kind="ExternalInput").ap()

        def scr(name, shape, dt):
            return nc.dram_tensor(name, list(shape), dt).ap()

        X = dict(
            x=ein("x", [NT, D], F32), w_in=ein("w_in", [2, D, IN_COLS], F32), w_uq=ein("w_uq", [2, 384, 768], F32),
            w_ukv=ein("w_ukv", [2, 256, 1024], F32), gmix=ein("gmix", [2, 128, 1024], F32), gqk=ein("gqk", [2, 128, 1024], F32),
            gcq=ein("gcq", [2, 128, 384], F32), gckv=ein("gckv", [2, 128, 256], F32), gbq=ein("gbq", [2, 128, 768], F32),
            gbk=ein("gbk", [2, 128, 768], F32), bgate=ein("bgate", [2, 128, 2048], F32), cos=ein("cos", [NT, 16], F32),
            sin=ein("sin", [NT, 16], F32), ident=ein("ident", [128, 128], BF16),
            mg=ein("mg", [128, 8, 128], BF16), diagi=ein("diagi", [128, 4, 512], BF16), selcol=ein("selcol", [128, 8], F32),
            sel=ein("sel", [128, 8, 128], BF16), negrow=ein("negrow", [1, 4, 128], BF16), onesrow=ein("onesrow", [1, 512], BF16),
            bd=ein("bd", [128, 4, 512], BF16), e65=ein("e65", [65, 64], F32), pow2=ein("pow2", [128, NIT], F32),
            tbg=ein("tbg", [128, 8, 384], F32), far=ein("far", [128, 8], F32),
            w_pa=ein("w_pa", [2, 512, D], F32), w_pb=ein("w_pb", [2, 512, D], F32), w_out=ein("w_out", [2, D, D], F32),
            gffn=ein("gffn", [2, 128, D], F32), w_up=ein("w_up", [2, D, 2 * D_FF], F32), w_down=ein("w_down", [2, D_FF, D], F32),
            convw=ein("convw", [2, 128, NCT_, 3], F32), convb=ein("convb", [2, 128, NCT_], F32),
            halosel=ein("halosel", [4 * NH2, NH2], F32),
        )
        xout = nc.dram_tensor("xout", [NT, D], F32, kind="ExternalOutput").ap()
        T = dict(
            qiT=scr("s_qiT", [64, NT * 8], BF16), w2=scr("s_w2", [NT, 8], F32), qaT=scr("s_qaT", [512, NT], BF16),
            qbT=scr("s_qbT", [768, NT], BF16), kiT=scr("s_kiT", [NI * 64, 512], BF16), kaT=scr("s_kaT", [NT, 512], BF16),
            va=scr("s_va", [NT, 520], BF16), kbT=scr("s_kbT", [NI * 768, 512], BF16), vb=scr("s_vb", [NT, 520], BF16),
            gate=scr("s_gate", [NT, 2048], F32),
            kiT_g=scr("g_kiT", [4 * NI * 64, 512], BF16), kaT_g=scr("g_kaT", [4 * NI * 512, 512], BF16), va_g=scr("g_va", [4 * NT, 520], BF16),
            kbT_g=scr("g_kbT", [4 * NI * 768, 512], BF16), vb_g=scr("g_vb", [4 * NT, 520], BF16),
            yaT=scr("s_yaT", [512, NT], BF16), ybT=scr("s_ybT", [512, NT], BF16), xmid=scr("s_xmid", [NT, D], F32),
            x1=scr("s_x1", [NT, D], F32), halo_loc=scr("s_hloc", [NH2, D], F32), halo_g=scr("g_halo", [4 * NH2, D], F32),
            halo=scr("s_halo", [NI, 2, D], F32),
        )
        WB = {k: scr("b_" + k, list(X[k].shape), BF16) for k in ("w_in", "w_uq", "w_ukv", "w_pa", "w_pb", "w_out", "w_up", "w_down")}

        def conv(k, l):
            rows = X[k].shape[1]
            for r0 in range(0, rows, 128):
                r1 = min(rows, r0 + 128)
                sc.dma("pool", WB[k][l, r0:r1, :], X[k][l, r0:r1, :], "wconv")
        for k in ("w_in", "w_uq", "w_ukv"):
            conv(k, 0)
        sc.emit()
        for k, l in (("w_pa", 0), ("w_pb", 0), ("w_out", 0), ("w_down", 0), ("w_up", 0), ("w_in", 1), ("w_uq", 1), ("w_ukv", 1),
                     ("w_pa", 1), ("w_pb", 1), ("w_out", 1), ("w_down", 1), ("w_up", 1)):
            conv(k, l)
        for l in range(2):
            xin = X["x"] if l == 0 else T["x1"]
            env.d = dict(x=xin, w_in=WB["w_in"][l], w_uq=WB["w_uq"][l], w_ukv=WB["w_ukv"][l], gmix=X["gmix"][l], gqk=X["gqk"][l],
                         gcq=X["gcq"][l], gckv=X["gckv"][l], gbq=X["gbq"][l], gbk=X["gbk"][l], bgate=X["bgate"][l],
                         cos=X["cos"], sin=X["sin"], ident=X["ident"],
                         qiT=T["qiT"], w2=T["w2"], qaT=T["qaT"], qbT=T["qbT"], kiT=T["kiT"], kaT=T["kaT"], va=T["va"],
                         kbT=T["kbT"], vb=T["vb"], gate=T["gate"])
            def hook_sb(i):
                for k, rows in (("kiT", 64), ("kaT", 512), ("kbT", 768), ("va", 512), ("vb", 512)):
                    sc.coll("AllGather", [T[k][rows * i:rows * (i + 1), :]], [T[k + "_g"][4 * rows * i:4 * rows * (i + 1), :]], GROUPS,
                            reads=[("dr", k, i)], writes=[("g", k, i)])
            env.hook_sb = hook_sb
            env.skip_cc = True
            build_A(NT, env)
            env.hook_sb = None
            env.skip_cc = False
            env.d = dict(qiT=T["qiT"], w2T=T["w2"].rearrange("t h -> (t h)").rearrange("(c p) -> p c", p=128),
                         qaT=T["qaT"], qbT=T["qbT"], kiT_g=T["kiT_g"], kaT_g=T["kaT_g"], va_g=T["va_g"], kbT_g=T["kbT_g"],
                         vb_g=T["vb_g"], yaT=T["yaT"], ybT=T["ybT"],
                         **{k: X[k] for k in ("ident", "mg", "diagi", "selcol", "sel", "negrow", "onesrow", "bd", "e65", "pow2", "tbg", "far")})
            build_B(S, NI, env)
            env.d = dict(x=xin, yaT=T["yaT"], ybT=T["ybT"], gate=T["gate"], w_pa=WB["w_pa"][l], w_pb=WB["w_pb"][l], w_out=WB["w_out"][l],
                         ident=X["ident"], xmid=T["xmid"])
            build_C1(NT, env)
            for i in range(NI):
                sc.dma("sp", T["halo_loc"][2 * i:2 * i + 2, :], T["xmid"][SB * i + SB - 2:SB * i + SB, :], "hl")
            sc.emit()
            sc.coll("AllGather", [T["halo_loc"]], [T["halo_g"]], GROUPS)
            sc.emit()
            with ExitStack() as es2:
                cx = Ctx(nc, es2, env)
                hg = cx.sb([4 * NH2, D], F32)
                hs = cx.sb([4 * NH2, NH2], F32)
                ho = cx.sb([NH2, D], F32)
                pp = [cx.ps([128, 512], F32) for _ in range(2)]
                sc.dma("sp", hg[:], T["halo_g"], "hx", writes=["hg"])
                sc.dma("sp", hs[:], X["halosel"], "hx", writes=["hs"])
                for ct in range(2):
                    sc.op("pe", lambda e, ct=ct: e.matmul(pp[ct][0:NH2, :], hs[:, :], hg[:, ct * 512:(ct + 1) * 512], start=True, stop=True),
                          reads=["hg", "hs"], writes=[("pp", ct)])
                    sc.op("act", lambda e, ct=ct: e.activation(out=ho[:, ct * 512:(ct + 1) * 512], in_=pp[ct][0:NH2, :], func=AF.Copy),
                          reads=[("pp", ct)], writes=[("ho", ct)])
                sc.dma("sp", T["halo"].rearrange("i k d -> (i k) d"), ho[:], "hy", reads=[("ho", 0), ("ho", 1)])
                sc.emit()
            env.d = dict(xmid=T["xmid"], halo=T["halo"], gffn=X["gffn"][l], w_up=WB["w_up"][l], w_down=WB["w_down"][l],
                         convw=X["convw"][l], convb=X["convb"][l], ident=X["ident"], xout=(T["x1"] if l == 0 else xout))
            build_C2(NI, env)
        print("fused program: ops", sc.nops, "waits", sc.nwaits)
    return nc


def maps_fused(P, S, NI):
    x = np.asarray(P["x"], np.float32)
    NH2 = NI * 2
    ident = np.eye(128, dtype=np.float32).astype(NPBF)
    bidx = tb_bucket_idx()
    rb = np.asarray(P["rel_bias"], np.float32)
    tbg = np.ascontiguousarray(rb[bidx].transpose(0, 2, 1))
    far = np.ascontiguousarray(np.broadcast_to(rb[15][None, :], (128, 8)))
    st = lambda f: np.stack([f(l) for l in range(2)])
    shared = dict(
        w_in=np.asarray(P["w_in"], np.float32), w_uq=np.asarray(P["b_w_uq"], np.float32), w_ukv=np.asarray(P["b_w_ukv"], np.float32),
        gmix=st(lambda l: rep(P["norm_mix"][l])),
        gqk=st(lambda l: rep(np.concatenate([np.tile(P["a_q_norm"][l], 8), np.tile(P["a_k_norm"][l], 8)]))),
        gcq=st(lambda l: rep(P["b_cq_norm"][l])), gckv=st(lambda l: rep(P["b_ckv_norm"][l])),
        gbq=st(lambda l: rep(np.tile(P["b_q_norm"][l], 8))), gbk=st(lambda l: rep(np.tile(P["b_k_norm"][l], 8))),
        bgate=st(lambda l: rep(P["b_gate"][l])), ident=ident, tbg=tbg, far=far,
        w_pa=np.asarray(P["w_proj_a"], np.float32), w_pb=np.asarray(P["w_proj_b"], np.float32), w_out=np.asarray(P["w_out"], np.float32),
        gffn=st(lambda l: rep(P["norm_ffn"][l])), w_up=np.asarray(P["w_up"], np.float32), w_down=np.asarray(P["w_down"], np.float32),
        convw=st(lambda l: np.ascontiguousarray(np.asarray(P["conv_w"][l], np.float32).reshape(3, NCT_, 128).transpose(2, 1, 0))),
        convb=st(lambda l: np.ascontiguousarray(np.asarray(P["conv_b"][l], np.float32).reshape(NCT_, 128).T)),
    )
    maps = []
    for c in range(NCORE):
        b, j = c // 4, c % 4
        tok = core_tokens(NI, j)
        cos, sin = rope_tables(tok)
        hs = np.zeros((4 * NH2, NH2), np.float32)
        for i in range(NI):
            jj, ii = (j - 1, i) if j > 0 else (3, i - 1)
            if ii >= 0:
                for k in range(2):
                    hs[jj * NH2 + 2 * ii + k, 2 * i + k] = 1.0
        m = dict(shared)
        m.update(consts_B(j))
        m.update(x=np.ascontiguousarray(x[b][tok]), cos=cos, sin=sin, halosel=hs)
        maps.append(m)
    return maps


def kernel_fused(**inputs):
    P = {k: np.asarray(v) for k, v in inputs.items()}
    S = P["x"].shape[1]
    NI = S // (4 * SB)
    res = _run(_prog(("F", S, NI), build_fused, S, NI), maps_fused(P, S, NI))
    return gather_tok(res, "xout", S, NI).astype(np.float32)


def kernel(**inputs):
    return kernel_fused(**inputs)
```

```python
import math
from contextlib import ExitStack

import numpy as np
import ml_dtypes

import concourse.bass as bass
import concourse.mybir as mybir
from concourse.bass_utils import run_bass_kernel_spmd

F32 = mybir.dt.float32
BF16 = mybir.dt.bfloat16
FP8 = mybir.dt.float8e4
AF = mybir.ActivationFunctionType
ALU = mybir.AluOpType
AX = mybir.AxisListType
NPBF = ml_dtypes.bfloat16

D = 1024
NCORE = 8
SB = 512
IN_COLS = 4840
D_FF = 2816
NEG_IDX = -1.0e30
NEG_B = -30000.0
NIT = 16
EPS = 1e-6


class Sched:
    ENG = ("pe", "act", "dve", "pool", "sp")

    def __init__(self, nc, es):
        self.nc = nc
        self.es = es
        self.ops = {e: [] for e in self.ENG}
        self.sems = {}
        self.cnt = {}
        self.seen = {e: {} for e in self.ENG}
        self.lastw = {}
        self.readers = {}
        self.nops = 0
        self.nwaits = 0
        for e in self.ENG[:4]:
            self._mk(e)

    def _mk(self, key):
        self.sems[key] = self.es.enter_context(self.nc.semaphore("s_%d" % len(self.sems)))
        self.cnt[key] = 0

    def _deps(self, eng, reads, writes):
        need = {}

        def add(ref):
            if ref is None:
                return
            k, v = ref
            if k not in self.ENG:
                v = self.cnt[k]
            if need.get(k, 0) < v:
                need[k] = v

        for r in reads:
            add(self.lastw.get(r))
        for w in writes:
            add(self.lastw.get(w))
            for k, v in self.readers.get(w, {}).items():
                add((k, v))
        waits = []
        for k, v in need.items():
            if k == "pe" and eng == "pe":
                continue
            if self.seen[eng].get(k, 0) < v:
                self.seen[eng][k] = v
                waits.append((k, v))
        return waits

    def _commit(self, ref, reads, writes):
        k, v = ref
        for r in reads:
            d = self.readers.setdefault(r, {})
            if d.get(k, 0) < v:
                d[k] = v
        for w in writes:
            self.lastw[w] = ref
            self.readers[w] = {}

    def record(self, f):
        rec = []
        self._rec = rec
        try:
            f()
        finally:
            self._rec = None
        return rec

    def interleave(self, a, b):
        na, nb = len(a), len(b)
        ia = ib = 0
        while ia < na or ib < nb:
            if ib >= nb or (ia < na and ia * nb <= ib * na):
                a[ia]()
                ia += 1
            else:
                b[ib]()
                ib += 1

    def op(self, eng, fn, reads=(), writes=()):
        if getattr(self, "_rec", None) is not None:
            rec = self._rec
            rec.append(lambda: self._op(eng, fn, reads, writes))
            return
        self._op(eng, fn, reads, writes)

    def _op(self, eng, fn, reads=(), writes=()):
        waits = self._deps(eng, reads, writes)
        self.cnt[eng] += 1
        self.ops[eng].append((waits, fn, eng, 1))
        self._commit((eng, self.cnt[eng]), reads, writes)
        self.nops += 1
        self.nwaits += len(waits)

    def dma(self, q, out, in_, semkey, reads=(), writes=(), **kw):
        if getattr(self, "_rec", None) is not None:
            rec = self._rec
            rec.append(lambda: self._dma(q, out, in_, semkey, reads, writes, **kw))
            return
        self._dma(q, out, in_, semkey, reads, writes, **kw)

    def _dma(self, q, out, in_, semkey, reads=(), writes=(), **kw):
        if semkey not in self.sems:
            self._mk(semkey)
        waits = self._deps(q, reads, writes)
        self.cnt[semkey] += 16
        self.ops[q].append((waits, lambda e: e.dma_start(out=out, in_=in_, **kw), semkey, 16))
        self._commit((semkey, self.cnt[semkey]), reads, writes)
        self.nops += 1
        self.nwaits += len(waits)

    def coll(self, kind, ins, outs, groups, reads=(), writes=()):
        if getattr(self, "_rec", None) is not None:
            rec = self._rec
            rec.append(lambda: self._coll(kind, ins, outs, groups, reads, writes))
            return
        self._coll(kind, ins, outs, groups, reads, writes)

    def _coll(self, kind, ins, outs, groups, reads=(), writes=()):
        if "cc" not in self.sems:
            self._mk("cc")
        waits = self._deps("pool", reads, writes)
        self.cnt["cc"] += 1
        self.ops["pool"].append((waits, lambda e: e.collective_compute(kind, ALU.bypass, replica_groups=groups, ins=ins, outs=outs),
                                 "cc", 1))
        self._commit(("cc", self.cnt["cc"]), reads, writes)

    def barrier(self, skip_cc=False):
        for eng in self.ENG:
            waits = []
            for k, v in self.cnt.items():
                if skip_cc and k == "cc":
                    continue
                if v > 0 and self.seen[eng].get(k, 0) < v:
                    self.seen[eng][k] = v
                    waits.append((k, v))
            self.ops[eng].append((waits, None, None, 0))
        if skip_cc:
            self.lastw = {r: ref for r, ref in self.lastw.items() if ref[0] == "cc"}
        else:
            self.lastw = {}
        self.readers = {}

    def emit(self, skip_cc=False):
        nc = self.nc
        self.barrier(skip_cc)
        fin = []
        with nc.Block() as block:
            def run(e, name):
                for waits, fn, sk, amt in self.ops[name]:
                    for k, v in waits:
                        e.wait_ge(self.sems[k], v)
                    if fn is not None:
                        fn(e).then_inc(self.sems[sk], amt)

            @block.tensor
            def _(e):
                run(e, "pe")

            @block.scalar
            def _(e):
                run(e, "act")

            @block.vector
            def _(e):
                run(e, "dve")

            @block.gpsimd
            def _(e):
                run(e, "pool")

            @block.sync
            def _(e):
                run(e, "sp")
        self.ops = {e: [] for e in self.ENG}


class Pipe:
    def __init__(self, lag):
        self.lag = lag
        self.q = []

    def push(self, fns):
        self.q.append(fns)
        if len(self.q) > self.lag:
            for f in self.q.pop(0):
                f()

    def flush(self):
        while self.q:
            for f in self.q.pop(0):
                f()


LAG = 2
MASK_ENG = ("dve", "dve")


def wq(ap):
    return "sp" if ap.dtype == BF16 else "pool"


class Ctx:
    def __init__(self, nc, es, env=None):
        self.nc = nc
        self.es = es
        self.env = env
        self.n = env.n if env else 0

    def sb(self, shape, dt, name=None):
        self.n += 1
        if self.env:
            self.env.n = self.n
        return self.es.enter_context(self.nc.sbuf_tensor(name or ("t%d" % self.n), list(shape), dt))

    def ps(self, shape, dt, name=None):
        self.n += 1
        if self.env:
            self.env.n = self.n
        return self.es.enter_context(self.nc.psum_tensor(name or ("p%d" % self.n), list(shape), dt))

    def din(self, name, shape, dt):
        if self.env:
            ap = self.env.d[name]
            assert tuple(ap.shape) == tuple(shape), (name, ap.shape, shape)
            return ap
        return self.nc.dram_tensor(name, list(shape), dt, kind="ExternalInput").ap()

    def dout(self, name, shape, dt):
        if self.env:
            ap = self.env.d[name]
            assert tuple(ap.shape) == tuple(shape), (name, ap.shape, shape)
            return ap
        return self.nc.dram_tensor(name, list(shape), dt, kind="ExternalOutput").ap()


class Env:
    def __init__(self, nc, sc):
        self.nc = nc
        self.sc = sc
        self.d = {}
        self.n = 0


def t5_bucket_np(rel):
    nb = 16
    max_exact = 8
    side = np.where(rel > 0, nb, 0)
    n = np.abs(rel)
    nf = np.maximum(n, 1).astype(np.float32)
    large = max_exact + (np.log(nf / max_exact) / math.log(128 / max_exact) * (nb - max_exact)).astype(np.int32)
    large = np.minimum(large, nb - 1)
    return side + np.where(n < max_exact, n, large)


def consts_B(j):
    c = {}
    c["ident"] = np.eye(128, dtype=np.float32).astype(NPBF)
    mg = np.zeros((128, 8, 128), np.float32)
    for p in range(128):
        t16 = p // 8
        for g in range(8):
            mg[p, g, 16 * g + t16] = 1.0
    c["mg"] = mg.astype(NPBF)
    t = np.arange(512)[:, None]
    s = np.arange(512)[None, :]
    diag = np.where((s // 64) > (t // 64), NEG_IDX, 0.0).astype(np.float32)
    c["diagi"] = diag.reshape(4, 128, 512).transpose(1, 0, 2).astype(NPBF).copy()
    selcol = np.zeros((128, 8), np.float32)
    for m in range(4):
        selcol[:, m] = 1.0 if m == j else 0.0
        selcol[:, 4 + m] = NEG_IDX if m > j else 0.0
    c["selcol"] = selcol
    sel = np.zeros((128, 8, 128), np.float32)
    for m in range(4):
        if m == j:
            sel[:, m, :] = np.eye(128)
        if m == j:
            sel[:, 4 + m, :] = np.eye(128)
    c["sel"] = sel.astype(NPBF)
    negrow = np.zeros((1, 4, 128), np.float32)
    for m in range(4):
        if m > j:
            negrow[0, m, :] = NEG_B
    c["negrow"] = negrow.astype(NPBF)
    c["onesrow"] = np.ones((1, 512), np.float32).astype(NPBF)
    k = np.arange(128)[:, None, None]
    sub = np.arange(4)[None, :, None]
    tt = np.arange(512)[None, None, :]
    bd = np.where(((128 * sub + k) // 64) > (tt // 64), NEG_B, 0.0).astype(np.float32)
    c["bd"] = bd.astype(NPBF)
    e65 = np.zeros((65, 64), np.float32)
    e65[64, :] = 1.0
    c["e65"] = e65
    c["pow2"] = np.tile((2.0 ** -(np.arange(NIT) + 1.0))[None, :], (128, 1)).astype(np.float32)
    return c


def tb_bucket_idx():
    k = np.arange(128)[:, None]
    cc = np.arange(384)[None, :]
    return t5_bucket_np(k - cc + 128)


def build_B(S, NI, env=None):
    NT = NI * SB
    nc = env.nc if env else bass.Bass("TRN2", target_bir_lowering=False)
    es = ExitStack()
    with es:
        cx = Ctx(nc, es, env)
        sc = env.sc if env else Sched(nc, es)
        d_qiT = cx.din("qiT", [64, NT * 8], BF16)
        d_w2T = cx.din("w2T", [128, NT * 8 // 128], F32)
        d_qaT = cx.din("qaT", [512, NT], BF16)
        d_qbT = cx.din("qbT", [768, NT], BF16)
        d_kiT = cx.din("kiT_g", [4 * NI * 64, 512], BF16)
        d_kaT = cx.din("kaT_g", [4 * NI * 512, 512], BF16)
        d_va = cx.din("va_g", [4 * NT, 520], BF16)
        d_kbT = cx.din("kbT_g", [4 * NI * 768, 512], BF16)
        d_vb = cx.din("vb_g", [4 * NT, 520], BF16)
        d_ident = cx.din("ident", [128, 128], BF16)
        d_mg = cx.din("mg", [128, 8, 128], BF16)
        d_diagi = cx.din("diagi", [128, 4, 512], BF16)
        d_selcol = cx.din("selcol", [128, 8], F32)
        d_sel = cx.din("sel", [128, 8, 128], BF16)
        d_negrow = cx.din("negrow", [1, 4, 128], BF16)
        d_onesrow = cx.din("onesrow", [1, 512], BF16)
        d_bd = cx.din("bd", [128, 4, 512], BF16)
        d_e65 = cx.din("e65", [65, 64], F32)
        d_pow2 = cx.din("pow2", [128, NIT], F32)
        d_tbg = cx.din("tbg", [128, 8, 384], F32)
        d_far = cx.din("far", [128, 8], F32)
        d_yaT = cx.dout("yaT", [512, NT], BF16)
        d_ybT = cx.dout("ybT", [512, NT], BF16)

        ident = cx.sb([128, 128], BF16)
        mg = cx.sb([128, 8, 128], BF16)
        diagi = cx.sb([128, 4, 512], BF16)
        selcol = cx.sb([128, 8], F32)
        sel = cx.sb([128, 8, 128], BF16)
        negrow = cx.sb([1, 4, 128], BF16)
        onesrow = cx.sb([1, 512], BF16)
        bd = cx.sb([128, 4, 512], BF16)
        e65 = cx.sb([65, 64], F32)
        pow2 = cx.sb([128, NIT], F32)
        far = cx.sb([128, 8], F32)
        dtmp = cx.sb([128, 2048], F32)
        tbg3 = dtmp[:, 0:1536].rearrange("p (h c) -> p h c", h=4)
        tb = cx.sb([128, 8, 384], BF16)
        ki_t = [cx.sb([64, 512], BF16) for _ in range(2)]
        w2T = cx.sb([128, NT * 8 // 128], F32)
        for i_, (dst, src) in enumerate([(ident, d_ident), (mg, d_mg), (diagi, d_diagi), (selcol, d_selcol),
                                         (sel, d_sel), (negrow, d_negrow), (onesrow, d_onesrow), (bd, d_bd),
                                         (e65, d_e65), (pow2, d_pow2), (far, d_far), (far, d_far)]):
            sc.dma("sp", dst[:], src, ("c", i_), writes=[("c", i_)])
        for hq in range(2):
            sc.dma("sp", tbg3, d_tbg[:, 4 * hq:4 * hq + 4, :], "tbg", writes=["dtmp"])
            sc.op("dve", lambda e, hq=hq: e.tensor_tensor(out=tb[:, 4 * hq:4 * hq + 4, :], in0=tbg3,
                                                          in1=far[:, 4 * hq:4 * hq + 4].unsqueeze(2).to_broadcast([128, 4, 384]),
                                                          op=ALU.subtract),
                  reads=[("c", 11), "dtmp"], writes=["tb"])

        score = cx.sb([128, S], F32)
        maskT = cx.sb([128, S // 128, 512], FP8)
        junk = cx.sb([128, 2048], BF16)
        qi_sb = cx.sb([64, SB * 8], BF16)
        qa_sb = qi_sb[:, :].rearrange("d (h t) -> d h t", h=8)
        qb2_sb = cx.sb([96, 2, SB], BF16)
        w2g = cx.sb([128, 8, 128], BF16)
        NH = 3
        hid = [cx.sb([128, 512], BF16) for _ in range(NH)]
        mts = [cx.sb([128, 512], BF16) for _ in range(2)]
        NPT = 4
        pT = [cx.sb([128, 512], BF16) for _ in range(NPT)]
        kt_b = [cx.sb([96, 4, 512], BF16) for _ in range(2)]
        v_b = [cx.sb([128, 4, 260], BF16) for _ in range(2)]
        o_sb = [cx.sb([65, 512], F32) for _ in range(2)]
        rec = [cx.sb([64, 512], F32) for _ in range(1)]
        yT = [cx.sb([64, 512], BF16) for _ in range(2)]
        st = cx.sb([128, 16], F32)
        cnts = cx.sb([128, 8], F32)
        steps = cx.sb([128, NIT], F32)
        ps = [cx.ps([128, 512], F32) for _ in range(8)]

        cst = lambda *idx: [("c", k) for k in idx]
        state = dict(hid=0, pT=0, kv=0, o=0, mts=0, ki=0)

        for i in range(NI):
            nkt = 4 * i + 4
            tok0 = i * SB
            sc.dma("sp", qi_sb[:], d_qiT[:, tok0 * 8:(tok0 + SB) * 8], "q", writes=["qi"])
            sc.dma("sp", w2T[:, 32 * i:32 * i + 32], d_w2T[:, 32 * i:32 * i + 32], "q", writes=[("w2T", i)],
                   allow_slow_non_contiguous=True)

            def b_piece(hpair, i=i, nkt=nkt, tok0=tok0):
                sc.dma("sp", qb2_sb[:], d_qbT[192 * hpair:192 * hpair + 192, tok0:tok0 + SB].rearrange("(h d) t -> d h t", d=96),
                       "q2", writes=["qb2"])
                pipe = Pipe(LAG)
                u = 0
                for kt in range(nkt):
                    slot = state["kv"] % 2
                    state["kv"] += 1
                    ktile, vtile = kt_b[slot], v_b[slot]
                    sc.dma("sp", ktile[:, 0:2, :],
                           d_kbT[768 * kt + 192 * hpair:768 * kt + 192 * hpair + 192, :].rearrange("(h d) s -> d h s", d=96),
                           ("kv", slot), reads=[("g", "kbT", kt // 4)], writes=[("k", slot)])
                    sc.dma("sp", vtile[:, :, 0:130],
                           d_vb[512 * kt:512 * kt + 512, 130 * hpair:130 * hpair + 130].rearrange("(u p) c -> p u c", p=128),
                           ("kv", slot), reads=[("g", "vb", kt // 4)], writes=[("v", slot)])
                    diag = kt >= 4 * i
                    m = kt - 4 * i
                    for sub in range(4):
                        for hh in range(2):
                            sb_i = 4 + (u % 4)
                            u += 1
                            pss = ps[sb_i]
                            sc.op("pe", lambda e, pss=pss, ktile=ktile, hh=hh, sub=sub, diag=diag: e.matmul(
                                pss[:, :], ktile[:, hh, sub * 128:(sub + 1) * 128], qb2_sb[:, hh, :], start=True, stop=not diag),
                                reads=[("k", slot), "qb2"], writes=[("ps", sb_i)])
                            if diag:
                                sc.op("pe", lambda e, pss=pss, m=m, sub=sub: e.matmul(
                                    pss[:, :], sel[:, m, :], bd[:, sub, :], start=False, stop=False),
                                    reads=cst(4, 7), writes=[("ps", sb_i)])
                                sc.op("pe", lambda e, pss=pss, m=m: e.matmul(
                                    pss[:, :], negrow[0:1, m, :], onesrow[0:1, :], start=False, stop=True),
                                    reads=cst(5, 6), writes=[("ps", sb_i)])
                            pslot = state["pT"] % NPT
                            state["pT"] += 1
                            pt = pT[pslot]
                            sc.op("act", lambda e, pt=pt, pss=pss: e.activation(out=pt[:, :], in_=pss[:, :], func=AF.Exp),
                                  reads=[("ps", sb_i)], writes=[("pT", pslot)])
                            first = (kt == 0 and sub == 0)
                            last = (kt == nkt - 1 and sub == 3)
                            pipe.push([lambda hh=hh, vtile=vtile, sub=sub, pt=pt, first=first, last=last, slot=slot, pslot=pslot: sc.op(
                                "pe", lambda e: e.matmul(
                                    ps[hh][0:65, :], vtile[:, sub, hh * 65:(hh + 1) * 65], pt[:, :], start=first, stop=last),
                                reads=[("v", slot), ("pT", pslot)], writes=[("ps", hh)])])
                pipe.flush()
                for hh in range(2):
                    h = 2 * hpair + hh
                    _finalize(sc, state, ps[hh], ps[4 + hh], o_sb, rec, yT, e65, 512,
                              d_ybT[h * 64:(h + 1) * 64, tok0:tok0 + SB], hh, 4 + hh)

            for half in range(1):
                for qb_i in range(4):
                    qq = qb_i
                    tq = qb_i * 128
                    gcol = (tok0 + tq) * 8 // 128
                    sc.op("dve", lambda e, gcol=gcol: e.tensor_tensor(
                        out=w2g[:], in0=mg[:], in1=w2T[:, gcol:gcol + 8].unsqueeze(2).to_broadcast([128, 8, 128]),
                        op=ALU.mult), reads=cst(1) + [("w2T", i)], writes=["w2g"])
                    pipe = Pipe(LAG)
                    for kt in range(nkt):
                        psc_i = 4 + (kt % 2)
                        psc = ps[psc_i]
                        kis = state["ki"] % 2
                        state["ki"] += 1
                        kit = ki_t[kis]
                        sc.dma("sp", kit[:, :], d_kiT[64 * kt:64 * kt + 64, :], ("ki", kis), reads=[("g", "kiT", kt // 4)], writes=[("ki", kis)])
                        for g in range(8):
                            ph_i = (state["hid"]) % NH
                            state["hid"] += 1
                            ph = ps[ph_i]
                            hd = hid[ph_i]
                            c0 = (tq + 16 * g) * 8
                            sc.op("pe", lambda e, ph=ph, c0=c0, kit=kit: e.matmul(
                                ph[:, :], qi_sb[:, c0:c0 + 128], kit[:, :], start=True, stop=True),
                                reads=["qi", ("ki", kis)], writes=[("ps", ph_i)])
                            sc.op("act", lambda e, hd=hd, ph=ph: e.activation(out=hd[:, :], in_=ph[:, :], func=AF.Relu),
                                  reads=[("ps", ph_i)], writes=[("hid", ph_i)])
                            fns = [lambda psc=psc, g=g, hd=hd, ph_i=ph_i, psc_i=psc_i: sc.op("pe", lambda e: e.matmul(
                                psc[:, :], w2g[:, g, :], hd[:, :], start=(g == 0), stop=(g == 7)),
                                reads=["w2g", ("hid", ph_i)], writes=[("ps", psc_i)])]
                            if g == 7:
                                sco = score[:, kt * 512:(kt + 1) * 512]
                                if kt >= 4 * i:
                                    m = kt - 4 * i
                                    fns.append(lambda sco=sco, psc=psc, m=m, qb_i=qb_i, psc_i=psc_i, kt=kt: sc.op(
                                        "dve", lambda e: e.scalar_tensor_tensor(
                                            out=sco, in0=diagi[:, qb_i, :], scalar=selcol[:, m:m + 1], in1=psc[:, :],
                                            op0=ALU.mult, op1=ALU.add), reads=[("ps", psc_i)] + cst(2, 3), writes=[("sc", kt)]))
                                    fns.append(lambda sco=sco, m=m, kt=kt: sc.op("dve", lambda e: e.tensor_scalar(
                                        out=sco, in0=sco, scalar1=selcol[:, 4 + m:5 + m], scalar2=None, op0=ALU.add),
                                        reads=[("sc", kt)] + cst(3), writes=[("sc", kt)]))
                                else:
                                    fns.append(lambda sco=sco, psc=psc, psc_i=psc_i, kt=kt: sc.op(
                                        "dve", lambda e: e.tensor_copy(out=sco, in_=psc[:, :]),
                                        reads=[("ps", psc_i)], writes=[("sc", kt)]))
                            pipe.push(fns)
                    pipe.flush()
                    sc._rec = recII = []
                    n = nkt * 512
                    nf = 4 * i * 512
                    allsc = [("sc", kt) for kt in range(nkt)]
                    sc.op("dve", lambda e, n=n: e.tensor_reduce(out=st[:, 0:1], in_=score[:, 0:n], axis=AX.X, op=ALU.max),
                          reads=allsc, writes=["st0"])
                    sc.op("dve", lambda e, nf=nf, n=n: e.tensor_scalar(
                        out=dtmp[:, 0:2048], in0=score[:, nf:n], scalar1=-1.0e29, scalar2=4.0e30, op0=ALU.is_lt, op1=ALU.mult),
                        reads=allsc, writes=["dtmp"])
                    sc.op("dve", lambda e, nf=nf, n=n: e.tensor_tensor(
                        out=dtmp[:, 0:2048], in0=dtmp[:, 0:2048], in1=score[:, nf:n], op=ALU.add),
                        reads=allsc + ["dtmp"], writes=["dtmp"])
                    sc.op("dve", lambda e: e.tensor_reduce(out=st[:, 3:4], in_=dtmp[:, 0:2048], axis=AX.X, op=ALU.min),
                          reads=["dtmp"], writes=["st3"])
                    if i > 0:
                        sc.op("dve", lambda e, nf=nf: e.tensor_reduce(out=st[:, 1:2], in_=score[:, 0:nf], axis=AX.X, op=ALU.min),
                              reads=allsc, writes=["st1"])
                        sc.op("dve", lambda e: e.tensor_tensor(out=st[:, 3:4], in0=st[:, 3:4], in1=st[:, 1:2], op=ALU.min),
                              reads=["st1", "st3"], writes=["st3"])
                    sc.op("dve", lambda e: e.tensor_tensor(out=st[:, 4:5], in0=st[:, 0:1], in1=st[:, 3:4], op=ALU.subtract),
                          reads=["st0", "st3"], writes=["st4"])
                    sc.op("dve", lambda e: e.tensor_scalar(out=steps[:, :], in0=pow2[:, :], scalar1=st[:, 4:5], scalar2=None,
                                                           op0=ALU.mult), reads=["st4"] + cst(9), writes=["steps"])
                    sc.op("dve", lambda e: e.scalar_tensor_tensor(out=st[:, 5:6], in0=st[:, 4:5], scalar=0.5, in1=st[:, 3:4],
                                                                  op0=ALU.mult, op1=ALU.add),
                          reads=["st4", "st3"], writes=["thr"])
                    nch = (n + 2047) // 2048
                    for it in range(NIT):
                        for ch in range(nch):
                            a0 = ch * 2048
                            a1 = min(n, a0 + 2048)
                            if ch % 3 == 1 and a1 - a0 == 2048:
                                sc.op("act", lambda e, a0=a0, a1=a1, ch=ch: e.activation(
                                    out=dtmp[:, 0:2048], in_=score[:, a0:a1], func=AF.Sign, scale=-1.0, bias=st[:, 5:6],
                                    accum_out=cnts[:, ch:ch + 1]),
                                    reads=allsc + ["thr", "dtmp"], writes=["dtmp", ("cnt", ch)])
                                sc.op("dve", lambda e, ch=ch: e.tensor_scalar(
                                    out=cnts[:, ch:ch + 1], in0=cnts[:, ch:ch + 1], scalar1=-0.5, scalar2=1024.0,
                                    op0=ALU.mult, op1=ALU.add), reads=[("cnt", ch)], writes=[("cnt", ch)])
                            else:
                                sc.op("dve", lambda e, a0=a0, a1=a1, ch=ch: e.tensor_scalar(
                                    out=junk[:, 0:a1 - a0], in0=score[:, a0:a1], scalar1=st[:, 5:6], scalar2=0.0,
                                    op0=ALU.is_ge, op1=ALU.add, accum_out=cnts[:, ch:ch + 1]),
                                    reads=allsc + ["thr"], writes=["junk", ("cnt", ch)])
                        if nch > 1:
                            sc.op("dve", lambda e, nch=nch: e.tensor_reduce(out=st[:, 8:9], in_=cnts[:, 0:nch], axis=AX.X, op=ALU.add),
                                  reads=[("cnt", ch) for ch in range(nch)], writes=["ctot"])
                            csrc = st[:, 8:9]
                            crd = ["ctot"]
                        else:
                            csrc = cnts[:, 0:1]
                            crd = [("cnt", 0)]
                        sc.op("dve", lambda e, csrc=csrc: e.tensor_scalar(out=st[:, 6:7], in0=csrc, scalar1=255.5, scalar2=0.5,
                                                                         op0=ALU.is_ge, op1=ALU.subtract),
                              reads=crd, writes=["g"])
                        sc.op("dve", lambda e, it=it: e.scalar_tensor_tensor(out=st[:, 5:6], in0=st[:, 6:7], scalar=steps[:, it:it + 1],
                                                                             in1=st[:, 5:6], op0=ALU.mult, op1=ALU.add),
                              reads=["g", "steps", "thr"], writes=["thr"])
                    sc.op("dve", lambda e: e.scalar_tensor_tensor(out=st[:, 7:8], in0=st[:, 4:5], scalar=-(2.0 ** -(NIT - 1)),
                                                                  in1=st[:, 5:6], op0=ALU.mult, op1=ALU.add),
                          reads=["st4", "thr"], writes=["thrF"])
                    sc._rec = None
                    recB = sc.record(lambda: b_piece(qb_i))
                    sc.interleave(recII, recB)
                    for kt in range(nkt):
                        ms = state["mts"] % 2
                        state["mts"] += 1
                        mt = mts[ms]
                        sc.op("dve", lambda e, mt=mt, kt=kt: e.tensor_scalar(
                            out=mt[:, :], in0=score[:, kt * 512:(kt + 1) * 512], scalar1=st[:, 7:8], scalar2=None, op0=ALU.is_ge),
                            reads=[("sc", kt), "thrF"], writes=[("mts", ms)])
                        pti = 6 + ms
                        ptb = ps[pti][:, :].bitcast(BF16)
                        for sub in range(4):
                            sc.op("pe", lambda e, ptb=ptb, mt=mt, sub=sub: e.transpose(
                                ptb[:, sub * 128:(sub + 1) * 128], mt[:, sub * 128:(sub + 1) * 128], ident[:, :]),
                                reads=[("mts", ms)] + cst(0), writes=[("ps", pti)])
                        sc.op("act", lambda e, ptb=ptb, kt=kt, qq=qq: e.activation(
                            out=maskT[:, kt * 4:kt * 4 + 4, qq * 128:(qq + 1) * 128],
                            in_=ptb[:, 0:512].rearrange("p (u t) -> p u t", u=4), func=AF.Copy, saturate=False),
                            reads=[("ps", pti)], writes=[("mT", kt, qq)])
            sc.dma("sp", qa_sb, d_qaT[:, tok0:tok0 + SB].rearrange("(h d) t -> d h t", d=64), "q", writes=["qi"])
            for hg in range(2):
                pipe = Pipe(LAG)
                u = 0
                for kt in range(nkt):
                    slot = state["kv"] % 2
                    state["kv"] += 1
                    ktile, vtile = kt_b[slot], v_b[slot]
                    sc.dma("sp", ktile[0:64, :, :],
                           d_kaT[512 * kt + 256 * hg:512 * kt + 256 * hg + 256, :].rearrange("(h d) s -> d h s", d=64),
                           ("kv", slot), reads=[("g", "kaT", kt // 4)], writes=[("k", slot)])
                    sc.dma("sp", vtile[:, :, :],
                           d_va[512 * kt:512 * kt + 512, 260 * hg:260 * hg + 260].rearrange("(u p) c -> p u c", p=128),
                           ("kv", slot), reads=[("g", "va", kt // 4)], writes=[("v", slot)])
                    for sub in range(4):
                        roles = []
                        if kt >= 4 * i:
                            roles.append((kt - 4 * i, sub))
                        if sub == 3 and kt >= 4 * i - 1 and kt - 4 * i + 1 <= 3:
                            roles.append((4 + (kt - 4 * i + 1), -1))
                        mms = []
                        for (si, sg) in roles:
                            tlo = max(128 * sg - 128, 0)
                            thi = min(128 * sg + 256, 512)
                            if thi > tlo:
                                mms.append((si, tlo, thi, tlo - 128 * sg + 128, thi - 128 * sg + 128))
                        for hh in range(4):
                            h = 4 * hg + hh
                            sb_i = 4 + (u % 4)
                            u += 1
                            pss = ps[sb_i]
                            sc.op("pe", lambda e, pss=pss, ktile=ktile, hh=hh, h=h, sub=sub, nm=len(mms): e.matmul(
                                pss[:, :], ktile[0:64, hh, sub * 128:(sub + 1) * 128], qa_sb[:, h, :], start=True, stop=(nm == 0),
                                skip_group_check=True),
                                reads=[("k", slot), "qi"], writes=[("ps", sb_i)])
                            for idx, (si, tlo, thi, clo, chi) in enumerate(mms):
                                sc.op("pe", lambda e, pss=pss, si=si, tlo=tlo, thi=thi, clo=clo, chi=chi, h=h,
                                      lastm=(idx == len(mms) - 1): e.matmul(
                                    pss[:, tlo:thi], sel[:, si, :], tb[:, h, clo:chi], start=False, stop=lastm, skip_group_check=True),
                                    reads=cst(4) + ["tb"], writes=[("ps", sb_i)])
                            pslot = state["pT"] % NPT
                            state["pT"] += 1
                            pt = pT[pslot]
                            sc.op("act", lambda e, pt=pt, pss=pss: e.activation(out=pt[:, :], in_=pss[:, :], func=AF.Exp),
                                  reads=[("ps", sb_i)], writes=[("pT", pslot)])
                            meng = MASK_ENG[u % len(MASK_ENG)]
                            sc.op(meng, lambda e, pt=pt, kt=kt, sub=sub: e.tensor_tensor(
                                out=pt[:, :], in0=pt[:, :], in1=maskT[:, kt * 4 + sub, :], op=ALU.mult),
                                reads=[("pT", pslot)] + [("mT", kt, q_) for q_ in range(4)], writes=[("pT", pslot)])
                            first = (kt == 0 and sub == 0)
                            last = (kt == nkt - 1 and sub == 3)
                            pipe.push([lambda hh=hh, vtile=vtile, sub=sub, pt=pt, first=first, last=last, slot=slot, pslot=pslot: sc.op(
                                "pe", lambda e: e.matmul(
                                    ps[hh][0:65, :], vtile[:, sub, hh * 65:(hh + 1) * 65], pt[:, :], start=first, stop=last),
                                reads=[("v", slot), ("pT", pslot)], writes=[("ps", hh)])])
                pipe.flush()
                for hh in range(4):
                    h = 4 * hg + hh
                    _finalize(sc, state, ps[hh], ps[4 + hh], o_sb, rec, yT, e65, 512,
                              d_yaT[h * 64:(h + 1) * 64, tok0:tok0 + SB], hh, 4 + hh)
        sc.emit()
        print("B program: ops", sc.nops, "waits", sc.nwaits)
    return nc


def _finalize(sc, state, acc, pden, o_sb, rec, yT, e65, n, dst, acc_i, den_i, outs=None):
    k = state["o"] % 2
    state["o"] += 1
    kr = k % len(rec)
    o, r, y = o_sb[k], rec[kr], yT[k]
    sc.op("act", lambda e: e.activation(out=o[:, 0:n], in_=acc[0:65, 0:n], func=AF.Copy),
          reads=[("ps", acc_i)], writes=[("o", k)])
    sc.op("pe", lambda e: e.matmul(pden[0:64, 0:n], e65[:, :], o[:, 0:n], start=True, stop=True),
          reads=[("o", k), ("c", 8)], writes=[("ps", den_i)])
    sc.op("dve", lambda e: e.reciprocal(out=r[:, 0:n], in_=pden[0:64, 0:n]),
          reads=[("ps", den_i)], writes=[("rec", kr)])
    sc.op("dve", lambda e: e.tensor_tensor(out=y[:, 0:n], in0=o[0:64, 0:n], in1=r[:, 0:n], op=ALU.mult),
          reads=[("o", k), ("rec", kr)], writes=[("y", k)])
    if outs is None:
        outs = [(dst, 0, n)]
    for (d, a, b) in outs:
        sc.dma("sp", d, y[:, a:b], ("yo", k), reads=[("y", k)], writes=[])


def core_tokens(NI, j):
    return np.concatenate([np.arange(SB * (4 * i + j), SB * (4 * i + j) + SB) for i in range(NI)])


def maps_B(rA, rel_bias, NI):
    bidx = tb_bucket_idx()
    tbg = np.ascontiguousarray(np.asarray(rel_bias, np.float32)[bidx].transpose(0, 2, 1))
    far = np.ascontiguousarray(np.broadcast_to(np.asarray(rel_bias, np.float32)[15][None, :], (128, 8)))
    maps = []
    for c in range(NCORE):
        b, j = c // 4, c % 4
        NT = NI * SB
        m = dict(consts_B(j))
        m["tbg"] = tbg
        m["far"] = far
        for k in ("qiT", "qaT", "qbT"):
            m[k] = rA[c][k]
        m["w2T"] = np.ascontiguousarray(rA[c]["w2"].reshape(NT * 8 // 128, 128).T)
        for k, rows in (("kiT", 64), ("kaT", 512), ("kbT", 768), ("va", 512), ("vb", 512)):
            m[k + "_g"] = np.concatenate([rA[4 * b + kt % 4][k][rows * (kt // 4):rows * (kt // 4) + rows] for kt in range(4 * NI)], axis=0)
        maps.append(m)
    return maps


def split_G(G, S, NI):
    out = []
    for c in range(NCORE):
        b, j = c // 4, c % 4
        tok = core_tokens(NI, j)
        NT = len(tok)
        r = {}
        r["qiT"] = np.ascontiguousarray(G["qiT"][b].reshape(64, S, 8)[:, tok, :].reshape(64, NT * 8))
        r["w2"] = np.ascontiguousarray(G["w2"][b][tok])
        for k in ("qaT", "qbT"):
            r[k] = np.ascontiguousarray(G[k][b][:, :, tok].reshape(-1, NT))
        for k in ("kaT", "kbT"):
            a_ = G[k][b][:, :, tok].reshape(-1, NI, SB)
            r[k] = np.ascontiguousarray(a_.transpose(1, 0, 2).reshape(-1, SB))
        r["kiT"] = np.ascontiguousarray(G["kiT"][b][:, tok].reshape(64, NI, SB).transpose(1, 0, 2).reshape(-1, SB))
        for k in ("va", "vb"):
            r[k] = np.ascontiguousarray(G[k][b][tok])
        out.append(r)
    return out


def gather_B(results, S, NI):
    yaT = np.zeros((2, 8, 64, S), NPBF)
    ybT = np.zeros((2, 8, 64, S), NPBF)
    for c in range(NCORE):
        b, j = c // 4, c % 4
        tok = core_tokens(NI, j)
        yaT[b][:, :, tok] = results[c]["yaT"].reshape(8, 64, -1)
        ybT[b][:, :, tok] = results[c]["ybT"].reshape(8, 64, -1)
    return yaT, ybT


OFF = dict(qa=0, ka=512, va=1024, qi=1536, ki=2048, wi=2112, cq=2120, ckv=2504, kr=2760, ga=2792)


def build_A(NT, env=None):
    nc = env.nc if env else bass.Bass("TRN2", target_bir_lowering=False)
    es = ExitStack()
    with es:
        cx = Ctx(nc, es, env)
        sc = env.sc if env else Sched(nc, es)
        d_x = cx.din("x", [NT, D], F32)
        d_win = cx.din("w_in", [D, IN_COLS], F32)
        d_wuq = cx.din("w_uq", [384, 768], F32)
        d_wukv = cx.din("w_ukv", [256, 1024], F32)
        d_gmix = cx.din("gmix", [128, 1024], F32)
        d_gqk = cx.din("gqk", [128, 1024], F32)
        d_gcq = cx.din("gcq", [128, 384], F32)
        d_gckv = cx.din("gckv", [128, 256], F32)
        d_gbq = cx.din("gbq", [128, 768], F32)
        d_gbk = cx.din("gbk", [128, 768], F32)
        d_bgate = cx.din("bgate", [128, 2048], F32)
        d_cos = cx.din("cos", [NT, 16], F32)
        d_sin = cx.din("sin", [NT, 16], F32)
        d_ident = cx.din("ident", [128, 128], BF16)
        o_qiT = cx.dout("qiT", [64, NT * 8], BF16)
        o_w2 = cx.dout("w2", [NT, 8], F32)
        o_qaT = cx.dout("qaT", [512, NT], BF16)
        o_qbT = cx.dout("qbT", [768, NT], BF16)
        o_kiT = cx.dout("kiT", [NT // 512 * 64, 512], BF16)
        o_kaT = cx.dout("kaT", [NT, 512], BF16)
        o_va = cx.dout("va", [NT, 520], BF16)
        o_kbT = cx.dout("kbT", [NT // 512 * 768, 512], BF16)
        o_vb = cx.dout("vb", [NT, 520], BF16)
        o_gate = cx.dout("gate", [NT, 2048], F32)

        win = cx.sb([128, 8, IN_COLS], BF16)
        wuq = cx.sb([128, 3, 768], BF16)
        wukv = cx.sb([128, 2, 1024], BF16)
        gmix = cx.sb([128, 1024], F32)
        gqk = cx.sb([128, 1024], F32)
        gcq = cx.sb([128, 384], F32)
        gckv = cx.sb([128, 256], F32)
        gbq = cx.sb([128, 768], F32)
        gbk = cx.sb([128, 768], F32)
        bgate = cx.sb([128, 2048], F32)
        ident = cx.sb([128, 128], BF16)
        for kc in range(8):
            sc.dma(wq(d_win), win[:, kc, :], d_win[kc * 128:(kc + 1) * 128, :], ("w", kc), writes=[("win", kc)])
        sc.dma(wq(d_wuq), wuq[:], d_wuq.rearrange("(k p) c -> p k c", p=128), "wuq", writes=["wuq"])
        sc.dma(wq(d_wukv), wukv[:], d_wukv.rearrange("(k p) c -> p k c", p=128), "wukv", writes=["wukv"])
        for nm, t_, d_ in (("gmix", gmix, d_gmix), ("gqk", gqk, d_gqk), ("gcq", gcq, d_gcq), ("gckv", gckv, d_gckv),
                           ("gbq", gbq, d_gbq), ("gbk", gbk, d_gbk), ("bgate", bgate, d_bgate), ("ident", ident, d_ident)):
            sc.dma("sp", t_[:], d_, nm, writes=[nm])
        sc.op("dve", lambda e: e.tensor_scalar(out=gqk[:, 0:512], in0=gqk[:, 0:512], scalar1=0.125, scalar2=None, op0=ALU.mult),
              reads=["gqk"], writes=["gqk"])
        sc.op("dve", lambda e: e.tensor_scalar(out=gbq[:, :], in0=gbq[:, :], scalar1=96.0 ** -0.5, scalar2=None, op0=ALU.mult),
              reads=["gbq"], writes=["gbq"])

        xt = [cx.sb([128, 1024], F32) for _ in range(2)]
        junkb = cx.sb([128, 1024], BF16)
        xg = cx.sb([128, 1024], BF16)
        xT = cx.sb([128, 1024], BF16)
        zz = [cx.sb([128, IN_COLS], F32) for _ in range(2)]
        tmpf = cx.sb([128, 1024], F32)
        qk_b = cx.sb([128, 1024], BF16)
        qkT = cx.sb([64, 16, 128], BF16)
        va_sb = cx.sb([128, 8, 65], BF16)
        vb_sb = cx.sb([128, 8, 65], BF16)
        qi_b = cx.sb([128, 512], BF16)
        qiT_sb = cx.sb([64, 1024], BF16)
        ki_b = cx.sb([128, 64], BF16)
        kiT_sb = cx.sb([64, 128], BF16)
        w2_sb = cx.sb([128, 8], F32)
        cq_b = cx.sb([128, 384], BF16)
        cqT = cx.sb([128, 3, 128], BF16)
        ckv_b = cx.sb([128, 256], BF16)
        ckvT = cx.sb([128, 2, 128], BF16)
        qb_f = cx.sb([128, 8, 96], F32)
        kv_f = cx.sb([128, 8, 128], F32)
        kb_f = cx.sb([128, 8, 96], F32)
        qb_b = cx.sb([128, 8, 96], BF16)
        kb_b = cx.sb([128, 8, 96], BF16)
        qbT_sb = cx.sb([96, 8, 128], BF16)
        kbT_sb = cx.sb([96, 8, 128], BF16)
        gate_sb = cx.sb([128, 1024], F32)
        cs = cx.sb([128, 32], F32)
        rt = cx.sb([128, 8, 64], F32)
        krr = cx.sb([128, 32], F32)
        st = cx.sb([128, 64], F32)
        ps = [cx.ps([128, 512], F32) for _ in range(8)]
        sc.op("dve", lambda e: e.memset(va_sb[:], 1.0), writes=["va"])
        sc.op("dve", lambda e: e.memset(vb_sb[:], 1.0), writes=["vb"])

        def rs_of(src, dst, dim, rd, wr):
            sc.op("dve", lambda e: e.tensor_scalar(out=dst, in0=src, scalar1=1.0 / dim, scalar2=EPS, op0=ALU.mult, op1=ALU.add),
                  reads=rd, writes=[wr])
            sc.op("act", lambda e: e.activation(out=dst, in_=dst, func=AF.Sqrt), reads=[wr], writes=[wr])
            sc.op("dve", lambda e: e.reciprocal(out=dst, in_=dst), reads=[wr], writes=[wr])

        def transposes(src_fn, n, rows, pbank, rd):
            ptb = ps[pbank][:, :].bitcast(BF16)
            for k in range(n):
                sc.op("pe", lambda e, k=k: e.transpose(ptb[0:rows, k * 128:(k + 1) * 128], src_fn(k), ident[:, :]),
                      reads=rd + ["ident"], writes=[("ps", pbank)])
            return ptb

        ntile = NT // 128
        def front(ti):
                r0 = ti * 128
                par = ti % 2
                z_ = zz[par]
                x_ = xt[ti % 2]
                xk = ("x", ti % 2)
                sc.dma("sp", x_[:], d_x[r0:r0 + 128, :], xk, writes=[xk])
                sc.op("act", lambda e, x_=x_: e.activation(out=junkb[:, :], in_=x_[:, :], func=AF.Square, accum_out=st[:, 60 + 2 * par:61 + 2 * par]),
                      reads=[xk], writes=["junkb", ("ss", par)])
                rs_of(st[:, 60 + 2 * par:61 + 2 * par], st[:, 61 + 2 * par:62 + 2 * par], 1024.0, [("ss", par)], ("rstd", par))
                sc.op("dve", lambda e, x_=x_: e.tensor_tensor(out=xg[:, :], in0=x_[:, :], in1=gmix[:, :], op=ALU.mult),
                      reads=[xk, "gmix"], writes=["xg"])
                ptb = transposes(lambda k: xg[:, k * 128:(k + 1) * 128], 8, 128, 0, ["xg"])
                sc.op("act", lambda e, ptb=ptb: e.activation(out=xT[:, :], in_=ptb[:, 0:1024], func=AF.Copy),
                      reads=[("ps", 0)], writes=["xT"])
                ncol = (IN_COLS + 511) // 512
                for ct in range(ncol):
                    c0 = ct * 512
                    w = min(512, IN_COLS - c0)
                    pb = 1 + (ct % 3)
                    for kc in range(8):
                        sc.op("pe", lambda e, pb=pb, kc=kc, c0=c0, w=w: e.matmul(
                            ps[pb][:, 0:w], xT[:, kc * 128:(kc + 1) * 128], win[:, kc, c0:c0 + w], start=(kc == 0), stop=(kc == 7)),
                            reads=["xT", ("win", kc)], writes=[("ps", pb)])
                    if ct % 2 == 0:
                        sc.op("act", lambda e, pb=pb, c0=c0, w=w: e.activation(out=z_[:, c0:c0 + w], in_=ps[pb][:, 0:w], func=AF.Copy,
                                                                               scale=st[:, 61 + 2 * par:62 + 2 * par]),
                              reads=[("ps", pb), ("rstd", par)], writes=[("z", par, ct)])
                    else:
                        sc.op("dve", lambda e, pb=pb, c0=c0, w=w: e.tensor_scalar(out=z_[:, c0:c0 + w], in0=ps[pb][:, 0:w], scalar1=st[:, 61 + 2 * par:62 + 2 * par],
                                                                                 scalar2=None, op0=ALU.mult),
                              reads=[("ps", pb), ("rstd", par)], writes=[("z", par, ct)])

        def back(ti):
                r0 = ti * 128
                par = ti % 2
                z_ = zz[par]
                sc.dma("sp", cs[:, 0:16], d_cos[r0:r0 + 128, :], "cs", writes=["cos"])
                sc.dma("sp", cs[:, 16:32], d_sin[r0:r0 + 128, :], "cs", writes=["sin"])
                zr = lambda a, b: [("z", par, c) for c in range(a // 512, (b - 1) // 512 + 1)]
                sc.op("act", lambda e: e.activation(out=tmpf[:, 0:1024], in_=z_[:, 0:1024], func=AF.Square),
                      reads=zr(0, 1024), writes=["tmpf"])
                sc.op("dve", lambda e: e.tensor_reduce(out=st[:, 8:24], in_=tmpf[:, 0:1024].rearrange("p (h d) -> p h d", d=64),
                                                       axis=AX.X, op=ALU.add), reads=["tmpf"], writes=["ss16"])
                rs_of(st[:, 8:24], st[:, 24:40], 64.0, ["ss16"], "rs16")
                sc.op("dve", lambda e: e.tensor_tensor(out=tmpf[:, 0:1024].rearrange("p (h d) -> p h d", d=64),
                                                       in0=z_[:, 0:1024].rearrange("p (h d) -> p h d", d=64),
                                                       in1=st[:, 24:40].unsqueeze(2).to_broadcast([128, 16, 64]), op=ALU.mult),
                      reads=zr(0, 1024) + ["rs16", "tmpf"], writes=["tmpf"])
                sc.op("dve", lambda e: e.tensor_tensor(out=qk_b[:, :], in0=tmpf[:, 0:1024], in1=gqk[:, :], op=ALU.mult),
                      reads=["tmpf", "gqk"], writes=["qk_b"])
                for half in range(2):
                    ptb = transposes(lambda k, half=half: qk_b[:, (half * 8 + k) * 64:(half * 8 + k + 1) * 64], 8, 64, 4 + half, ["qk_b"])
                    sc.op("act", lambda e, ptb=ptb, half=half: e.activation(
                        out=qkT[:, half * 8:(half + 1) * 8, :], in_=ptb[0:64, 0:1024].rearrange("p (h t) -> p h t", h=8), func=AF.Copy),
                        reads=[("ps", 4 + half)], writes=[("qkT", half)])
                    if half == 0:
                        dst = o_qaT[:, r0:r0 + 128].rearrange("(h d) t -> d h t", d=64)
                    else:
                        dst = o_kaT[512 * (r0 // 512):512 * (r0 // 512) + 512, r0 % 512:r0 % 512 + 128].rearrange("(h d) t -> d h t", d=64)
                    sc.dma("sp", dst, qkT[:, half * 8:(half + 1) * 8, :], ("oqk", half), reads=[("qkT", half)],
                           writes=([("dr", "kaT", r0 // 512)] if half == 1 else []))
                sc.op("act", lambda e: e.activation(out=va_sb[:, :, 0:64], in_=z_[:, 1024:1536].rearrange("p (h d) -> p h d", d=64), func=AF.Copy),
                      reads=zr(1024, 1536) + ["va"], writes=["va"])
                sc.dma("sp", o_va[r0:r0 + 128, :], va_sb[:].rearrange("p h d -> p (h d)"), "ova", reads=["va"], writes=[("dr", "va", r0 // 512)])
                sc.op("act", lambda e: e.activation(out=qi_b[:, :], in_=z_[:, 1536:2048], func=AF.Copy), reads=zr(1536, 2048), writes=["qi_b"])
                ptb = transposes(lambda k: qi_b[:, k * 64:(k + 1) * 64], 8, 64, 6, ["qi_b"])
                sc.op("act", lambda e, ptb=ptb: e.activation(out=qiT_sb[:, :].rearrange("p (t h) -> p h t", h=8),
                                                            in_=ptb[0:64, 0:1024].rearrange("p (h t) -> p h t", h=8), func=AF.Copy),
                      reads=[("ps", 6)], writes=["qiT"])
                sc.dma("sp", o_qiT[:, r0 * 8:(r0 + 128) * 8], qiT_sb[:, :], "oqi", reads=["qiT"])
                sc.op("act", lambda e: e.activation(out=ki_b[:, :], in_=z_[:, 2048:2112], func=AF.Copy), reads=zr(2048, 2112), writes=["ki_b"])
                ptb = transposes(lambda k: ki_b[:, :], 1, 64, 7, ["ki_b"])
                sc.op("act", lambda e, ptb=ptb: e.activation(out=kiT_sb[:, :], in_=ptb[0:64, 0:128], func=AF.Copy),
                      reads=[("ps", 7)], writes=["kiT"])
                sc.dma("sp", o_kiT[64 * (r0 // 512):64 * (r0 // 512) + 64, r0 % 512:r0 % 512 + 128], kiT_sb[:, :], "oki", reads=["kiT"],
                       writes=[("dr", "kiT", r0 // 512)])
                sc.op("dve", lambda e: e.tensor_scalar(out=w2_sb[:, :], in0=z_[:, 2112:2120], scalar1=(8 ** -0.5) * (64 ** -0.5), scalar2=None,
                                                       op0=ALU.mult), reads=zr(2112, 2120), writes=["w2"])
                sc.dma("sp", o_w2[r0:r0 + 128, :], w2_sb[:, :], "ow2", reads=["w2"])
                sc.op("act", lambda e: e.activation(out=junkb[:, 0:384], in_=z_[:, 2120:2504], func=AF.Square, accum_out=st[:, 2:3]),
                      reads=zr(2120, 2504) + ["junkb"], writes=["junkb", "sscq"])
                rs_of(st[:, 2:3], st[:, 3:4], 384.0, ["sscq"], "rscq")
                sc.op("dve", lambda e: e.scalar_tensor_tensor(out=cq_b[:, :], in0=z_[:, 2120:2504], scalar=st[:, 3:4], in1=gcq[:, :],
                                                              op0=ALU.mult, op1=ALU.mult), reads=zr(2120, 2504) + ["rscq", "gcq"], writes=["cq_b"])
                ptb = transposes(lambda k: cq_b[:, k * 128:(k + 1) * 128], 3, 128, 4, ["cq_b"])
                sc.op("act", lambda e, ptb=ptb: e.activation(out=cqT[:, :, :], in_=ptb[:, 0:384].rearrange("p (k t) -> p k t", k=3), func=AF.Copy),
                      reads=[("ps", 4)], writes=["cqT"])
                for (c0, w, pb) in ((0, 512, 5), (512, 256, 6)):
                    for kc in range(3):
                        sc.op("pe", lambda e, pb=pb, kc=kc, c0=c0, w=w: e.matmul(ps[pb][:, 0:w], cqT[:, kc, :], wuq[:, kc, c0:c0 + w],
                                                                                 start=(kc == 0), stop=(kc == 2)),
                              reads=["cqT", "wuq"], writes=[("ps", pb)])
                    sc.op("act", lambda e, pb=pb, c0=c0, w=w: e.activation(out=qb_f[:].rearrange("p h d -> p (h d)")[:, c0:c0 + w],
                                                                           in_=ps[pb][:, 0:w], func=AF.Copy),
                          reads=[("ps", pb)], writes=[("qb_f", c0)])
                sc.op("act", lambda e: e.activation(out=junkb[:, 0:256], in_=z_[:, 2504:2760], func=AF.Square, accum_out=st[:, 4:5]),
                      reads=zr(2504, 2760) + ["junkb"], writes=["junkb", "ssckv"])
                rs_of(st[:, 4:5], st[:, 5:6], 256.0, ["ssckv"], "rsckv")
                sc.op("dve", lambda e: e.scalar_tensor_tensor(out=ckv_b[:, :], in0=z_[:, 2504:2760], scalar=st[:, 5:6], in1=gckv[:, :],
                                                              op0=ALU.mult, op1=ALU.mult), reads=zr(2504, 2760) + ["rsckv", "gckv"], writes=["ckv_b"])
                ptb = transposes(lambda k: ckv_b[:, k * 128:(k + 1) * 128], 2, 128, 7, ["ckv_b"])
                sc.op("act", lambda e, ptb=ptb: e.activation(out=ckvT[:, :, :], in_=ptb[:, 0:256].rearrange("p (k t) -> p k t", k=2), func=AF.Copy),
                      reads=[("ps", 7)], writes=["ckvT"])
                for (c0, pb) in ((0, 4), (512, 5)):
                    for kc in range(2):
                        sc.op("pe", lambda e, pb=pb, kc=kc, c0=c0: e.matmul(ps[pb][:, 0:512], ckvT[:, kc, :], wukv[:, kc, c0:c0 + 512],
                                                                            start=(kc == 0), stop=(kc == 1)),
                              reads=["ckvT", "wukv"], writes=[("ps", pb)])
                    sc.op("act", lambda e, pb=pb, c0=c0: e.activation(out=kv_f[:].rearrange("p h d -> p (h d)")[:, c0:c0 + 512],
                                                                      in_=ps[pb][:, 0:512], func=AF.Copy),
                          reads=[("ps", pb)], writes=[("kv_f", c0)])
                cosb = cs[:, 0:16].unsqueeze(1).to_broadcast([128, 8, 16])
                sinb = cs[:, 16:32].unsqueeze(1).to_broadcast([128, 8, 16])
                qbr = [("qb_f", 0), ("qb_f", 512)]
                x1 = qb_f[:, :, 64:80]
                x2 = qb_f[:, :, 80:96]
                for k_, (a_, b_) in enumerate(((x1, cosb), (x2, sinb), (x1, sinb), (x2, cosb))):
                    sc.op("dve", lambda e, k_=k_, a_=a_, b_=b_: e.tensor_tensor(out=rt[:, :, k_ * 16:(k_ + 1) * 16], in0=a_, in1=b_, op=ALU.mult),
                          reads=qbr + ["cos", "sin"], writes=[("rt", k_)])
                sc.op("dve", lambda e: e.tensor_tensor(out=qb_f[:, :, 64:80], in0=rt[:, :, 0:16], in1=rt[:, :, 16:32], op=ALU.subtract),
                      reads=[("rt", 0), ("rt", 1)] + qbr, writes=qbr)
                sc.op("dve", lambda e: e.tensor_tensor(out=qb_f[:, :, 80:96], in0=rt[:, :, 32:48], in1=rt[:, :, 48:64], op=ALU.add),
                      reads=[("rt", 2), ("rt", 3)] + qbr, writes=qbr)
                k1 = z_[:, 2760:2776]
                k2 = z_[:, 2776:2792]
                for k_, (a_, b_) in enumerate(((k1, cs[:, 0:16]), (k2, cs[:, 16:32]), (k1, cs[:, 16:32]), (k2, cs[:, 0:16]))):
                    sc.op("dve", lambda e, k_=k_, a_=a_, b_=b_: e.tensor_tensor(out=rt[:, 0, k_ * 16:(k_ + 1) * 16], in0=a_, in1=b_, op=ALU.mult),
                          reads=zr(2760, 2792) + ["cos", "sin"] + [("rt", q) for q in range(4)], writes=[("rt", k_)])
                sc.op("dve", lambda e: e.tensor_tensor(out=krr[:, 0:16], in0=rt[:, 0, 0:16], in1=rt[:, 0, 16:32], op=ALU.subtract),
                      reads=[("rt", 0), ("rt", 1)], writes=["krr"])
                sc.op("dve", lambda e: e.tensor_tensor(out=krr[:, 16:32], in0=rt[:, 0, 32:48], in1=rt[:, 0, 48:64], op=ALU.add),
                      reads=[("rt", 2), ("rt", 3), "krr"], writes=["krr"])
                kvr = [("kv_f", 0), ("kv_f", 512)]
                sc.op("act", lambda e: e.activation(out=kb_f[:, :, 0:64], in_=kv_f[:, :, 0:64], func=AF.Copy), reads=kvr, writes=["kb_f"])
                sc.op("dve", lambda e: e.tensor_copy(out=kb_f[:, :, 64:96], in_=krr[:, :].unsqueeze(1).to_broadcast([128, 8, 32])),
                      reads=["krr", "kb_f"], writes=["kb_f"])
                sc.op("act", lambda e: e.activation(out=vb_sb[:, :, 0:64], in_=kv_f[:, :, 64:128], func=AF.Copy), reads=kvr + ["vb"], writes=["vb"])
                sc.dma("sp", o_vb[r0:r0 + 128, :], vb_sb[:].rearrange("p h d -> p (h d)"), "ovb", reads=["vb"], writes=[("dr", "vb", r0 // 512)])
                for nm, src, rdk, gt, dstb, dstT, outd, pbk in (("q", qb_f, qbr, gbq, qb_b, qbT_sb, o_qbT, 6), ("k", kb_f, ["kb_f"], gbk, kb_b, kbT_sb, o_kbT, 7)):
                    flat = src[:].rearrange("p h d -> p (h d)")
                    sc.op("act", lambda e, flat=flat: e.activation(out=tmpf[:, 0:768], in_=flat, func=AF.Square), reads=rdk + ["tmpf"], writes=["tmpf"])
                    sc.op("dve", lambda e: e.tensor_reduce(out=st[:, 40:48], in_=tmpf[:, 0:768].rearrange("p (h d) -> p h d", d=96),
                                                           axis=AX.X, op=ALU.add), reads=["tmpf"], writes=["ss8"])
                    rs_of(st[:, 40:48], st[:, 48:56], 96.0, ["ss8"], "rs8")
                    sc.op("dve", lambda e, src=src: e.tensor_tensor(out=tmpf[:, 0:768].rearrange("p (h d) -> p h d", d=96), in0=src[:, :, :],
                                                                   in1=st[:, 48:56].unsqueeze(2).to_broadcast([128, 8, 96]), op=ALU.mult),
                          reads=rdk + ["rs8", "tmpf"], writes=["tmpf"])
                    sc.op("dve", lambda e, dstb=dstb, gt=gt: e.tensor_tensor(out=dstb[:].rearrange("p h d -> p (h d)"), in0=tmpf[:, 0:768], in1=gt[:, :],
                                                                            op=ALU.mult), reads=["tmpf", "gb" + nm], writes=["b_" + nm])
                    ptb = transposes(lambda k, dstb=dstb: dstb[:, k, :], 8, 96, pbk, ["b_" + nm])
                    sc.op("act", lambda e, ptb=ptb, dstT=dstT: e.activation(out=dstT[:, :, :], in_=ptb[0:96, 0:1024].rearrange("p (h t) -> p h t", h=8),
                                                                           func=AF.Copy), reads=[("ps", pbk)], writes=["T_" + nm])
                    if nm == "q":
                        dstd = outd[:, r0:r0 + 128]
                    else:
                        dstd = outd[768 * (r0 // 512):768 * (r0 // 512) + 768, r0 % 512:r0 % 512 + 128]
                    sc.dma("sp", dstd.rearrange("(h d) t -> d h t", d=96), dstT[:, :, :], "oT" + nm, reads=["T_" + nm],
                           writes=([("dr", "kbT", r0 // 512)] if nm == "k" else []))
                for gh in range(2):
                    sc.op("dve", lambda e, gh=gh: e.tensor_tensor(out=tmpf[:, :], in0=z_[:, 2792 + 1024 * gh:2792 + 1024 * (gh + 1)],
                                                                 in1=bgate[:, 1024 * gh:1024 * (gh + 1)], op=ALU.add),
                          reads=zr(2792, 4840) + ["bgate", "tmpf"], writes=["tmpf"])
                    sc.op("act", lambda e: e.activation(out=gate_sb[:, :], in_=tmpf[:, :], func=AF.Sigmoid), reads=["tmpf"], writes=["gate"])
                    sc.dma("sp", o_gate[r0:r0 + 128, 1024 * gh:1024 * (gh + 1)], gate_sb[:, :], "ogate", reads=["gate"])
                if env is not None and getattr(env, "hook_sb", None) and ti % 4 == 3:
                    env.hook_sb(ti // 4)


        front(0)
        for ti in range(ntile):
            fr = sc.record(lambda: front(ti + 1)) if ti + 1 < ntile else []
            bk = sc.record(lambda: back(ti))
            sc.interleave(fr, bk)
        sc.emit(skip_cc=bool(env is not None and getattr(env, "skip_cc", False)))
        print("A program: ops", sc.nops, "waits", sc.nwaits)
    return nc


def rope_tables(pos):
    inv = (np.float32(10000.0) ** (-np.arange(16, dtype=np.float32) / np.float32(16))).astype(np.float32)
    ang = pos.astype(np.float32)[:, None] * inv[None, :]
    return np.cos(ang).astype(np.float32), np.sin(ang).astype(np.float32)


def rep(v, n=128):
    return np.ascontiguousarray(np.broadcast_to(np.asarray(v, np.float32)[None, :], (n, v.shape[0])))


def shard_A(x, P, l, NI):
    maps = []
    ident = np.eye(128, dtype=np.float32).astype(NPBF)
    for c in range(NCORE):
        b, j = c // 4, c % 4
        tok = core_tokens(NI, j)
        cos, sin = rope_tables(tok)
        m = dict(x=np.ascontiguousarray(x[b][tok]), w_in=P["w_in"][l], w_uq=P["b_w_uq"][l], w_ukv=P["b_w_ukv"][l],
                 gmix=rep(P["norm_mix"][l]),
                 gqk=rep(np.concatenate([np.tile(P["a_q_norm"][l], 8), np.tile(P["a_k_norm"][l], 8)])),
                 gcq=rep(P["b_cq_norm"][l]), gckv=rep(P["b_ckv_norm"][l]),
                 gbq=rep(np.tile(P["b_q_norm"][l], 8)), gbk=rep(np.tile(P["b_k_norm"][l], 8)),
                 bgate=rep(P["b_gate"][l]), cos=cos, sin=sin, ident=ident)
        maps.append(m)
    return maps


def gather_A(results, S, NI):
    B = 2
    G = dict(qiT=np.zeros((B, 64, S * 8), NPBF), w2=np.zeros((B, S, 8), np.float32), qaT=np.zeros((B, 8, 64, S), NPBF),
             qbT=np.zeros((B, 8, 96, S), NPBF), kiT=np.zeros((B, 64, S), NPBF), kaT=np.zeros((B, 8, 64, S), NPBF),
             va=np.zeros((B, S, 520), NPBF), kbT=np.zeros((B, 8, 96, S), NPBF), vb=np.zeros((B, S, 520), NPBF),
             gate=np.zeros((B, S, 2048), np.float32))
    for c in range(NCORE):
        b, j = c // 4, c % 4
        tok = core_tokens(NI, j)
        r = results[c]
        G["qiT"][b].reshape(64, S, 8)[:, tok, :] = r["qiT"].reshape(64, len(tok), 8)
        G["w2"][b][tok] = r["w2"]
        for k in ("qaT", "qbT"):
            G[k][b][:, :, tok] = r[k].reshape(8, -1, len(tok))
        for k, dh in (("kaT", 64), ("kbT", 96)):
            G[k][b][:, :, tok] = r[k].reshape(NI, 8, dh, SB).transpose(1, 2, 0, 3).reshape(8, dh, len(tok))
        G["kiT"][b][:, tok] = r["kiT"].reshape(NI, 64, SB).transpose(1, 0, 2).reshape(64, len(tok))
        for k in ("va", "vb", "gate"):
            G[k][b][tok] = r[k]
    return G


def build_C1(NT, env=None):
    nc = env.nc if env else bass.Bass("TRN2", target_bir_lowering=False)
    es = ExitStack()
    with es:
        cx = Ctx(nc, es, env)
        sc = env.sc if env else Sched(nc, es)
        d_x = cx.din("x", [NT, D], F32)
        d_yaT = cx.din("yaT", [512, NT], BF16)
        d_ybT = cx.din("ybT", [512, NT], BF16)
        d_gate = cx.din("gate", [NT, 2048], F32)
        d_wpa = cx.din("w_pa", [512, D], F32)
        d_wpb = cx.din("w_pb", [512, D], F32)
        d_wout = cx.din("w_out", [D, D], F32)
        d_ident = cx.din("ident", [128, 128], BF16)
        o_x = cx.dout("xmid", [NT, D], F32)
        wpa = cx.sb([64, 8, D], BF16)
        wpb = cx.sb([64, 8, D], BF16)
        wout = cx.sb([128, 8, D], BF16)
        ident = cx.sb([128, 128], BF16)
        sc.dma(wq(d_wpa), wpa[:], d_wpa.rearrange("(h d) c -> d h c", d=64), "wpa", writes=["wpa"])
        sc.dma(wq(d_wpb), wpb[:], d_wpb.rearrange("(h d) c -> d h c", d=64), "wpb", writes=["wpb"])
        sc.dma(wq(d_wout), wout[:], d_wout.rearrange("(k p) c -> p k c", p=128), "wout", writes=["wout"])
        sc.dma("sp", ident[:], d_ident, "ident", writes=["ident"])
        yT = [[cx.sb([64, 8, 128], BF16) for _ in range(2)] for _ in range(2)]
        gt = [cx.sb([128, 2048], F32) for _ in range(2)]
        xt = [cx.sb([128, D], F32) for _ in range(2)]
        t1 = cx.sb([128, D], F32)
        t2 = cx.sb([128, D], F32)
        mb = cx.sb([128, D], BF16)
        mT = cx.sb([128, D], BF16)
        xo = [cx.sb([128, D], F32) for _ in range(2)]
        ps = [cx.ps([128, 512], F32) for _ in range(8)]
        for ti in range(NT // 128):
            r0 = ti * 128
            s = ti % 2
            sc.dma("sp", yT[0][s][:], d_yaT[:, r0:r0 + 128].rearrange("(h d) t -> d h t", d=64), ("ld", s), writes=[("ya", s)])
            sc.dma("sp", yT[1][s][:], d_ybT[:, r0:r0 + 128].rearrange("(h d) t -> d h t", d=64), ("ld", s), writes=[("yb", s)])
            sc.dma("sp", gt[s][:], d_gate[r0:r0 + 128, :], ("ld", s), writes=[("g", s)])
            sc.dma("sp", xt[s][:], d_x[r0:r0 + 128, :], ("ld", s), writes=[("x", s)])
            for br, (w_, wn, yk) in enumerate(((wpa, "wpa", "ya"), (wpb, "wpb", "yb"))):
                for ct in range(2):
                    pb = br * 2 + ct
                    for h in range(8):
                        sc.op("pe", lambda e, pb=pb, h=h, w_=w_, br=br, ct=ct, s=s: e.matmul(
                            ps[pb][:, :], yT[br][s][:, h, :], w_[:, h, ct * 512:(ct + 1) * 512], start=(h == 0), stop=(h == 7)),
                            reads=[(yk, s), wn], writes=[("ps", pb)])
            for ct in range(2):
                cs_ = slice(ct * 512, (ct + 1) * 512)
                sc.op("dve", lambda e, ct=ct, cs_=cs_, s=s: e.tensor_tensor(out=t1[:, cs_], in0=ps[ct][:, :], in1=gt[s][:, cs_], op=ALU.mult),
                      reads=[("ps", ct), ("g", s)], writes=[("t1", ct)])
                sc.op("dve", lambda e, ct=ct, cs_=cs_, s=s: e.tensor_tensor(out=t2[:, cs_], in0=ps[2 + ct][:, :],
                                                                           in1=gt[s][:, 1024 + ct * 512:1024 + (ct + 1) * 512], op=ALU.mult),
                      reads=[("ps", 2 + ct), ("g", s)], writes=[("t2", ct)])
                sc.op("dve", lambda e, cs_=cs_: e.tensor_tensor(out=mb[:, cs_], in0=t1[:, cs_], in1=t2[:, cs_], op=ALU.add),
                      reads=[("t1", ct), ("t2", ct)], writes=[("mb", ct)])
            ptb = ps[4][:, :].bitcast(BF16)
            for k in range(8):
                sc.op("pe", lambda e, k=k, ptb=ptb: e.transpose(ptb[:, k * 128:(k + 1) * 128], mb[:, k * 128:(k + 1) * 128], ident[:, :]),
                      reads=[("mb", k // 4), "ident"], writes=[("ps", 4)])
            sc.op("act", lambda e, ptb=ptb: e.activation(out=mT[:, :], in_=ptb[:, 0:1024], func=AF.Copy), reads=[("ps", 4)], writes=["mT"])
            for ct in range(2):
                pb = 5 + ct
                for kc in range(8):
                    sc.op("pe", lambda e, pb=pb, kc=kc, ct=ct: e.matmul(ps[pb][:, :], mT[:, kc * 128:(kc + 1) * 128],
                                                                       wout[:, kc, ct * 512:(ct + 1) * 512], start=(kc == 0), stop=(kc == 7)),
                          reads=["mT", "wout"], writes=[("ps", pb)])
                sc.op("dve", lambda e, pb=pb, ct=ct, s=s: e.tensor_tensor(out=xo[s][:, ct * 512:(ct + 1) * 512], in0=ps[pb][:, :],
                                                                         in1=xt[s][:, ct * 512:(ct + 1) * 512], op=ALU.add),
                      reads=[("ps", pb), ("x", s)], writes=[("xo", s, ct)])
            sc.dma("sp", o_x[r0:r0 + 128, :], xo[s][:, :], ("st", s), reads=[("xo", s, 0), ("xo", s, 1)])
        sc.emit()
        print("C1 program: ops", sc.nops, "waits", sc.nwaits)
    return nc


def build_C2(NI, env=None):
    NT = NI * SB
    NCT = 2 * D_FF // 128
    NV = D_FF // 128
    nc = env.nc if env else bass.Bass("TRN2", target_bir_lowering=False)
    es = ExitStack()
    with es:
        cx = Ctx(nc, es, env)
        sc = env.sc if env else Sched(nc, es)
        d_x = cx.din("xmid", [NT, D], F32)
        d_halo = cx.din("halo", [NI, 2, D], F32)
        d_g = cx.din("gffn", [128, D], F32)
        d_wup = cx.din("w_up", [D, 2 * D_FF], F32)
        d_wdn = cx.din("w_down", [D_FF, D], F32)
        d_cw = cx.din("convw", [128, NCT, 3], F32)
        d_cb = cx.din("convb", [128, NCT], F32)
        d_ident = cx.din("ident", [128, 128], BF16)
        o_x = cx.dout("xout", [NT, D], F32)
        gffn = cx.sb([128, D], F32)
        wdn = cx.sb([128, NV, D], BF16)
        cw = cx.sb([128, NCT, 3], F32)
        cb = cx.sb([128, NCT], F32)
        ident = cx.sb([128, 128], BF16)
        sc.dma("sp", gffn[:], d_g, "gffn", writes=["gffn"])
        sc.dma("sp", cw[:], d_cw, "cw", writes=["cw"])
        sc.dma("sp", cb[:], d_cb, "cb", writes=["cb"])
        sc.dma("sp", ident[:], d_ident, "ident", writes=["ident"])
        for v in range(NV):
            sc.dma(wq(d_wdn), wdn[:, v, :], d_wdn[v * 128:(v + 1) * 128, :], ("wdn", v % 2), writes=[("wdn", v)])
        xt = [cx.sb([128, D], F32) for _ in range(4)]
        xh = cx.sb([2, D], F32)
        junkb = cx.sb([128, D], BF16)
        hb = cx.sb([128, D], BF16)
        h2T = cx.sb([128, 8, 514], BF16)
        wup = [[cx.sb([128, 8, 128], BF16) for _ in range(2)] for _ in range(2)]
        u2 = [[cx.sb([128, 514], F32) for _ in range(2)] for _ in range(2)]
        acc2 = [[cx.sb([128, 512], F32) for _ in range(2)] for _ in range(2)]
        sg2 = [cx.sb([128, 512], F32) for _ in range(2)]
        aT = cx.sb([128, NV, 512], BF16)
        xo = [cx.sb([128, D], F32) for _ in range(2)]
        st = cx.sb([128, 8], F32)
        ps = [cx.ps([128, 512], F32) for _ in range(8)]
        wslot = 0
        for i in range(NI):
            tok0 = i * SB
            sc.dma("sp", xh[:, :], d_halo[i, :, :], "xh", writes=["xh"])
            tiles = [("h", xh, 2, 0)] + [(k, xt[k], 128, 2 + 128 * k) for k in range(4)]
            for (tk, xx, np_, c0) in tiles:
                key = ("xt", tk)
                if tk != "h":
                    sc.dma("sp", xx[:, :], d_x[tok0 + tk * 128:tok0 + (tk + 1) * 128, :], key, writes=[key])
                else:
                    key = "xh"
                sc.op("act", lambda e, xx=xx, np_=np_: e.activation(out=junkb[0:np_, :], in_=xx[0:np_, :], func=AF.Square, accum_out=st[0:np_, 0:1]),
                      reads=[key, "junkb"], writes=["junkb", "ss"])
                sc.op("dve", lambda e, np_=np_: e.tensor_scalar(out=st[0:np_, 1:2], in0=st[0:np_, 0:1], scalar1=1.0 / D, scalar2=EPS,
                                                               op0=ALU.mult, op1=ALU.add), reads=["ss"], writes=["rs"])
                sc.op("act", lambda e, np_=np_: e.activation(out=st[0:np_, 1:2], in_=st[0:np_, 1:2], func=AF.Sqrt), reads=["rs"], writes=["rs"])
                sc.op("dve", lambda e, np_=np_: e.reciprocal(out=st[0:np_, 1:2], in_=st[0:np_, 1:2]), reads=["rs"], writes=["rs"])
                sc.op("dve", lambda e, xx=xx, np_=np_: e.scalar_tensor_tensor(out=hb[0:np_, :], in0=xx[0:np_, :], scalar=st[0:np_, 1:2],
                                                                             in1=gffn[0:np_, :], op0=ALU.mult, op1=ALU.mult),
                      reads=[key, "rs", "gffn"], writes=["hb"])
                ptb = ps[7][:, :].bitcast(BF16)
                for k in range(8):
                    sc.op("pe", lambda e, k=k, ptb=ptb, np_=np_: e.transpose(ptb[:, k * 128:k * 128 + np_], hb[0:np_, k * 128:(k + 1) * 128],
                                                                           ident[0:np_, 0:np_]),
                          reads=["hb", "ident"], writes=[("ps", 7)])
                sc.op("act", lambda e, ptb=ptb, np_=np_, c0=c0: e.activation(
                    out=h2T[:, :, c0:c0 + np_], in_=ptb[:, 0:1024].rearrange("p (k t) -> p k t", k=8)[:, :, 0:np_], func=AF.Copy),
                    reads=[("ps", 7)], writes=[("h2T", tk)])
            h2r = [("h2T", tk) for tk in ("h", 0, 1, 2, 3)]
            for v in range(NV):
                ws = wslot % 2
                wslot += 1
                par = v % 2
                u = [(par, 0), (par, 1)]
                for gi, ct in enumerate((v, NV + v)):
                    sc.dma(wq(d_wup), wup[ws][gi][:], d_wup[:, ct * 128:(ct + 1) * 128].rearrange("(k p) c -> p k c", p=128),
                           ("wup", ws, gi), writes=[("wup", ws, gi)])
                for gi, ct in enumerate((v, NV + v)):
                    pm = 3 * par + gi
                    ph = 3 * par + 2
                    hc = 2 * gi
                    ug = u2[par][gi]
                    ag = acc2[par][gi]
                    for kc in range(8):
                        sc.op("pe", lambda e, pm=pm, kc=kc, ws=ws, gi=gi: e.matmul(ps[pm][:, :], wup[ws][gi][:, kc, :], h2T[:, kc, 2:514],
                                                                                  start=(kc == 0), stop=(kc == 7)),
                              reads=[("wup", ws, gi)] + h2r, writes=[("ps", pm)])
                    for kc in range(8):
                        sc.op("pe", lambda e, ph=ph, kc=kc, ws=ws, gi=gi, hc=hc: e.matmul(ps[ph][:, hc:hc + 2], wup[ws][gi][:, kc, :], h2T[:, kc, 0:2],
                                                                                  start=(kc == 0), stop=(kc == 7)),
                              reads=[("wup", ws, gi)] + h2r, writes=[("ps", ph)])
                    uk = ("u", par, gi)
                    ak = ("acc", par, gi)
                    sc.op("act", lambda e, pm=pm, ug=ug: e.activation(out=ug[:, 2:514], in_=ps[pm][:, :], func=AF.Copy),
                          reads=[("ps", pm)], writes=[uk])
                    sc.op("act", lambda e, ph=ph, ug=ug, hc=hc: e.activation(out=ug[:, 0:2], in_=ps[ph][:, hc:hc + 2], func=AF.Copy),
                          reads=[("ps", ph), uk], writes=[uk])
                    sc.op("dve", lambda e, ug=ug, ag=ag, ct=ct: e.tensor_scalar(out=ag[:, :], in0=ug[:, 2:514], scalar1=cw[:, ct, 2:3],
                                                                               scalar2=cb[:, ct:ct + 1], op0=ALU.mult, op1=ALU.add),
                          reads=[uk, "cw", "cb"], writes=[ak])
                    sc.op("dve", lambda e, ug=ug, ag=ag, ct=ct: e.scalar_tensor_tensor(out=ag[:, :], in0=ug[:, 1:513], scalar=cw[:, ct, 1:2],
                                                                                      in1=ag[:, :], op0=ALU.mult, op1=ALU.add),
                          reads=[uk, "cw", ak], writes=[ak])
                    sc.op("dve", lambda e, ug=ug, ag=ag, ct=ct: e.scalar_tensor_tensor(out=ag[:, :], in0=ug[:, 0:512], scalar=cw[:, ct, 0:1],
                                                                                      in1=ag[:, :], op0=ALU.mult, op1=ALU.add),
                          reads=[uk, "cw", ak], writes=[ak])
                sc.op("act", lambda e, par=par: e.activation(out=sg2[par][:, :], in_=acc2[par][1][:, :], func=AF.Silu),
                      reads=[("acc", par, 1)], writes=[("sg", par)])
                sc.op("pool", lambda e, v=v, par=par: e.tensor_tensor(out=aT[:, v, :], in0=sg2[par][:, :], in1=acc2[par][0][:, :], op=ALU.mult),
                      reads=[("sg", par), ("acc", par, 0)], writes=[("aT", v)])
            for tk in range(4):
                s = tk % 2
                for ct in range(2):
                    pb = 6 + ct
                    for v in range(NV):
                        sc.op("pe", lambda e, pb=pb, v=v, tk=tk, ct=ct: e.matmul(ps[pb][:, :], aT[:, v, tk * 128:(tk + 1) * 128],
                                                                                wdn[:, v, ct * 512:(ct + 1) * 512], start=(v == 0), stop=(v == NV - 1)),
                              reads=[("aT", v), ("wdn", v)], writes=[("ps", pb)])
                    sc.op("dve", lambda e, pb=pb, ct=ct, tk=tk, s=s: e.tensor_tensor(out=xo[s][:, ct * 512:(ct + 1) * 512], in0=ps[pb][:, :],
                                                                                   in1=xt[tk][:, ct * 512:(ct + 1) * 512], op=ALU.add),
                          reads=[("ps", pb), ("xt", tk)], writes=[("xo", s, ct)])
                sc.dma("sp", o_x[tok0 + tk * 128:tok0 + (tk + 1) * 128, :], xo[s][:, :], ("st", s), reads=[("xo", s, 0), ("xo", s, 1)])
        sc.emit()
        print("C2 program: ops", sc.nops, "waits", sc.nwaits)
    return nc


def maps_C1(xloc, rB, rA, P, l):
    ident = np.eye(128, dtype=np.float32).astype(NPBF)
    return [dict(x=xloc[c], yaT=rB[c]["yaT"], ybT=rB[c]["ybT"], gate=rA[c]["gate"],
                 w_pa=P["w_proj_a"][l], w_pb=P["w_proj_b"][l], w_out=P["w_out"][l], ident=ident) for c in range(NCORE)]


def shard_C1(x, yaT, ybT, gate, P, l, NI):
    xloc, rB, rA = [], [], []
    for c in range(NCORE):
        b, j = c // 4, c % 4
        tok = core_tokens(NI, j)
        xloc.append(np.ascontiguousarray(x[b][tok]))
        rB.append(dict(yaT=np.ascontiguousarray(yaT[b][:, :, tok].reshape(512, -1)), ybT=np.ascontiguousarray(ybT[b][:, :, tok].reshape(512, -1))))
        rA.append(dict(gate=np.ascontiguousarray(gate[b][tok])))
    return maps_C1(xloc, rB, rA, P, l)


def shard_C2(xmid, P, l, NI):
    ident = np.eye(128, dtype=np.float32).astype(NPBF)
    NCT = 2 * D_FF // 128
    cw = np.ascontiguousarray(np.asarray(P["conv_w"][l], np.float32).reshape(3, NCT, 128).transpose(2, 1, 0))
    cb = np.ascontiguousarray(np.asarray(P["conv_b"][l], np.float32).reshape(NCT, 128).T)
    maps = []
    for c in range(NCORE):
        b, j = c // 4, c % 4
        tok = core_tokens(NI, j)
        halo = np.zeros((NI, 2, D), np.float32)
        for i in range(NI):
            g0 = SB * (4 * i + j)
            if g0 >= 2:
                halo[i] = xmid[b][g0 - 2:g0]
        maps.append(dict(xmid=np.ascontiguousarray(xmid[b][tok]), halo=halo, gffn=rep(P["norm_ffn"][l]),
                         w_up=P["w_up"][l], w_down=P["w_down"][l], convw=cw, convb=cb, ident=ident))
    return maps


def gather_tok(results, key, S, NI):
    out = np.zeros((2, S, D), np.float32)
    for c in range(NCORE):
        b, j = c // 4, c % 4
        out[b][core_tokens(NI, j)] = results[c][key]
    return out


_PROG = {}


def _prog(key, fn, *args):
    if key not in _PROG:
        _PROG[key] = fn(*args)
    return _PROG[key]


def _run(nc, maps):
    return run_bass_kernel_spmd(nc, maps, core_ids=list(range(NCORE))).results


def kernel_unfused(**inputs):
    P = {k: np.asarray(v) for k, v in inputs.items()}
    x = np.asarray(P["x"], np.float32)
    S = x.shape[1]
    NI = S // (4 * SB)
    NT = NI * SB
    for l in range(2):
        rA = _run(_prog(("A", NT), build_A, NT), shard_A(x, P, l, NI))
        rB = _run(_prog(("B", S, NI), build_B, S, NI), maps_B(rA, P["rel_bias"], NI))
        xloc = [np.ascontiguousarray(x[c // 4][core_tokens(NI, c % 4)]) for c in range(NCORE)]
        xmid = gather_tok(_run(_prog(("C1", NT), build_C1, NT), maps_C1(xloc, rB, rA, P, l)), "xmid", S, NI)
        x = gather_tok(_run(_prog(("C2", NI), build_C2, NI), shard_C2(xmid, P, l, NI)), "xout", S, NI)
    return x.astype(np.float32)


GROUPS = [[0, 1, 2, 3], [4, 5, 6, 7]]
NCT_ = 2 * D_FF // 128


def build_fused(S, NI):
    NT = NI * SB
    NH2 = NI * 2
    nc = bass.Bass("TRN2", target_bir_lowering=False)
    es = ExitStack()
    with es:
        sc = Sched(nc, es)
        env = Env(nc, sc)

        def ein(name, shape, dt):
            return nc.dram_tensor(name, list(shape), dt, kind="ExternalInput").ap()

        def scr(name, shape, dt):
            return nc.dram_tensor(name, list(shape), dt).ap()

        X = dict(
            x=ein("x", [NT, D], F32), w_in=ein("w_in", [2, D, IN_COLS], F32), w_uq=ein("w_uq", [2, 384, 768], F32),
            w_ukv=ein("w_ukv", [2, 256, 1024], F32), gmix=ein("gmix", [2, 128, 1024], F32), gqk=ein("gqk", [2, 128, 1024], F32),
            gcq=ein("gcq", [2, 128, 384], F32), gckv=ein("gckv", [2, 128, 256], F32), gbq=ein("gbq", [2, 128, 768], F32),
            gbk=ein("gbk", [2, 128, 768], F32), bgate=ein("bgate", [2, 128, 2048], F32), cos=ein("cos", [NT, 16], F32),
            sin=ein("sin", [NT, 16], F32), ident=ein("ident", [128, 128], BF16),
            mg=ein("mg", [128, 8, 128], BF16), diagi=ein("diagi", [128, 4, 512], BF16), selcol=ein("selcol", [128, 8], F32),
            sel=ein("sel", [128, 8, 128], BF16), negrow=ein("negrow", [1, 4, 128], BF16), onesrow=ein("onesrow", [1, 512], BF16),
            bd=ein("bd", [128, 4, 512], BF16), e65=ein("e65", [65, 64], F32), pow2=ein("pow2", [128, NIT], F32),
            tbg=ein("tbg", [128, 8, 384], F32), far=ein("far", [128, 8], F32),
            w_pa=ein("w_pa", [2, 512, D], F32), w_pb=ein("w_pb", [2, 512, D], F32), w_out=ein("w_out", [2, D, D], F32),
            gffn=ein("gffn", [2, 128, D], F32), w_up=ein("w_up", [2, D, 2 * D_FF], F32), w_down=ein("w_down", [2, D_FF, D], F32),
            convw=ein("convw", [2, 128, NCT_, 3], F32), convb=ein("convb", [2, 128, NCT_], F32),
            halosel=ein("halosel", [4 * NH2, NH2], F32),
        )
        xout = nc.dram_tensor("xout", [NT, D], F32, kind="ExternalOutput").ap()
        T = dict(
            qiT=scr("s_qiT", [64, NT * 8], BF16), w2=scr("s_w2", [NT, 8], F32), qaT=scr("s_qaT", [512, NT], BF16),
            qbT=scr("s_qbT", [768, NT], BF16), kiT=scr("s_kiT", [NI * 64, 512], BF16), kaT=scr("s_kaT", [NT, 512], BF16),
            va=scr("s_va", [NT, 520], BF16), kbT=scr("s_kbT", [NI * 768, 512], BF16), vb=scr("s_vb", [NT, 520], BF16),
            gate=scr("s_gate", [NT, 2048], F32),
            kiT_g=scr("g_kiT", [4 * NI * 64, 512], BF16), kaT_g=scr("g_kaT", [4 * NI * 512, 512], BF16), va_g=scr("g_va", [4 * NT, 520], BF16),
            kbT_g=scr("g_kbT", [4 * NI * 768, 512], BF16), vb_g=scr("g_vb", [4 * NT, 520], BF16),
            yaT=scr("s_yaT", [512, NT], BF16), ybT=scr("s_ybT", [512, NT], BF16), xmid=scr("s_xmid", [NT, D], F32),
            x1=scr("s_x1", [NT, D], F32), halo_loc=scr("s_hloc", [NH2, D], F32), halo_g=scr("g_halo", [4 * NH2, D], F32),
            halo=scr("s_halo", [NI, 2, D], F32),
        )
        WB = {k: scr("b_" + k, list(X[k].shape), BF16) for k in ("w_in", "w_uq", "w_ukv", "w_pa", "w_pb", "w_out", "w_up", "w_down")}

        def conv(k, l):
            rows = X[k].shape[1]
            for r0 in range(0, rows, 128):
                r1 = min(rows, r0 + 128)
                sc.dma("pool", WB[k][l, r0:r1, :], X[k][l, r0:r1, :], "wconv")
        for k in ("w_in", "w_uq", "w_ukv"):
            conv(k, 0)
        sc.emit()
        for k, l in (("w_pa", 0), ("w_pb", 0), ("w_out", 0), ("w_down", 0), ("w_up", 0), ("w_in", 1), ("w_uq", 1), ("w_ukv", 1),
                     ("w_pa", 1), ("w_pb", 1), ("w_out", 1), ("w_down", 1), ("w_up", 1)):
            conv(k, l)
        for l in range(2):
            xin = X["x"] if l == 0 else T["x1"]
            env.d = dict(x=xin, w_in=WB["w_in"][l], w_uq=WB["w_uq"][l], w_ukv=WB["w_ukv"][l], gmix=X["gmix"][l], gqk=X["gqk"][l],
                         gcq=X["gcq"][l], gckv=X["gckv"][l], gbq=X["gbq"][l], gbk=X["gbk"][l], bgate=X["bgate"][l],
                         cos=X["cos"], sin=X["sin"], ident=X["ident"],
                         qiT=T["qiT"], w2=T["w2"], qaT=T["qaT"], qbT=T["qbT"], kiT=T["kiT"], kaT=T["kaT"], va=T["va"],
                         kbT=T["kbT"], vb=T["vb"], gate=T["gate"])
            def hook_sb(i):
                for k, rows in (("kiT", 64), ("kaT", 512), ("kbT", 768), ("va", 512), ("vb", 512)):
                    sc.coll("AllGather", [T[k][rows * i:rows * (i + 1), :]], [T[k + "_g"][4 * rows * i:4 * rows * (i + 1), :]], GROUPS,
                            reads=[("dr", k, i)], writes=[("g", k, i)])
            env.hook_sb = hook_sb
            env.skip_cc = True
            build_A(NT, env)
            env.hook_sb = None
            env.skip_cc = False
            env.d = dict(qiT=T["qiT"], w2T=T["w2"].rearrange("t h -> (t h)").rearrange("(c p) -> p c", p=128),
                         qaT=T["qaT"], qbT=T["qbT"], kiT_g=T["kiT_g"], kaT_g=T["kaT_g"], va_g=T["va_g"], kbT_g=T["kbT_g"],
                         vb_g=T["vb_g"], yaT=T["yaT"], ybT=T["ybT"],
                         **{k: X[k] for k in ("ident", "mg", "diagi", "selcol", "sel", "negrow", "onesrow", "bd", "e65", "pow2", "tbg", "far")})
            build_B(S, NI, env)
            env.d = dict(x=xin, yaT=T["yaT"], ybT=T["ybT"], gate=T["gate"], w_pa=WB["w_pa"][l], w_pb=WB["w_pb"][l], w_out=WB["w_out"][l],
                         ident=X["ident"], xmid=T["xmid"])
            build_C1(NT, env)
            for i in range(NI):
                sc.dma("sp", T["halo_loc"][2 * i:2 * i + 2, :], T["xmid"][SB * i + SB - 2:SB * i + SB, :], "hl")
            sc.emit()
            sc.coll("AllGather", [T["halo_loc"]], [T["halo_g"]], GROUPS)
            sc.emit()
            with ExitStack() as es2:
                cx = Ctx(nc, es2, env)
                hg = cx.sb([4 * NH2, D], F32)
                hs = cx.sb([4 * NH2, NH2], F32)
                ho = cx.sb([NH2, D], F32)
                pp = [cx.ps([128, 512], F32) for _ in range(2)]
                sc.dma("sp", hg[:], T["halo_g"], "hx", writes=["hg"])
                sc.dma("sp", hs[:], X["halosel"], "hx", writes=["hs"])
                for ct in range(2):
                    sc.op("pe", lambda e, ct=ct: e.matmul(pp[ct][0:NH2, :], hs[:, :], hg[:, ct * 512:(ct + 1) * 512], start=True, stop=True),
                          reads=["hg", "hs"], writes=[("pp", ct)])
                    sc.op("act", lambda e, ct=ct: e.activation(out=ho[:, ct * 512:(ct + 1) * 512], in_=pp[ct][0:NH2, :], func=AF.Copy),
                          reads=[("pp", ct)], writes=[("ho", ct)])
                sc.dma("sp", T["halo"].rearrange("i k d -> (i k) d"), ho[:], "hy", reads=[("ho", 0), ("ho", 1)])
                sc.emit()
            env.d = dict(xmid=T["xmid"], halo=T["halo"], gffn=X["gffn"][l], w_up=WB["w_up"][l], w_down=WB["w_down"][l],
                         convw=X["convw"][l], convb=X["convb"][l], ident=X["ident"], xout=(T["x1"] if l == 0 else xout))
            build_C2(NI, env)
        print("fused program: ops", sc.nops, "waits", sc.nwaits)
    return nc


def maps_fused(P, S, NI):
    x = np.asarray(P["x"], np.float32)
    NH2 = NI * 2
    ident = np.eye(128, dtype=np.float32).astype(NPBF)
    bidx = tb_bucket_idx()
    rb = np.asarray(P["rel_bias"], np.float32)
    tbg = np.ascontiguousarray(rb[bidx].transpose(0, 2, 1))
    far = np.ascontiguousarray(np.broadcast_to(rb[15][None, :], (128, 8)))
    st = lambda f: np.stack([f(l) for l in range(2)])
    shared = dict(
        w_in=np.asarray(P["w_in"], np.float32), w_uq=np.asarray(P["b_w_uq"], np.float32), w_ukv=np.asarray(P["b_w_ukv"], np.float32),
        gmix=st(lambda l: rep(P["norm_mix"][l])),
        gqk=st(lambda l: rep(np.concatenate([np.tile(P["a_q_norm"][l], 8), np.tile(P["a_k_norm"][l], 8)]))),
        gcq=st(lambda l: rep(P["b_cq_norm"][l])), gckv=st(lambda l: rep(P["b_ckv_norm"][l])),
        gbq=st(lambda l: rep(np.tile(P["b_q_norm"][l], 8))), gbk=st(lambda l: rep(np.tile(P["b_k_norm"][l], 8))),
        bgate=st(lambda l: rep(P["b_gate"][l])), ident=ident, tbg=tbg, far=far,
        w_pa=np.asarray(P["w_proj_a"], np.float32), w_pb=np.asarray(P["w_proj_b"], np.float32), w_out=np.asarray(P["w_out"], np.float32),
        gffn=st(lambda l: rep(P["norm_ffn"][l])), w_up=np.asarray(P["w_up"], np.float32), w_down=np.asarray(P["w_down"], np.float32),
        convw=st(lambda l: np.ascontiguousarray(np.asarray(P["conv_w"][l], np.float32).reshape(3, NCT_, 128).transpose(2, 1, 0))),
        convb=st(lambda l: np.ascontiguousarray(np.asarray(P["conv_b"][l], np.float32).reshape(NCT_, 128).T)),
    )
    maps = []
    for c in range(NCORE):
        b, j = c // 4, c % 4
        tok = core_tokens(NI, j)
        cos, sin = rope_tables(tok)
        hs = np.zeros((4 * NH2, NH2), np.float32)
        for i in range(NI):
            jj, ii = (j - 1, i) if j > 0 else (3, i - 1)
            if ii >= 0:
                for k in range(2):
                    hs[jj * NH2 + 2 * ii + k, 2 * i + k] = 1.0
        m = dict(shared)
        m.update(consts_B(j))
        m.update(x=np.ascontiguousarray(x[b][tok]), cos=cos, sin=sin, halosel=hs)
        maps.append(m)
    return maps


def kernel_fused(**inputs):
    P = {k: np.asarray(v) for k, v in inputs.items()}
    S = P["x"].shape[1]
    NI = S // (4 * SB)
    res = _run(_prog(("F", S, NI), build_fused, S, NI), maps_fused(P, S, NI))
    return gather_tok(res, "xout", S, NI).astype(np.float32)


def kernel(**inputs):
    return kernel_fused(**inputs)
```
